# Optimizing a Trainium2 kernel written in Bass

```python
import math
import jax
import jax.numpy as jnp
from jax import lax
import numpy as np

D_MODEL = 2048
BATCH = 4
SEQ = 2048
DEPTH = 1

D_MIX = D_MODEL
D_RET = D_MIX // 2
D_S5 = D_MIX - D_RET
RET_HEADS = 4
RET_HEAD_DIM = D_RET // RET_HEADS
RET_CHUNK = 128
S5_GROUP = 16
S5_GROUPS = D_S5 // S5_GROUP
S5_STATE = 64
S5_DT_MIN = 0.001
S5_DT_MAX = 0.1
D_FF = -(-8 * D_MODEL // (3 * 256)) * 256
N_PROJ = 4 * D_RET + D_S5
ROPE_BASE = 10000.0
EPS = 1e-6

kernel_name = 'hybrid_retention_s5_adaln_block'


def rms_norm(x, w):
    xf = x.astype(jnp.float32)
    y = xf * lax.rsqrt(jnp.mean(xf * xf, axis=-1, keepdims=True) + EPS)
    return (y * w.astype(jnp.float32)).astype(x.dtype)


def rotary(x, pos):
    half = x.shape[-1] // 2
    freqs = ROPE_BASE ** (-jnp.arange(half, dtype=jnp.float32) / half)
    ang = pos[:, None] * freqs[None, :]
    cos = jnp.cos(ang)[None, :, None, :].astype(x.dtype)
    sin = jnp.sin(ang)[None, :, None, :].astype(x.dtype)
    x1, x2 = x[..., :half], x[..., half:]
    return jnp.concatenate([x1 * cos - x2 * sin, x2 * cos + x1 * sin], axis=-1)


def retention(q, k, v):
    b, l, h, dk = q.shape
    dv = v.shape[-1]
    nc = l // RET_CHUNK
    log_gamma = jnp.log1p(-jnp.exp2(-5.0 - jnp.arange(h, dtype=jnp.float32)))
    idx = jnp.arange(RET_CHUNK, dtype=jnp.float32)
    diff = idx[:, None] - idx[None, :]
    intra = jnp.where(diff[None] >= 0.0,
                      jnp.exp(log_gamma[:, None, None] * jnp.maximum(diff, 0.0)[None]), 0.0)
    k_decay = jnp.exp(log_gamma[:, None] * (RET_CHUNK - 1.0 - idx)[None])
    q_decay = jnp.exp(log_gamma[:, None] * (idx + 1.0)[None])
    chunk_decay = jnp.exp(log_gamma * RET_CHUNK)
    qc = q.astype(jnp.float32).reshape(b, nc, RET_CHUNK, h, dk)
    kc = (k.astype(jnp.float32) * dk ** -0.5).reshape(b, nc, RET_CHUNK, h, dk)
    vc = v.astype(jnp.float32).reshape(b, nc, RET_CHUNK, h, dv)
    scores = jnp.einsum('bnihd,bnjhd->bnhij', qc, kc) * intra
    inner = jnp.einsum('bnhij,bnjhv->bnihv', scores, vc)
    kv = jnp.einsum('bnjhd,hj,bnjhv->bnhdv', kc, k_decay, vc)

    def step(state, kv_n):
        return state * chunk_decay[None, :, None, None] + kv_n, state

    _, prev = lax.scan(step, jnp.zeros((b, h, dk, dv), jnp.float32), jnp.moveaxis(kv, 1, 0))
    prev = jnp.moveaxis(prev, 0, 1)
    cross = jnp.einsum('bnihd,bnhdv->bnihv', qc, prev) * q_decay.T[None, None, :, :, None]
    return (inner + cross).reshape(b, l, h, dv)


def head_group_norm(y, w):
    mu = jnp.mean(y, axis=-1, keepdims=True)
    var = jnp.mean(jnp.square(y - mu), axis=-1, keepdims=True)
    yn = (y - mu) * lax.rsqrt(var + EPS)
    return yn.reshape(y.shape[0], y.shape[1], -1) * w.astype(jnp.float32)


def _ssm_combine(e1, e2):
    a1, b1 = e1
    a2, b2 = e2
    return (a2 * a1, a2 * b1 + b2)


def s5_scan(u, a_re, a_im, log_step, b_re, b_im, c_re, c_im, d_skip):
    b, l, _ = u.shape
    uf = u.astype(jnp.float32).reshape(b, l, S5_GROUPS, S5_GROUP)
    lam = lax.complex(a_re.astype(jnp.float32), a_im.astype(jnp.float32))
    dt = jnp.exp(log_step.astype(jnp.float32))[:, None]
    lam_bar = jnp.exp(lam * dt)
    b_mat = lax.complex(b_re.astype(jnp.float32), b_im.astype(jnp.float32))
    b_bar = ((lam_bar - 1.0) / lam)[..., None] * b_mat
    bu = jnp.einsum('gnp,blgp->blgn', b_bar, uf.astype(jnp.complex64))
    a = jnp.broadcast_to(lam_bar[None, None], bu.shape)
    _, states = lax.associative_scan(_ssm_combine, (a, bu), axis=1)
    c_mat = lax.complex(c_re.astype(jnp.float32), c_im.astype(jnp.float32))
    y = jnp.einsum('gpn,blgn->blgp', c_mat, states).real + d_skip.astype(jnp.float32) * uf
    return y.reshape(b, l, D_S5)


def setup_inputs(seed: int = 0) -> dict:
    key = jax.random.key(seed)
    ks = jax.random.split(key, 24)
    f32 = jnp.float32

    def nrm(k, shape, scale):
        return jax.random.normal(k, shape, f32) * scale

    n_idx = jnp.arange(S5_STATE, dtype=f32)
    x = nrm(ks[0], (BATCH, SEQ, D_MODEL), 1.0)
    c = nrm(ks[1], (BATCH, D_MODEL), 1.0)
    w_ada = nrm(ks[2], (DEPTH, D_MODEL, 6 * D_MODEL), D_MODEL ** -0.5)
    b_ada = nrm(ks[3], (DEPTH, 6 * D_MODEL), 0.01)
    norm1_w = 1.0 + nrm(ks[4], (DEPTH, D_MODEL), 0.02)
    w_in = nrm(ks[5], (DEPTH, D_MODEL, N_PROJ), D_MODEL ** -0.5)
    ret_norm_w = 1.0 + nrm(ks[6], (DEPTH, D_RET), 0.02)
    s5_a_re = -0.5 + nrm(ks[7], (DEPTH, S5_GROUPS, S5_STATE), 0.01)
    s5_a_im = math.pi * n_idx + nrm(ks[8], (DEPTH, S5_GROUPS, S5_STATE), 0.01)
    s5_log_step = jax.random.uniform(ks[9], (DEPTH, S5_GROUPS), f32,
                                     math.log(S5_DT_MIN), math.log(S5_DT_MAX))
    s5_b_re = nrm(ks[10], (DEPTH, S5_GROUPS, S5_STATE, S5_GROUP), (2 * S5_GROUP) ** -0.5)
    s5_b_im = nrm(ks[11], (DEPTH, S5_GROUPS, S5_STATE, S5_GROUP), (2 * S5_GROUP) ** -0.5)
    s5_c_re = nrm(ks[12], (DEPTH, S5_GROUPS, S5_GROUP, S5_STATE), 0.5)
    s5_c_im = nrm(ks[13], (DEPTH, S5_GROUPS, S5_GROUP, S5_STATE), 0.5)
    s5_d = nrm(ks[14], (DEPTH, S5_GROUPS, S5_GROUP), 1.0)
    w_glu = nrm(ks[15], (DEPTH, D_S5, D_S5), D_S5 ** -0.5)
    b_glu = nrm(ks[16], (DEPTH, D_S5), 0.01)
    w_out = nrm(ks[17], (DEPTH, D_MIX, D_MODEL), D_MIX ** -0.5)
    norm2_w = 1.0 + nrm(ks[18], (DEPTH, D_MODEL), 0.02)
    w_gate_up = nrm(ks[19], (DEPTH, D_MODEL, 2 * D_FF), D_MODEL ** -0.5)
    w_down = nrm(ks[20], (DEPTH, D_FF, D_MODEL), D_FF ** -0.5)
    final_norm_w = 1.0 + nrm(ks[21], (D_MODEL,), 0.02)
    return {'x': x, 'c': c, 'w_ada': w_ada, 'b_ada': b_ada, 'norm1_w': norm1_w,
            'w_in': w_in, 'ret_norm_w': ret_norm_w,
            's5_a_re': s5_a_re, 's5_a_im': s5_a_im, 's5_log_step': s5_log_step,
            's5_b_re': s5_b_re, 's5_b_im': s5_b_im, 's5_c_re': s5_c_re, 's5_c_im': s5_c_im,
            's5_d': s5_d, 'w_glu': w_glu, 'b_glu': b_glu, 'w_out': w_out,
            'norm2_w': norm2_w, 'w_gate_up': w_gate_up, 'w_down': w_down,
            'final_norm_w': final_norm_w}


def reference(x, c, w_ada, b_ada, norm1_w, w_in, ret_norm_w,
              s5_a_re, s5_a_im, s5_log_step, s5_b_re, s5_b_im, s5_c_re, s5_c_im,
              s5_d, w_glu, b_glu, w_out, norm2_w, w_gate_up, w_down, final_norm_w):
    bsz, seq, _ = x.shape
    pos = jnp.arange(seq, dtype=jnp.float32)
    cond = jax.nn.silu(c)
    for layer in range(DEPTH):
        mod = (cond @ w_ada[layer] + b_ada[layer])[:, None, :]
        sh1, sc1, g1, sh2, sc2, g2 = jnp.split(mod, 6, axis=-1)

        h = rms_norm(x, norm1_w[layer]) * (1.0 + sc1) + sh1
        proj = h @ w_in[layer]
        q, k, v, g, u = jnp.split(proj, [D_RET, 2 * D_RET, 3 * D_RET, 4 * D_RET], axis=-1)
        q = rotary(q.reshape(bsz, seq, RET_HEADS, RET_HEAD_DIM), pos)
        k = rotary(k.reshape(bsz, seq, RET_HEADS, RET_HEAD_DIM), pos)
        v = v.reshape(bsz, seq, RET_HEADS, RET_HEAD_DIM)
        ret = head_group_norm(retention(q, k, v), ret_norm_w[layer])
        ret_out = jax.nn.silu(g) * ret.astype(x.dtype)

        ssm = s5_scan(u, s5_a_re[layer], s5_a_im[layer], s5_log_step[layer],
                      s5_b_re[layer], s5_b_im[layer], s5_c_re[layer], s5_c_im[layer],
                      s5_d[layer]).astype(x.dtype)
        ssm_g = jax.nn.gelu(ssm)
        ssm_out = ssm_g * jax.nn.sigmoid(ssm_g @ w_glu[layer] + b_glu[layer])

        mix = jnp.concatenate([ret_out, ssm_out], axis=-1) @ w_out[layer]
        x = x + g1 * mix

        h = rms_norm(x, norm2_w[layer]) * (1.0 + sc2) + sh2
        gate, up = jnp.split(h @ w_gate_up[layer], 2, axis=-1)
        x = x + g2 * ((jax.nn.silu(gate) * up) @ w_down[layer])
    return rms_norm(x, final_norm_w)
```

```python
import contextlib
import numpy as np
import ml_dtypes
import concourse.bass as bass
import concourse.mybir as mybir
from concourse.bass_utils import run_bass_kernel_spmd

F32 = mybir.dt.float32
BF16 = mybir.dt.bfloat16
I32 = mybir.dt.int32
AF = mybir.ActivationFunctionType
ALU = mybir.AluOpType

D = 2048
NT = 1024
NP = 1024
DFF = 5632
EPS = 1e-6
TWO_PI = 6.283185307179586


class Res:
    __slots__ = ("name", "w", "r")

    def __init__(self, name):
        self.name = name
        self.w = None
        self.r = {}


class FW:
    NDS = 6

    def __init__(self, nc, es):
        self.nc = nc
        self.engs = {"pe": nc.tensor, "act": nc.scalar, "dve": nc.vector, "pool": nc.gpsimd, "sp": nc.sync}
        self.sem = {k: es.enter_context(nc.semaphore("s_" + k)) for k in ["pe", "act", "dve", "pool"]}
        self.cnt = {k: 0 for k in self.sem}
        self.seen = {e: {} for e in self.engs}
        self.dsem = {q: [es.enter_context(nc.semaphore(f"d_{q}{i}")) for i in range(self.NDS)] for q in ["sp", "pool"]}
        self.dcnt = {q: [0] * self.NDS for q in self.dsem}
        self.drr = {q: 0 for q in self.dsem}
        self.psum = []
        for i in range(6):
            t = es.enter_context(nc.psum_tensor(f"psum{i}", [128, 512], F32))
            self.psum.append((t, Res(f"psum{i}")))
        self.y2 = (es.enter_context(nc.psum_tensor("psum_y2", [128, 1024], F32)), Res("psum_y2"))
        self.prr = 0
        self.out_events = []

    def R(self, name):
        return Res(name)

    def ps(self):
        self.prr = (self.prr + 1) % len(self.psum)
        t, r = self.psum[self.prr]
        return t, r

    def reserve(self, k):
        out = [self.psum.pop() for _ in range(k)]
        self.prr = 0
        return out

    def release(self, banks):
        self.psum.extend(banks)

    def _wait(self, e, ev):
        key, h, val = ev
        if self.seen[e].get(key, 0) >= val:
            return
        self.engs[e].wait_ge(h, val)
        self.seen[e][key] = val

    def _deps(self, e, reads, writes):
        skip = "pe" if e == "pe" else None
        for r in reads:
            if r.w is not None and r.w[0] != skip:
                self._wait(e, r.w)
        for w in writes:
            if w.w is not None and w.w[0] != skip:
                self._wait(e, w.w)
            for ev in w.r.values():
                if ev[0] != skip:
                    self._wait(e, ev)

    def _record(self, ev, reads, writes):
        for r in reads:
            r.r[ev[0]] = ev
        for w in writes:
            w.w = ev
            w.r = {}

    def op(self, e, fn, reads=(), writes=()):
        self._deps(e, reads, writes)
        ins = fn()
        self.cnt[e] += 1
        ins.then_inc(self.sem[e], 1)
        ev = (e, self.sem[e], self.cnt[e])
        self._record(ev, reads, writes)
        return ev

    def dma(self, q, out, in_, reads=(), writes=(), is_out=False):
        i = self.drr[q]
        self.drr[q] = (i + 1) % self.NDS
        key = f"d_{q}{i}"
        h = self.dsem[q][i]
        if self.dcnt[q][i] > 0:
            self._wait(q, (key, h, self.dcnt[q][i]))
        self._deps(q, reads, writes)
        ins = self.engs[q].dma_start(out=out, in_=in_)
        self.dcnt[q][i] += 16
        ins.then_inc(h, 16)
        ev = (key, h, self.dcnt[q][i])
        self._record(ev, reads, writes)
        if is_out:
            self.out_events.append(ev)
        return ev

    def barrier(self):
        evs = [(k, self.sem[k], self.cnt[k]) for k in self.sem if self.cnt[k] > 0]
        for q in self.dsem:
            for i in range(self.NDS):
                if self.dcnt[q][i] > 0:
                    evs.append((f"d_{q}{i}", self.dsem[q][i], self.dcnt[q][i]))
        for e in self.engs:
            for ev in evs:
                if not (e == "pe" and ev[0] == "pe"):
                    self._wait(e, ev)


def build(dbg=None, stop_after=99):
    nc = bass.Bass("TRN2", target_bir_lowering=False)
    dbg_out = {}

    def din(name, shape, dt=F32):
        return nc.dram_tensor(name, list(shape), dt, kind="ExternalInput").ap()

    xm = din("xm", [NT, D]); xp = din("xp", [NP, D])
    pmask_d = din("pmask", [128, 1]); cT_d = din("cT", [128, 16])
    w_ada = din("w_ada", [D, 6 * D]); b_ada = din("b_ada", [1, 6 * D])
    n1w_d = din("n1w", [128, 16]); n2w_d = din("n2w", [128, 16]); fnw_d = din("fnw", [1, D])
    w_in = din("w_in", [D, 5120]); rnw_d = din("rnw", [128, 8])
    A_re_d = din("A_re", [128, 32]); A_im_d = din("A_im", [128, 32]); LS_d = din("LS", [128, 32])
    B_re_d = din("B_re", [128, 512]); B_im_d = din("B_im", [128, 512])
    C_re_d = din("C_re", [128, 512]); C_im_d = din("C_im", [128, 512])
    Dq_d = din("Dq", [128, 8])
    w_glu = din("w_glu", [1024, 1024]); bglu_d = din("bglu", [128, 8])
    w_out = din("w_out", [D, D]); w_gu = din("w_gu", [D, 2 * DFF]); w_dn = din("w_dn", [DFF, D])
    ident_d = din("ident", [128, 128]); maskT_d = din("maskT", [128, 512]); qdec_d = din("qdec", [128, 512])
    kdec_d = din("kdec", [128, 4]); cosm_d = din("cosm", [128, NT]); sinm_d = din("sinm", [128, NT])
    cosp_d = din("cosp", [128, NP]); sinp_d = din("sinp", [128, NP])
    iota_d = din("iota", [128, 256]); E9_d = din("E9", [128, 288]); hm_d = din("hm", [128, 2])
    bmask_d = din("bmask", [128, 128]); rm3_d = din("rm3", [128, 1])
    out_d = nc.dram_tensor("out", [NT, D], F32, kind="ExternalOutput").ap()

    def dbg_tensor(name, shape):
        dbg_out[name] = nc.dram_tensor("dbg_" + name, list(shape), F32, kind="ExternalOutput").ap()
        return dbg_out[name]

    CD = [float(np.float32(np.exp(np.float32(128.0) * np.log1p(-np.exp2(np.float32(-5.0 - h)))))) for h in range(4)]

    with contextlib.ExitStack() as es:
        fw = FW(nc, es)
        V, A, P, T, SP = "dve", "act", "pool", "pe", "sp"

        def sb(stack, name, shape, dt=F32):
            return stack.enter_context(nc.sbuf_tensor("sb_" + name, list(shape), dt))

        def dump(name, ap, shape, reads, stack):
            t = sb(stack, "dmp_" + name, shape, F32)
            r = fw.R("dmp_" + name)
            fw.op(V, lambda: nc.vector.tensor_copy(t[:], ap), reads=reads, writes=[r])
            fw.dma(SP, dbg_tensor(name, shape)[:], t[:], reads=[r], is_out=True)

        identb = sb(es, "identb", [128, 128], BF16); identf = sb(es, "identf", [128, 128])
        pmask = sb(es, "pmask_sb", [128, 1])
        a1 = sb(es, "a1", [128, 16]); sh1 = sb(es, "sh1", [128, 16]); a2 = sb(es, "a2", [128, 16]); sh2 = sb(es, "sh2", [128, 16])
        g1bc = sb(es, "g1bc", [128, D]); g2bc = sb(es, "g2bc", [128, D])
        rnw = sb(es, "rnw_sb", [128, 8]); bglu = sb(es, "bglu_sb", [128, 8])
        R_const = fw.R("const")
        R_mod = fw.R("mod")
        for t, d in [(identf, ident_d), (pmask, pmask_d), (rnw, rnw_d), (bglu, bglu_d)]:
            fw.dma(SP, t[:], d[:], writes=[R_const])
        fw.dma(P, identb[:], ident_d[:], writes=[R_const])

        open_stacks = []
        ph25 = contextlib.ExitStack(); open_stacks.append(ph25)
        mixT = sb(ph25, "mixR", [128, 8, NT], BF16)
        ssmg = sb(ph25, "ssmg", [128, 8, NT], BF16)
        R_mix = fw.R("mixT"); R_u = fw.R("uT"); R_ssmg = fw.R("ssmg")
        ph13 = contextlib.ExitStack(); open_stacks.append(ph13)
        h1T = sb(ph13, "h1T", [128, 16, NP + NT], BF16)
        R_h1 = fw.R("h1T")
        R_mod1 = fw.R("mod1")

        def rms_norm_T(ph, tiles, avec, shvec, hT, hR, tag, bg=None, Rm=None):
            xn4 = sb(ph, "xn4" + tag, [128, 4, D], BF16); junk = sb(ph, "junk" + tag, [128, D], BF16)
            ssq = sb(ph, "ssq" + tag, [128, 1]); rstd = sb(ph, "rstd" + tag, [128, 1])
            R_xn = [fw.R(f"xn{i}") for i in range(4)]; R_j = fw.R("junk"); R_s = fw.R("ssq")
            for gi in range(len(tiles) // 4):
                for t4 in range(4):
                    xap, xR = tiles[gi * 4 + t4]()
                    xRl = xR if isinstance(xR, list) else [xR]
                    fw.op(A, lambda xap=xap: nc.scalar.activation(junk[:], xap, AF.Square, accum_out=ssq[:]),
                          reads=xRl, writes=[R_j, R_s])
                    fw.op(V, lambda: nc.vector.tensor_scalar(rstd[:], ssq[:], 1.0 / D, EPS, op0=ALU.mult, op1=ALU.add),
                          reads=[R_s], writes=[R_s])
                    fw.op(A, lambda: nc.scalar.sqrt(rstd[:], rstd[:]), reads=[R_s], writes=[R_s])
                    fw.op(V, lambda: nc.vector.reciprocal(rstd[:], rstd[:]), reads=[R_s], writes=[R_s])
                    fw.op(V, lambda xap=xap, t4=t4: nc.vector.tensor_scalar(
                        xn4[:, t4, :], xap, rstd[:, 0:1], None, op0=ALU.mult), reads=xRl + [R_s], writes=[R_xn[t4]])
                for fc in range(16):
                    pt, pr = fw.ps()
                    ptb = pt[:].bitcast(BF16)
                    for t4 in range(4):
                        fw.op(T, lambda ptb=ptb, t4=t4, fc=fc: nc.tensor.transpose(
                            ptb[:, t4 * 128:(t4 + 1) * 128], xn4[:, t4, fc * 128:(fc + 1) * 128], identb[:]),
                            reads=[R_xn[t4], R_const], writes=[pr])
                    fw.op(A, lambda ptb=ptb, fc=fc, gi=gi: nc.scalar.activation(
                        hT[:, fc, gi * 512:(gi + 1) * 512], ptb[:, 0:512], AF.Identity,
                        bias=shvec[:, fc:fc + 1], scale=avec[:, fc:fc + 1]), reads=[pr, Rm if Rm is not None else R_mod], writes=[hR])
                    if bg is not None:
                        next(bg, None)


        with contextlib.ExitStack() as ph:
            cT = sb(ph, "cT_sb", [128, 16]); condb = sb(ph, "condb", [128, 16], BF16)
            crep = sb(ph, "crep", [128, 16, 128], BF16)
            n1w = sb(ph, "n1w_sb", [128, 16]); n2w = sb(ph, "n2w_sb", [128, 16])
            seg_sb = sb(ph, "seg_sb", [128, D]); bbc = sb(ph, "bbc", [128, D])
            tmpd = sb(ph, "tmpd", [128, 16, 128])
            wsl = [sb(ph, f"wada{i}", [128, D], BF16) for i in range(4)]
            wslR = [fw.R(f"wada{i}") for i in range(4)]
            R_c = fw.R("c"); R_seg = fw.R("seg"); R_bbc = fw.R("bbc"); R_tmpd = fw.R("tmpd")
            fw.dma(SP, cT[:], cT_d[:], writes=[R_c])
            fw.dma(SP, n1w[:], n1w_d[:], writes=[R_c])
            fw.dma(SP, n2w[:], n2w_d[:], writes=[R_c])
            fw.op(A, lambda: nc.scalar.activation(condb[:], cT[:], AF.Silu), reads=[R_c], writes=[R_c])
            fw.op(V, lambda: nc.vector.tensor_copy(crep[:], condb[:].unsqueeze(2).to_broadcast([128, 16, 128])),
                  reads=[R_c], writes=[R_c])
            wi = [0]

            def mod_segs(segs, banks_fn):
                for seg in segs:
                    Rm = R_mod1 if seg in (0, 1) else R_mod
                    fw.dma(SP, bbc[:], b_ada[0:1, seg * D:(seg + 1) * D].partition_broadcast(128), writes=[R_bbc])
                    pss = banks_fn()
                    for kc in range(16):
                        s = wi[0] % 4; wi[0] += 1
                        fw.dma(P, wsl[s][:], w_ada[kc * 128:(kc + 1) * 128, seg * D:(seg + 1) * D], writes=[wslR[s]])
                        for cg in range(4):
                            pt, pr = pss[cg]
                            fw.op(T, lambda pt=pt, s=s, cg=cg, kc=kc: nc.tensor.matmul(
                                pt, crep[:, kc, :], wsl[s][:, cg * 512:(cg + 1) * 512], start=(kc == 0), stop=(kc == 15)),
                                reads=[R_c, wslR[s]], writes=[pr])
                        yield
                    dst = g1bc if seg == 2 else (g2bc if seg == 5 else seg_sb)
                    Rd = R_mod if seg in (2, 5) else R_seg
                    for cg in range(4):
                        pt, pr = pss[cg]
                        fw.op(V, lambda pt=pt, cg=cg, dst=dst: nc.vector.tensor_tensor(
                            dst[:, cg * 512:(cg + 1) * 512], pt, bbc[:, cg * 512:(cg + 1) * 512], op=ALU.add),
                            reads=[pr, R_bbc], writes=[Rd])
                    if seg in (0, 1, 3, 4):
                        vec = {0: sh1, 1: a1, 3: sh2, 4: a2}[seg]
                        fw.op(V, lambda: nc.vector.tensor_tensor(
                            tmpd[:], seg_sb[:].rearrange("p (f m) -> p f m", m=128),
                            identf[:].unsqueeze(1).to_broadcast([128, 16, 128]), op=ALU.mult),
                            reads=[R_seg, R_const], writes=[R_tmpd])
                        fw.op(V, lambda vec=vec: nc.vector.tensor_reduce(
                            vec[:], tmpd[:], axis=mybir.AxisListType.X, op=ALU.add), reads=[R_tmpd], writes=[Rm])
                        if seg in (1, 4):
                            nw = n1w if seg == 1 else n2w
                            fw.op(V, lambda vec=vec: nc.vector.tensor_scalar(vec[:], vec[:], 1.0, None, op0=ALU.add),
                                  reads=[Rm], writes=[Rm])
                            fw.op(V, lambda vec=vec, nw=nw: nc.vector.tensor_tensor(vec[:], vec[:], nw[:], op=ALU.mult),
                                  reads=[R_c, Rm], writes=[Rm])
                    yield

            def banks_rot():
                return [(t[:], r) for (t, r) in [fw.ps() for _ in range(4)]]
            for _ in mod_segs([1, 0], banks_rot):
                pass
            if dbg == 0:
                for _ in mod_segs([2, 4, 3, 5], banks_rot):
                    pass
                dump("a1", a1[:], [128, 16], [R_mod1], ph); dump("sh1", sh1[:], [128, 16], [R_mod1], ph)
                dump("a2", a2[:], [128, 16], [R_mod], ph); dump("g1", g1bc[:], [128, D], [R_mod], ph); dump("n1w", n1w[:], [128, 16], [R_c], ph)
            if stop_after >= 1:
                resv = fw.reserve(2)
                y2t, y2r = fw.y2
                fixed = [(y2t[:, 0:512], y2r), (y2t[:, 512:1024], fw.R("y2b")), (resv[0][0][:], resv[0][1]), (resv[1][0][:], resv[1][1])]
                bg = mod_segs([2, 4, 3, 5], lambda: fixed)
                with contextlib.ExitStack() as ph1:
                    xs = [sb(ph1, f"xs{i}", [128, D]) for i in range(3)]
                    xsR = [fw.R(f"xs{i}") for i in range(3)]
                    cnt = [0]

                    def mk_loader(src, t):
                        def f():
                            s = cnt[0] % 3; cnt[0] += 1
                            fw.dma(SP, xs[s][:], src[t * 128:(t + 1) * 128, :], writes=[xsR[s]])
                            return xs[s][:], xsR[s]
                        return f
                    tiles = [mk_loader(xp, t) for t in range(8)] + [mk_loader(xm, t) for t in range(8)]
                    rms_norm_T(ph1, tiles, a1, sh1, h1T, R_h1, "_n1", bg=bg, Rm=R_mod1)
                    for _ in bg:
                        pass
                    fw.release(resv)
                    if dbg == 1:
                        dump("h1T0", h1T[:, 0, :], [128, 2048], [R_h1], ph1); dump("h1T15", h1T[:, 15, :], [128, 2048], [R_h1], ph1)
                    fw.barrier()
            else:
                fw.barrier()

        wcnt = [0]

        def load_w(slots, slotR, src_ap):
            s = wcnt[0] % len(slots); wcnt[0] += 1
            fw.dma(P, slots[s][:], src_ap, writes=[slotR[s]])
            return slots[s], slotR[s]

        w_in_v = w_in.rearrange("(kc p) c -> p kc c", p=128)

        if stop_after >= 2:
            with contextlib.ExitStack() as ph:
                wsl = [sb(ph, f"win{i}", [128, 16, 128], BF16) for i in range(2)]
                wslR = [fw.R(f"win{i}") for i in range(2)]
                wv = [sb(ph, f"wv{i}", [128, 16, 256], BF16) for i in range(2)]
                wvR = [fw.R(f"wv{i}") for i in range(2)]
                maskT = sb(ph, "maskT", [128, 4, 128]); qdec = sb(ph, "qdec", [128, 4, 128]); kdec = sb(ph, "kdec", [128, 4])
                kdecm = sb(ph, "kdecm", [128, 4])
                cosm = sb(ph, "cosm", [128, NT]); sinm = sb(ph, "sinm", [128, NT])
                cosp = sb(ph, "cosp", [128, NP]); sinp = sb(ph, "sinp", [128, NP])
                R_tab = fw.R("tab")
                fw.dma(SP, maskT[:], maskT_d[:].rearrange("p (h i) -> p h i", h=4), writes=[R_tab])
                fw.dma(SP, qdec[:], qdec_d[:].rearrange("p (h i) -> p h i", h=4), writes=[R_tab])
                fw.dma(SP, kdec[:], kdec_d[:], writes=[R_tab])
                for t, d in [(cosm, cosm_d), (sinm, sinm_d), (cosp, cosp_d), (sinp, sinp_d)]:
                    fw.dma(SP, t[:], d[:], writes=[R_tab])
                fw.op(V, lambda: nc.vector.tensor_scalar(kdecm[:], kdec[:], pmask[:, 0:1], None, op0=ALU.mult),
                      reads=[R_tab, R_const], writes=[R_tab])
                qT = sb(ph, "qT", [128, 2, NT], BF16); qdT = sb(ph, "qdT", [128, 2, NT], BF16)
                kT = sb(ph, "kT", [128, 2, NP + NT], BF16)
                ktok = sb(ph, "ktok", [128, 16, 256], BF16); vtok = sb(ph, "vtok", [128, 16, 256], BF16)
                gtok = sb(ph, "gtok", [128, 8, 256], BF16)
                state = sb(ph, "state", [128, 2, 256])
                rt = [sb(ph, f"rt{i}", [128, 512]) for i in range(4)]
                ssflat = ssmg[:].rearrange("p a b -> p (a b)")
                stateb8 = ssflat[:, 0:4096].rearrange("p (n d v) -> p n d v", n=8, d=2)
                yn4L = [ssflat[:, 4096:6144].bitcast(F32).rearrange("p (a b) -> p a b", a=4)] * 2
                sq2L = [ssflat[:, 6144:8192].bitcast(F32).rearrange("p (k b) -> p k b", k=2)] * 2
                msk4L = [sb(ph, f"msk4{i}", [128, 4, 128], BF16) for i in range(2)]
                rtok4L = [sb(ph, "rtok4", [128, 4, 256], BF16)] * 2
                stL = [sb(ph, f"st{i}", [128, 16]) for i in range(2)]
                R_q = fw.R("qT"); R_qd = fw.R("qdT"); R_k = fw.R("kT"); R_kt = fw.R("ktok"); R_v = fw.R("vtok"); R_g = fw.R("gtok")
                R_state = fw.R("state"); R_sb = fw.R("stateb"); R_rt = [fw.R(f"rt{i}") for i in range(4)]
                R_mskL = [fw.R(f"msk{i}") for i in range(2)]; R_rtokL = [fw.R("rtok4")] * 2
                _ryn = fw.R("yn4"); _rsq = fw.R("sq2"); R_ynL = [_ryn, _ryn]; R_sqL = [_rsq, _rsq]
                R_stL = [fw.R(f"st{i}") for i in range(2)]; R_sb8 = [fw.R(f"stateb8_{i}") for i in range(8)]

                def projT_rot(col0, tok0, ntok, dstT, Rdst, ctab, stab, toff):
                    w0, w0R = load_w(wsl, wslR, w_in_v[:, :, col0:col0 + 128])
                    w1, w1R = load_w(wsl, wslR, w_in_v[:, :, col0 + 128:col0 + 256])
                    for g in range(ntok // 512):
                        p1, p1R = fw.ps(); p2, p2R = fw.ps()
                        for (w, wR, pt, pr) in [(w0, w0R, p1, p1R), (w1, w1R, p2, p2R)]:
                            for kc in range(16):
                                fw.op(T, lambda w=w, pt=pt, kc=kc, g=g: nc.tensor.matmul(
                                    pt[:], w[:, kc, :], h1T[:, kc, tok0 + g * 512: tok0 + (g + 1) * 512],
                                    start=(kc == 0), stop=(kc == 15)), reads=[wR, R_h1], writes=[pr])
                        cs = ctab[:, toff + g * 512: toff + (g + 1) * 512]; sn = stab[:, toff + g * 512: toff + (g + 1) * 512]
                        fw.op(V, lambda p1=p1, cs=cs: nc.vector.tensor_tensor(rt[0][:], p1[:], cs, op=ALU.mult), reads=[p1R, R_tab], writes=[R_rt[0]])
                        fw.op(V, lambda p2=p2, sn=sn: nc.vector.tensor_tensor(rt[1][:], p2[:], sn, op=ALU.mult), reads=[p2R, R_tab], writes=[R_rt[1]])
                        fw.op(V, lambda p2=p2, cs=cs: nc.vector.tensor_tensor(rt[2][:], p2[:], cs, op=ALU.mult), reads=[p2R, R_tab], writes=[R_rt[2]])
                        fw.op(V, lambda p1=p1, sn=sn: nc.vector.tensor_tensor(rt[3][:], p1[:], sn, op=ALU.mult), reads=[p1R, R_tab], writes=[R_rt[3]])
                        o0 = dstT[:, 0, tok0 - (0 if dstT is kT else NP) + g * 512: tok0 - (0 if dstT is kT else NP) + (g + 1) * 512]
                        o1 = dstT[:, 1, tok0 - (0 if dstT is kT else NP) + g * 512: tok0 - (0 if dstT is kT else NP) + (g + 1) * 512]
                        fw.op(P, lambda o0=o0: nc.gpsimd.tensor_tensor(o0, rt[0][:], rt[1][:], op=ALU.subtract),
                              reads=[R_rt[0], R_rt[1]], writes=[Rdst])
                        fw.op(P, lambda o1=o1: nc.gpsimd.tensor_tensor(o1, rt[2][:], rt[3][:], op=ALU.add),
                              reads=[R_rt[2], R_rt[3]], writes=[Rdst])

                deferred = []
                for h in range(4):
                    projT_rot(h * 256, NP, NT, qT, R_q, cosm, sinm, 0)
                    projT_rot(1024 + h * 256, 0, NP, kT, R_k, cosp, sinp, 0)
                    projT_rot(1024 + h * 256, NP, NT, kT, R_k, cosm, sinm, 0)
                    while deferred:
                        deferred.pop(0)()
                    for dc in range(2):
                        fw.op(P, lambda dc=dc, h=h: nc.gpsimd.tensor_tensor(
                            qdT[:, dc, :].rearrange("p (t i) -> p t i", i=128), qT[:, dc, :].rearrange("p (t i) -> p t i", i=128),
                            qdec[:, h, :].unsqueeze(1).to_broadcast([128, 8, 128]), op=ALU.mult),
                            reads=[R_q, R_tab], writes=[R_qd])
                    wvt, wvtR = load_w(wv, wvR, w_in_v[:, :, 2048 + h * 256: 2048 + (h + 1) * 256])
                    for tt in range(0, 16, 2):
                        pt, pr = fw.ps()
                        for u2 in range(2):
                            for kc in range(16):
                                fw.op(T, lambda pt=pt, u2=u2, kc=kc, tt=tt: nc.tensor.matmul(
                                    pt[:, u2 * 256:(u2 + 1) * 256], h1T[:, kc, (tt + u2) * 128:(tt + u2 + 1) * 128], wvt[:, kc, :],
                                    start=(kc == 0), stop=(kc == 15)), reads=[wvtR, R_h1], writes=[pr])
                        fw.op(A, lambda pt=pt, tt=tt: nc.scalar.copy(vtok[:, tt:tt + 2, :], pt[:].rearrange("p (a b) -> p a b", a=2)),
                              reads=[pr], writes=[R_v])
                    wgt, wgtR = load_w(wv, wvR, w_in_v[:, :, 3072 + h * 256: 3072 + (h + 1) * 256])
                    for tt in range(0, 8, 2):
                        pt, pr = fw.ps()
                        for u2 in range(2):
                            for kc in range(16):
                                fw.op(T, lambda pt=pt, u2=u2, kc=kc, tt=tt: nc.tensor.matmul(
                                    pt[:, u2 * 256:(u2 + 1) * 256], h1T[:, kc, NP + (tt + u2) * 128: NP + (tt + u2 + 1) * 128], wgt[:, kc, :],
                                    start=(kc == 0), stop=(kc == 15)), reads=[wgtR, R_h1], writes=[pr])
                        fw.op(A, lambda pt=pt, tt=tt: nc.scalar.activation(
                            gtok[:, tt:tt + 2, :], pt[:].rearrange("p (a b) -> p a b", a=2), AF.Silu), reads=[pr], writes=[R_g])
                    for n in range(0, 16, 2):
                        pt, pr = fw.ps(); ptb = pt[:].bitcast(BF16)
                        for u2 in range(2):
                            for dc in range(2):
                                fw.op(T, lambda ptb=ptb, u2=u2, dc=dc, n=n: nc.tensor.transpose(
                                    ptb[:, u2 * 256 + dc * 128: u2 * 256 + (dc + 1) * 128], kT[:, dc, (n + u2) * 128:(n + u2 + 1) * 128], identb[:]),
                                    reads=[R_k, R_const], writes=[pr])
                        kd = kdecm if n < 8 else kdec
                        fw.op(A, lambda ptb=ptb, n=n, kd=kd, h=h: nc.scalar.activation(
                            ktok[:, n:n + 2, :], ptb[:, 0:512].rearrange("p (a b) -> p a b", a=2), AF.Copy, scale=kd[:, h:h + 1]),
                            reads=[pr, R_tab], writes=[R_kt])
                    for bt in range(2):
                        i0_ = bt * 4
                        msk4 = msk4L[bt]; R_msk = R_mskL[bt]
                        ps_s, ps_sR = fw.ps()
                        for ci in range(4):
                            i = i0_ + ci; n = 8 + i
                            for dc in range(2):
                                fw.op(T, lambda dc=dc, n=n, i=i, ci=ci, ps_s=ps_s: nc.tensor.matmul(
                                    ps_s[:, ci * 128:(ci + 1) * 128], kT[:, dc, n * 128:(n + 1) * 128], qT[:, dc, i * 128:(i + 1) * 128],
                                    start=(dc == 0), stop=(dc == 1)), reads=[R_k, R_q], writes=[ps_sR])
                        fw.op(V, lambda ps_s=ps_s, h=h: nc.vector.tensor_tensor(
                            msk4[:], ps_s[:].rearrange("p (a b) -> p a b", a=4), maskT[:, h, :].unsqueeze(1).to_broadcast([128, 4, 128]), op=ALU.mult),
                            reads=[ps_sR, R_tab], writes=[R_msk])
                    fw.op(V, lambda: nc.vector.memset(state[:], 0.0), writes=[R_state])
                    for n in range(15):
                        ps_k, ps_kR = fw.ps()
                        for dc in range(2):
                            fw.op(T, lambda ps_k=ps_k, dc=dc, n=n: nc.tensor.matmul(
                                ps_k[:, dc * 256:(dc + 1) * 256], ktok[:, n, dc * 128:(dc + 1) * 128], vtok[:, n, :], start=True, stop=True),
                                reads=[R_kt, R_v], writes=[ps_kR])
                        fw.op(V, lambda ps_k=ps_k, h=h: nc.vector.scalar_tensor_tensor(
                            state[:], state[:], CD[h], ps_k[:].rearrange("p (a b) -> p a b", a=2), op0=ALU.mult, op1=ALU.add),
                            reads=[ps_kR], writes=[R_state])
                        if n >= 7:
                            fw.op(A, lambda n=n: nc.scalar.copy(stateb8[:, n - 7, :, :], state[:]), reads=[R_state], writes=[R_sb8[n - 7]])
                    for bt in range(2):
                        i0_ = bt * 4
                        msk4, yn4, rtok4, sq2, st = msk4L[bt], yn4L[bt], rtok4L[bt], sq2L[bt], stL[bt]
                        R_msk, R_yn, R_rtok, R_sq, R_st = R_mskL[bt], R_ynL[bt], R_rtokL[bt], R_sqL[bt], R_stL[bt]
                        pos_ = [fw.ps(), fw.ps()]
                        for ci in range(4):
                            i = i0_ + ci; n = 8 + i
                            ps_o, ps_oR = pos_[ci // 2]
                            reg = ps_o[:, (ci % 2) * 256:(ci % 2 + 1) * 256]
                            fw.op(T, lambda reg=reg, n=n, ci=ci: nc.tensor.matmul(reg, msk4[:, ci, :], vtok[:, n, :], start=True, stop=False),
                                  reads=[R_msk, R_v], writes=[ps_oR])
                            if i > 0 or True:
                                for dc in range(2):
                                    fw.op(T, lambda reg=reg, dc=dc, i=i: nc.tensor.matmul(
                                        reg, qdT[:, dc, i * 128:(i + 1) * 128], stateb8[:, i, dc, :], start=False, stop=(dc == 1)),
                                        reads=[R_qd, R_sb8[i]], writes=[ps_oR])
                        if bt == 1:
                            deferred.pop(0)()
                        for ci in range(4):
                            ps_o, ps_oR = pos_[ci // 2]
                            reg = ps_o[:, (ci % 2) * 256:(ci % 2 + 1) * 256]
                            fw.op(A, lambda reg=reg, ci=ci: nc.scalar.activation(sq2[:, 0, 0:256], reg, AF.Copy, accum_out=st[:, ci:ci + 1]),
                                  reads=[ps_oR], writes=[R_sq, R_st])
                            fw.op(A, lambda reg=reg, ci=ci: nc.scalar.activation(sq2[:, 1, 0:256], reg, AF.Square, accum_out=st[:, 4 + ci:5 + ci]),
                                  reads=[ps_oR], writes=[R_sq, R_st])
                        fw.op(V, lambda: nc.vector.tensor_scalar(st[:, 0:8], st[:, 0:8], 1.0 / 256, None, op0=ALU.mult), reads=[R_st], writes=[R_st])
                        fw.op(V, lambda: nc.vector.tensor_tensor(st[:, 8:12], st[:, 0:4], st[:, 0:4], op=ALU.mult), reads=[R_st], writes=[R_st])
                        fw.op(V, lambda: nc.vector.scalar_tensor_tensor(st[:, 12:16], st[:, 4:8], EPS, st[:, 8:12], op0=ALU.add, op1=ALU.subtract),
                              reads=[R_st], writes=[R_st])
                        fw.op(A, lambda: nc.scalar.sqrt(st[:, 12:16], st[:, 12:16]), reads=[R_st], writes=[R_st])
                        fw.op(V, lambda: nc.vector.reciprocal(st[:, 12:16], st[:, 12:16]), reads=[R_st], writes=[R_st])
                        for ci in range(4):
                            ps_o, ps_oR = pos_[ci // 2]
                            reg = ps_o[:, (ci % 2) * 256:(ci % 2 + 1) * 256]
                            fw.op(V, lambda reg=reg, ci=ci: nc.vector.tensor_scalar(
                                yn4[:, ci, :], reg, st[:, ci:ci + 1], st[:, 12 + ci:13 + ci], op0=ALU.subtract, op1=ALU.mult),
                                reads=[ps_oR, R_st], writes=[R_yn])
                        fw.op(P, lambda i0_=i0_: nc.gpsimd.tensor_tensor(rtok4[:], yn4[:], gtok[:, i0_:i0_ + 4, :], op=ALU.mult),
                              reads=[R_yn, R_g], writes=[R_rtok])
                        def _tr_evac(rtok4=rtok4, R_rtok=R_rtok, i0_=i0_, h=h):
                            ps_t, ps_tR = fw.ps(); ptb = ps_t[:].bitcast(BF16)
                            for dc in range(2):
                                for ci in range(4):
                                    fw.op(T, lambda ptb=ptb, dc=dc, ci=ci: nc.tensor.transpose(
                                        ptb[:, dc * 512 + ci * 128: dc * 512 + (ci + 1) * 128], rtok4[:, ci, dc * 128:(dc + 1) * 128], identb[:]),
                                        reads=[R_rtok, R_const], writes=[ps_tR])
                            for dc in range(2):
                                fw.op(A, lambda ptb=ptb, dc=dc, i0_=i0_, h=h: nc.scalar.activation(
                                    mixT[:, h * 2 + dc, i0_ * 128:(i0_ + 4) * 128], ptb[:, dc * 512:(dc + 1) * 512], AF.Copy,
                                    scale=rnw[:, h * 2 + dc: h * 2 + dc + 1]), reads=[ps_tR, R_const], writes=[R_mix])
                        deferred.append(_tr_evac)
                while deferred:
                    deferred.pop(0)()
                fw.barrier()
            if dbg == 2:
                with contextlib.ExitStack() as dph:
                    dump("mixT0", mixT[:, 0, :], [128, NT], [R_mix], dph); dump("mixT7", mixT[:, 7, :], [128, NT], [R_mix], dph)
                    fw.barrier()

        iota5_d = din("iota5", [128, 512])
        if stop_after >= 3:
            with contextlib.ExitStack() as ph:
                uT = sb(ph, "uT", [128, 8, NP + NT], BF16)
                wsl = [sb(ph, f"winu{i}", [128, 16, 128], BF16) for i in range(2)]
                wslR = [fw.R(f"winu{i}") for i in range(2)]
                for cc in range(8):
                    w, wR = load_w(wsl, wslR, w_in_v[:, :, 4096 + cc * 128: 4096 + (cc + 1) * 128])
                    for g in range(4):
                        pt, pr = fw.ps()
                        for kc in range(16):
                            fw.op(T, lambda pt=pt, w=w, kc=kc, g=g: nc.tensor.matmul(
                                pt[:], w[:, kc, :], h1T[:, kc, g * 512:(g + 1) * 512], start=(kc == 0), stop=(kc == 15)),
                                reads=[wR, R_h1], writes=[pr])
                        if g < 2:
                            fw.op(A, lambda pt=pt, cc=cc, g=g: nc.scalar.activation(
                                uT[:, cc, g * 512:(g + 1) * 512], pt[:], AF.Copy, scale=pmask[:, 0:1]), reads=[pr, R_const], writes=[R_u])
                        else:
                            fw.op(A, lambda pt=pt, cc=cc, g=g: nc.scalar.copy(uT[:, cc, g * 512:(g + 1) * 512], pt[:]), reads=[pr], writes=[R_u])
                fw.barrier()
                scr_off = [0]

                h1flat = h1T[:].rearrange("p a b -> p (a b)")

                def scr(shape, dt=F32):
                    nel = int(np.prod(shape[1:]))
                    esz = 4 if dt in (F32, I32) else 2
                    nby = -(-(nel * esz) // 64) * 64
                    o = scr_off[0]; scr_off[0] += nby
                    assert scr_off[0] <= 65536, "out of h1T scratch"
                    flat = h1flat[:, o // 2:(o + nel * esz) // 2]
                    if dt != BF16:
                        flat = flat.bitcast(dt)
                    v = flat
                    if len(shape) == 3:
                        v = v.rearrange("p (a b) -> p a b", a=shape[1])
                    elif len(shape) == 4:
                        v = v.rearrange("p (a b c) -> p a b c", a=shape[1], b=shape[2])
                    elif len(shape) == 5:
                        v = v.rearrange("p (a b c d) -> p a b c d", a=shape[1], b=shape[2], c=shape[3])
                    return v

                class _T:
                    def __init__(self, v):
                        self.v = v

                    def __getitem__(self, k):
                        return self.v[k] if not (isinstance(k, slice) and k == slice(None)) else self.v

                def scrT(shape, dt=F32):
                    return _T(scr(shape, dt))
                def t32(name):
                    return sb(ph, name, [128, 32])[:]
                Are = t32("Are"); Aim = t32("Aim"); dtt = t32("dtt"); ar = t32("ar"); ph2 = t32("ph2"); ph8 = t32("ph8")
                cre = t32("cre"); cim = t32("cim"); tA = t32("tA"); tB = t32("tB"); tC = t32("tC")
                hm = sb(ph, "hm_sb", [128, 2]); bmask = sb(ph, "bmask_sb", [128, 128]); Dq = sb(ph, "Dq_sb", [128, 8])
                rm3 = sb(ph, "rm3_sb", [128, 1]); iota = sb(ph, "iota_sb", [128, 256])
                E9 = sb(ph, "E9_sb", [128, 9, 32]); magE = sb(ph, "magE", [128, 9, 32]); PWre = sb(ph, "PWre", [128, 9, 32]); PWim = sb(ph, "PWim", [128, 9, 32])
                eF = sb(ph, "eF", [128, 9, 32]); eF2 = sb(ph, "eF2", [128, 9, 32]); eA = sb(ph, "eA", [128, 9, 32]); eI = sb(ph, "eI", [128, 9, 32], I32)
                small7 = scr([128, 7, 32, 16])
                Bre, Bim, Cre, Cim, bbre, bbim, tb1 = [small7[:, i_] for i_ in range(7)]
                big_off = scr_off[0]
                big1 = scr([128, 9, 32, 16]); big2 = scr([128, 9, 32, 16])
                CL = [sb(ph, "CLre", [128, 9, 32, 16], BF16), sb(ph, "CLim", [128, 9, 32, 16], BF16)]
                BstA = [sb(ph, "BstAre", [128, 8, 32, 16], BF16), sb(ph, "BstAim", [128, 8, 32, 16], BF16)]
                R_p = fw.R("s5p")
                for t, d in [(Are, A_re_d), (Aim, A_im_d), (dtt, LS_d), (hm[:], hm_d), (bmask[:], bmask_d), (Dq[:], Dq_d), (iota[:], iota_d), (rm3[:], rm3_d)]:
                    fw.dma(SP, t, d[:], writes=[R_p])
                fw.dma(SP, E9[:], E9_d[:].rearrange("p (a b) -> p a b", b=32), writes=[R_p])
                for t, d in [(Bre, B_re_d), (Bim, B_im_d), (Cre, C_re_d), (Cim, C_im_d)]:
                    fw.dma(SP, t, d[:].rearrange("p (a b) -> p a b", b=16), writes=[R_p])
                RW = [R_p]

                def vop(f):
                    fw.op(V, f, reads=RW, writes=RW)

                def aop(f):
                    fw.op(A, f, reads=RW, writes=RW)

                def sincos(ang_t, sin_out, cos_out, tmpF, tmpF2, tmpI):
                    vop(lambda: nc.vector.tensor_copy(tmpI, ang_t))
                    vop(lambda: nc.vector.tensor_copy(tmpF, tmpI))
                    vop(lambda: nc.vector.tensor_tensor(tmpF, ang_t, tmpF, op=ALU.subtract))
                    aop(lambda: nc.scalar.activation(sin_out, tmpF, AF.Sin, scale=TWO_PI))
                    vop(lambda: nc.vector.tensor_scalar(tmpF2, ang_t, 0.25, None, op0=ALU.add))
                    vop(lambda: nc.vector.tensor_copy(tmpI, tmpF2))
                    vop(lambda: nc.vector.tensor_copy(tmpF, tmpI))
                    vop(lambda: nc.vector.tensor_tensor(tmpF, tmpF2, tmpF, op=ALU.subtract))
                    aop(lambda: nc.scalar.activation(cos_out, tmpF, AF.Sin, scale=TWO_PI))

                zlhs = sb(ph, "zlhs", [128, 128], BF16)
                vop(lambda: nc.vector.memset(zlhs[:], 0.0))
                aop(lambda: nc.scalar.activation(dtt, dtt, AF.Exp))
                vop(lambda: nc.vector.tensor_tensor(ar, Are, dtt, op=ALU.mult))
                vop(lambda: nc.vector.tensor_tensor(ph2, Aim, dtt, op=ALU.mult))
                vop(lambda: nc.vector.tensor_scalar(ph2, ph2, 1.0 / TWO_PI, None, op0=ALU.mult))
                vop(lambda: nc.vector.tensor_scalar(ph8, ph2, 8.0, None, op0=ALU.mult))
                b9 = lambda t: t.unsqueeze(1).to_broadcast([128, 9, 32])
                vop(lambda: nc.vector.tensor_tensor(eA[:], E9[:], b9(ar), op=ALU.mult))
                aop(lambda: nc.scalar.activation(magE[:], eA[:], AF.Exp))
                vop(lambda: nc.vector.tensor_tensor(eA[:], E9[:], b9(ph2), op=ALU.mult))
                sincos(eA[:], PWim[:], PWre[:], eF[:], eF2[:], eI[:])
                vop(lambda: nc.vector.tensor_tensor(PWre[:], PWre[:], magE[:], op=ALU.mult))
                vop(lambda: nc.vector.tensor_tensor(PWim[:], PWim[:], magE[:], op=ALU.mult))
                lre = PWre[:, 1, :]; lim = PWim[:, 1, :]
                vop(lambda: nc.vector.tensor_scalar(tA, lre, -1.0, None, op0=ALU.add))
                vop(lambda: nc.vector.tensor_tensor(tB, Are, Are, op=ALU.mult))
                vop(lambda: nc.vector.tensor_tensor(tC, Aim, Aim, op=ALU.mult))
                vop(lambda: nc.vector.tensor_tensor(tB, tB, tC, op=ALU.add))
                vop(lambda: nc.vector.reciprocal(tB, tB))
                vop(lambda: nc.vector.tensor_tensor(cre, tA, Are, op=ALU.mult))
                vop(lambda: nc.vector.tensor_tensor(tC, lim, Aim, op=ALU.mult))
                vop(lambda: nc.vector.tensor_tensor(cre, cre, tC, op=ALU.add))
                vop(lambda: nc.vector.tensor_tensor(cre, cre, tB, op=ALU.mult))
                vop(lambda: nc.vector.tensor_tensor(cim, lim, Are, op=ALU.mult))
                vop(lambda: nc.vector.tensor_tensor(tC, tA, Aim, op=ALU.mult))
                vop(lambda: nc.vector.tensor_tensor(cim, cim, tC, op=ALU.subtract))
                vop(lambda: nc.vector.tensor_tensor(cim, cim, tB, op=ALU.mult))
                bc = lambda t: t.unsqueeze(2).to_broadcast([128, 32, 16])
                vop(lambda: nc.vector.tensor_tensor(bbre, Bre, bc(cre), op=ALU.mult))
                vop(lambda: nc.vector.tensor_tensor(tb1, Bim, bc(cim), op=ALU.mult))
                vop(lambda: nc.vector.tensor_tensor(bbre, bbre, tb1, op=ALU.subtract))
                vop(lambda: nc.vector.tensor_tensor(bbim, Bim, bc(cre), op=ALU.mult))
                vop(lambda: nc.vector.tensor_tensor(tb1, Bre, bc(cim), op=ALU.mult))
                vop(lambda: nc.vector.tensor_tensor(bbim, bbim, tb1, op=ALU.add))
                X9 = lambda t: t.unsqueeze(1).to_broadcast([128, 9, 32, 16])
                PW9 = lambda t: t.unsqueeze(3).to_broadcast([128, 9, 32, 16])
                vop(lambda: nc.vector.tensor_tensor(big1, X9(Cre), PW9(PWre[:]), op=ALU.mult))
                vop(lambda: nc.vector.tensor_tensor(big2, X9(Cim), PW9(PWim[:]), op=ALU.mult))
                vop(lambda: nc.vector.tensor_tensor(CL[0][:], big1, big2, op=ALU.subtract))
                vop(lambda: nc.vector.tensor_tensor(big1, X9(Cre), PW9(PWim[:]), op=ALU.mult))
                vop(lambda: nc.vector.tensor_tensor(big2, X9(Cim), PW9(PWre[:]), op=ALU.mult))
                vop(lambda: nc.vector.tensor_tensor(big1, big1, big2, op=ALU.add))
                vop(lambda: nc.vector.tensor_scalar(CL[1][:], big1, -1.0, None, op0=ALU.mult))
                X8 = lambda t: t.unsqueeze(1).to_broadcast([128, 8, 32, 16])
                PW8 = lambda t: t[:, 0:8, :].unsqueeze(3).to_broadcast([128, 8, 32, 16])
                vop(lambda: nc.vector.tensor_tensor(big1[:, 0:8], X8(bbre), PW8(PWre), op=ALU.mult))
                vop(lambda: nc.vector.tensor_tensor(big2[:, 0:8], X8(bbim), PW8(PWim), op=ALU.mult))
                vop(lambda: nc.vector.tensor_tensor(BstA[0][:], big1[:, 0:8], big2[:, 0:8], op=ALU.subtract))
                vop(lambda: nc.vector.tensor_tensor(big1[:, 0:8], X8(bbim), PW8(PWre), op=ALU.mult))
                vop(lambda: nc.vector.tensor_tensor(big2[:, 0:8], X8(bbre), PW8(PWim), op=ALU.mult))
                vop(lambda: nc.vector.tensor_tensor(BstA[1][:], big1[:, 0:8], big2[:, 0:8], op=ALU.add))
                fw.barrier()
                scr_off[0] = big_off
                XEc = [scrT([128, 8, 4, 2, 16], BF16) for r in range(2)]
                CLXc = [scrT([128, 9, 4, 2, 16], BF16) for r in range(2)]
                CLX3 = [scrT([128, 9, 64], BF16) for r in range(2)]
                LBT = [scrT([128, 8, 128], BF16) for r in range(2)]
                LBT3 = [scrT([128, 8, 128], BF16) for r in range(2)]
                BD = scrT([128, 8, 128], BF16)
                cosC2 = [scrT([128, 256]) for _ in range(2)]; sinC2 = [scrT([128, 256]) for _ in range(2)]; angC = scrT([128, 256])
                tF = scrT([128, 256]); tF2 = scrT([128, 256]); tIl = scrT([128, 256], I32)
                tF3 = sb(ph, "tF3", [128, 256]); tIl2 = sb(ph, "tIl2", [128, 256], I32)
                R_tab2 = [fw.R("tab2a"), fw.R("tab2b")]; R_tmpL = fw.R("tmpL"); R_tmpL2 = fw.R("tmpL2"); nonlocal_RW = [None]
                wa = [scrT([128, 256]) for i in range(4)]
                rr = scrT([128, 256]); rim = scrT([128, 256])
                sbf = [[scrT([128, 128], BF16) for b_ in range(2)] for r in range(2)]
                ysb = [scrT([128, 512]) for i in range(2)]
                R_xe = fw.R("xec"); R_clx = fw.R("clxc"); R_lbt = fw.R("lbt"); R_bd = fw.R("bd")
                R_wa = [fw.R(f"wa{i}") for i in range(4)]; R_rr = fw.R("rr"); R_ri = fw.R("ri")
                R_sbf = [[fw.R(f"sbf{r}{b_}") for b_ in range(2)] for r in range(2)]; R_y = [fw.R(f"ysb{i}") for i in range(2)]
                for r in range(2):
                    fw.op(V, lambda r=r: nc.vector.memset(CLX3[r][:], 0.0), writes=[R_clx])
                y2, y2R = fw.y2
                pairn = 0

                def prep_xe(cc_):
                    for r in range(2):
                        for g2 in range(2):
                            fw.op(V, lambda r=r, g2=g2: nc.vector.tensor_scalar(
                                XEc[r][:, :, :, g2, :], BstA[r][:, :, cc_ * 4:(cc_ + 1) * 4, :], hm[:, g2:g2 + 1], None, op0=ALU.mult),
                                reads=[R_p], writes=[R_xe])
                for cc in range(8):
                    if cc == 0:
                        prep_xe(0)
                    for r in range(2):
                        for g2 in range(2):
                            fw.op(V, lambda r=r, g2=g2, cc=cc: nc.vector.tensor_scalar(
                                CLXc[r][:, :, :, g2, :], CL[r][:, :, cc * 4:(cc + 1) * 4, :], hm[:, g2:g2 + 1], None, op0=ALU.mult),
                                reads=[R_p], writes=[R_clx])
                        fw.op(V, lambda r=r: nc.vector.tensor_copy(
                            CLX3[r][:, :, 32:64], CLXc[r][:, :, 3, :, :].rearrange("p e a b -> p e (a b)")), reads=[R_clx], writes=[R_clx])
                    for r in range(2):
                        for eh in range(2):
                            pt, pr = fw.ps(); ptb = pt[:].bitcast(BF16)
                            for e4 in range(4):
                                e = eh * 4 + e4
                                fw.op(T, lambda ptb=ptb, r=r, e=e, e4=e4: nc.tensor.transpose(
                                    ptb[:, e4 * 128:(e4 + 1) * 128], XEc[r][:, e, :, :, :].rearrange("p a b c -> p (a b c)"), identb[:]),
                                    reads=[R_xe, R_const], writes=[pr])
                            fw.op(A, lambda ptb=ptb, r=r, eh=eh: nc.scalar.copy(
                                LBT[r][:, eh * 4:(eh + 1) * 4, :], ptb[:, 0:512].rearrange("p (a b) -> p a b", a=4)), reads=[pr], writes=[R_lbt])
                            fw.op(V, lambda ptb=ptb, r=r, eh=eh: nc.vector.tensor_scalar(
                                LBT3[r][64:128, eh * 4:(eh + 1) * 4, :], ptb[64:128, 0:512].rearrange("p (a b) -> p a b", a=4), rm3[64:128, 0:1], None, op0=ALU.mult),
                                reads=[pr, R_p], writes=[R_lbt])
                    for dh in range(2):
                        pt, pr = fw.ps()
                        for d4 in range(4):
                            d_ = dh * 4 + d4
                            for r in range(2):
                                fw.op(T, lambda pt=pt, d4=d4, d_=d_, r=r: nc.tensor.matmul(
                                    pt[:, d4 * 128:(d4 + 1) * 128], XEc[r][:, 0, :, :, :].rearrange("p a b c -> p (a b c)"),
                                    CLXc[r][:, d_, :, :, :].rearrange("p a b c -> p (a b c)"), start=(r == 0), stop=(r == 1)),
                                    reads=[R_xe, R_clx], writes=[pr])
                        fw.op(V, lambda pt=pt, dh=dh: nc.vector.tensor_tensor(
                            BD[:, dh * 4:(dh + 1) * 4, :], pt[:].rearrange("p (a b) -> p a b", a=4), bmask[:].unsqueeze(1).to_broadcast([128, 4, 128]), op=ALU.mult),
                            reads=[pr, R_p], writes=[R_bd])
                    for bk in range(2):
                        fw.op(T, lambda bk=bk, cc=cc: nc.tensor.matmul(
                            y2[:, bk * 512:(bk + 1) * 512], zlhs[:], uT[:, cc, 0:512], start=True, stop=False),
                            reads=[R_p, R_u], writes=[y2R])
                    for i in range(8):
                        for j in range(i + 1):
                            fw.op(T, lambda i=i, j=j, cc=cc: nc.tensor.matmul(
                                y2[:, i * 128:(i + 1) * 128], BD[:, i - j, :], uT[:, cc, NP + j::8], start=False, stop=False),
                                reads=[R_bd, R_u], writes=[y2R])
                    if cc + 1 < 8:
                        prep_xe(cc + 1)

                    def stageA(gpl):
                        Pp = cc * 4 + gpl
                        tb = Pp % 2
                        if gpl < 3:
                            rows = slice(32 * gpl, 32 * gpl + 32); Ls = LBT
                        else:
                            rows = slice(64, 128); Ls = LBT3
                        pS, pSR = fw.ps()
                        for r in range(2):
                            for j in range(8):
                                fw.op(T, lambda pS=pS, r=r, j=j, rows=rows, Ls=Ls: nc.tensor.matmul(
                                    pS[:, r * 256:(r + 1) * 256], Ls[r][rows, 7 - j, :], uT[rows, cc, j::8], start=(j == 0), stop=(j == 7)),
                                    reads=[R_lbt, R_u], writes=[pSR])
                        nonlocal_RW[0] = [R_tmpL]
                        fw.op(V, lambda Pp=Pp: nc.vector.tensor_scalar(angC[:], iota[:], 1.0, ph8[:, Pp:Pp + 1], op0=ALU.add, op1=ALU.mult),
                              reads=[R_p, R_tmpL], writes=[R_tmpL])
                        cT_, sT_ = cosC2[tb], sinC2[tb]
                        fw.op(V, lambda: nc.vector.tensor_copy(tIl[:], angC[:]), reads=[R_tmpL], writes=[R_tmpL])
                        fw.op(V, lambda: nc.vector.tensor_copy(tF[:], tIl[:]), reads=[R_tmpL], writes=[R_tmpL])
                        fw.op(V, lambda: nc.vector.tensor_tensor(tF[:], angC[:], tF[:], op=ALU.subtract), reads=[R_tmpL], writes=[R_tmpL])
                        fw.op(A, lambda sT_=sT_: nc.scalar.activation(sT_[:], tF[:], AF.Sin, scale=TWO_PI), reads=[R_tmpL], writes=[R_tab2[tb]])
                        fw.op(V, lambda: nc.vector.tensor_scalar(tF2[:], angC[:], 0.25, None, op0=ALU.add), reads=[R_tmpL], writes=[R_tmpL2])
                        fw.op(V, lambda: nc.vector.tensor_copy(tIl2[:], tF2[:]), reads=[R_tmpL2], writes=[R_tmpL2])
                        fw.op(V, lambda: nc.vector.tensor_copy(tF3[:], tIl2[:]), reads=[R_tmpL2], writes=[R_tmpL2])
                        fw.op(V, lambda: nc.vector.tensor_tensor(tF3[:], tF2[:], tF3[:], op=ALU.subtract), reads=[R_tmpL2], writes=[R_tmpL2])
                        fw.op(A, lambda cT_=cT_: nc.scalar.activation(cT_[:], tF3[:], AF.Sin, scale=TWO_PI), reads=[R_tmpL2], writes=[R_tab2[tb]])
                        return (gpl, Pp, tb, rows, pS, pSR)

                    def stageB(ctx):
                        nonlocal pairn
                        gpl, Pp, tb, rows, pS, pSR = ctx
                        b_ = pairn % 2; pairn += 1
                        cosC, sinC, R_tabL = cosC2[tb], sinC2[tb], R_tab2[tb]
                        Sre = pS[:, 0:256]; Sim = pS[:, 256:512]
                        fw.op(V, lambda: nc.vector.tensor_tensor(wa[0][:], Sre, cosC[:], op=ALU.mult), reads=[pSR, R_tabL], writes=[R_wa[0]])
                        fw.op(V, lambda: nc.vector.tensor_tensor(wa[1][:], Sim, sinC[:], op=ALU.mult), reads=[pSR, R_tabL], writes=[R_wa[1]])
                        fw.op(V, lambda: nc.vector.tensor_tensor(wa[2][:], Sim, cosC[:], op=ALU.mult), reads=[pSR, R_tabL], writes=[R_wa[2]])
                        fw.op(V, lambda: nc.vector.tensor_tensor(wa[3][:], Sre, sinC[:], op=ALU.mult), reads=[pSR, R_tabL], writes=[R_wa[3]])
                        fw.op(P, lambda: nc.gpsimd.tensor_tensor(wa[0][:], wa[0][:], wa[1][:], op=ALU.add), reads=[R_wa[0], R_wa[1]], writes=[R_wa[0]])
                        fw.op(P, lambda: nc.gpsimd.tensor_tensor(wa[2][:], wa[2][:], wa[3][:], op=ALU.subtract), reads=[R_wa[2], R_wa[3]], writes=[R_wa[2]])
                        rho = magE[:, 8, Pp:Pp + 1].to_broadcast([128, 256])
                        fw.op(V, lambda: nc.vector.tensor_tensor_scan(rr[:], rho, wa[0][:], 0.0, ALU.mult, ALU.add),
                              reads=[R_wa[0], R_p], writes=[R_rr])
                        fw.op(V, lambda: nc.vector.tensor_tensor_scan(rim[:], rho, wa[2][:], 0.0, ALU.mult, ALU.add),
                              reads=[R_wa[2], R_p], writes=[R_ri])
                        cs = cosC[:, 127:255]; sn = sinC[:, 127:255]
                        fw.op(P, lambda: nc.gpsimd.tensor_tensor(wa[0][:, 0:128], rr[:, 127:255], cs, op=ALU.mult), reads=[R_rr, R_tabL], writes=[R_wa[0]])
                        fw.op(P, lambda: nc.gpsimd.tensor_tensor(wa[1][:, 0:128], rim[:, 127:255], sn, op=ALU.mult), reads=[R_ri, R_tabL], writes=[R_wa[1]])
                        fw.op(V, lambda: nc.vector.tensor_tensor(wa[2][:, 0:128], rim[:, 127:255], cs, op=ALU.mult), reads=[R_ri, R_tabL], writes=[R_wa[2]])
                        fw.op(V, lambda: nc.vector.tensor_tensor(wa[3][:, 0:128], rr[:, 127:255], sn, op=ALU.mult), reads=[R_rr, R_tabL], writes=[R_wa[3]])
                        fw.op(P, lambda: nc.gpsimd.tensor_tensor(sbf[0][b_][:], wa[0][:, 0:128], wa[1][:, 0:128], op=ALU.subtract),
                              reads=[R_wa[0], R_wa[1]], writes=[R_sbf[0][b_]])
                        fw.op(V, lambda: nc.vector.tensor_tensor(sbf[1][b_][:], wa[2][:, 0:128], wa[3][:, 0:128], op=ALU.add),
                              reads=[R_wa[2], R_wa[3]], writes=[R_sbf[1][b_]])
                        for i in range(8):
                            for r in range(2):
                                if gpl < 3:
                                    lhs = CLXc[r][:, i + 1, gpl, :, :].rearrange("p a b -> p (a b)")
                                else:
                                    lhs = CLX3[r][:, i + 1, :]
                                fw.op(T, lambda i=i, r=r, lhs=lhs: nc.tensor.matmul(
                                    y2[rows, i * 128:(i + 1) * 128], lhs, sbf[r][b_][:], start=False, stop=False),
                                    reads=[R_clx, R_sbf[r][b_]], writes=[y2R])

                    ctxs = [stageA(0), stageA(1)]
                    stageB(ctxs[0]); ctxs.append(stageA(2)); stageB(ctxs[1]); ctxs.append(stageA(3)); stageB(ctxs[2]); stageB(ctxs[3])
                    for bk in range(2):
                        fw.op(T, lambda bk=bk, cc=cc: nc.tensor.matmul(
                            y2[:, bk * 512:(bk + 1) * 512], zlhs[:], uT[:, cc, 0:512], start=False, stop=True),
                            reads=[R_p, R_u], writes=[y2R])
                    y2v = y2[:].rearrange("p (i c) -> p c i", i=8)
                    for g in range(2):
                        fw.op(V, lambda g=g, cc=cc: nc.vector.scalar_tensor_tensor(
                            ysb[g][:].rearrange("p (c i) -> p c i", i=8), uT[:, cc, NP + g * 512: NP + (g + 1) * 512].rearrange("p (c i) -> p c i", i=8),
                            Dq[:, cc:cc + 1], y2v[:, g * 64:(g + 1) * 64, :], op0=ALU.mult, op1=ALU.add),
                            reads=[y2R, R_u, R_p], writes=[R_y[g]])
                        fw.op(A, lambda cc=cc, g=g: nc.scalar.activation(ssmg[:, cc, g * 512:(g + 1) * 512], ysb[g][:], AF.Gelu_apprx_tanh),
                              reads=[R_y[g]], writes=[R_ssmg])
                fw.barrier()
            if dbg == 3:
                with contextlib.ExitStack() as dph:
                    dump("ssmg0", ssmg[:, 0, :], [128, NT], [R_ssmg], dph); dump("ssmg7", ssmg[:, 7, :], [128, NT], [R_ssmg], dph)
                    fw.barrier()
        for stk in reversed(open_stacks[1:]):
            stk.close()
        open_stacks = open_stacks[:1]

        if stop_after >= 5:
            phx = contextlib.ExitStack(); open_stacks.append(phx)
            xres = sb(phx, "xres", [128, 8, D])
            R_xc = [[fw.R(f"xres{t}_{c}") for c in range(4)] for t in range(8)]
            R_x = R_xc
            for tt in range(8):
                fw.dma(SP, xres[:, tt, :], xm[tt * 128:(tt + 1) * 128, :], writes=R_xc[tt])
            with contextlib.ExitStack() as ph:
                mixS = sb(ph, "mixS", [128, 8, NT], BF16); wglu = sb(ph, "wglu", [128, 8, 1024], BF16)
                wo = [sb(ph, f"wo{i}", [128, 16, 512], BF16) for i in range(2)]
                woR = [fw.R(f"wo{i}") for i in range(2)]
                sg = [sb(ph, f"sg{i}", [128, 512]) for i in range(2)]; sgR = [fw.R(f"sg{i}") for i in range(2)]
                R_ms = fw.R("mixS"); R_wg = fw.R("wglu")
                fw.dma(P, wglu[:], w_glu.rearrange("(kc p) c -> p kc c", p=128), writes=[R_wg])
                k = 0
                for co in range(8):
                    for g in range(2):
                        pt, pr = fw.ps()
                        for cc in range(8):
                            fw.op(T, lambda pt=pt, cc=cc, co=co, g=g: nc.tensor.matmul(
                                pt[:], wglu[:, cc, co * 128:(co + 1) * 128], ssmg[:, cc, g * 512:(g + 1) * 512], start=(cc == 0), stop=(cc == 7)),
                                reads=[R_wg, R_ssmg], writes=[pr])
                        s_ = k % 2; k += 1
                        fw.op(A, lambda pt=pt, co=co, s_=s_: nc.scalar.activation(sg[s_][:], pt[:], AF.Sigmoid, bias=bglu[:, co:co + 1]),
                              reads=[pr, R_const], writes=[sgR[s_]])
                        fw.op(V, lambda co=co, g=g, s_=s_: nc.vector.tensor_tensor(
                            mixS[:, co, g * 512:(g + 1) * 512], ssmg[:, co, g * 512:(g + 1) * 512], sg[s_][:], op=ALU.mult),
                            reads=[sgR[s_], R_ssmg], writes=[R_ms])
                w_out_v = w_out.rearrange("(kc p) c -> p kc c", p=128)
                for cg in range(4):
                    w, wR = load_w(wo, woR, w_out_v[:, :, cg * 512:(cg + 1) * 512])
                    fw.op(V, lambda w=w, cg=cg: nc.vector.tensor_tensor(
                        w[:], w[:], g1bc[:, cg * 512:(cg + 1) * 512].unsqueeze(1).to_broadcast([128, 16, 512]), op=ALU.mult),
                        reads=[R_mod, wR], writes=[wR])
                    for tt in range(8):
                        pt, pr = fw.ps()
                        for fc in range(16):
                            src = mixT[:, fc, tt * 128:(tt + 1) * 128] if fc < 8 else mixS[:, fc - 8, tt * 128:(tt + 1) * 128]
                            fw.op(T, lambda pt=pt, src=src, w=w, fc=fc: nc.tensor.matmul(pt[:], src, w[:, fc, :], start=(fc == 0), stop=(fc == 15)),
                                  reads=[R_mix, R_ms, wR], writes=[pr])
                        fw.op(V, lambda pt=pt, tt=tt, cg=cg: nc.vector.tensor_tensor(
                            xres[:, tt, cg * 512:(cg + 1) * 512], pt[:], xres[:, tt, cg * 512:(cg + 1) * 512], op=ALU.add),
                            reads=[pr, R_xc[tt][cg]], writes=[R_xc[tt][cg]])
                fw.barrier()
            if dbg == 5:
                with contextlib.ExitStack() as dph:
                    dump("x1_0", xres[:, 0, :], [128, D], R_x[0], dph); dump("x1_7", xres[:, 7, :], [128, D], R_x[7], dph)
                    fw.barrier()

        if stop_after >= 6:
            with contextlib.ExitStack() as ph:
                h2T = sb(ph, "h2T", [128, 16, NT], BF16); R_h2 = fw.R("h2T")
                with contextlib.ExitStack() as ph_n:
                    tiles = [(lambda t=t: (xres[:, t, :], R_x[t])) for t in range(8)]
                    rms_norm_T(ph_n, tiles, a2, sh2, h2T, R_h2, "_n2")
                    fw.barrier()
                NPART = 11; FPP = 4
                act = sb(ph, "act", [128, FPP, NT], BF16); R_act = fw.R("act")
                wgu = [sb(ph, f"wgu{i}", [128, 16, 128], BF16) for i in range(6)]; wguR = [fw.R(f"wgu{i}") for i in range(6)]
                wd = [sb(ph, f"wd{i}", [128, FPP, 512], BF16) for i in range(4)]; wdR = [fw.R(f"wd{i}") for i in range(4)]
                sg = [sb(ph, f"sgf{i}", [128, 512]) for i in range(2)]; sgR = [fw.R(f"sgf{i}") for i in range(2)]
                sgb = [sb(ph, f"sgb{i}", [128, 512], BF16) for i in range(2)]; sgbR = [fw.R(f"sgb{i}") for i in range(2)]
                w_gu_v = w_gu.rearrange("(kc p) c -> p kc c", p=128)
                k = 0; wdc = [0]
                for part in range(NPART):
                    for fi in range(FPP):
                        f = part * FPP + fi
                        wg_, wgR_ = load_w(wgu, wguR, w_gu_v[:, :, f * 128:(f + 1) * 128])
                        wu_, wuR_ = load_w(wgu, wguR, w_gu_v[:, :, DFF + f * 128: DFF + (f + 1) * 128])
                        for g in range(2):
                            pg, pgR = fw.ps(); pu, puR = fw.ps()
                            for (w, wR, pt, pr) in [(wg_, wgR_, pg, pgR), (wu_, wuR_, pu, puR)]:
                                for kc in range(16):
                                    fw.op(T, lambda w=w, pt=pt, kc=kc, g=g: nc.tensor.matmul(
                                        pt[:], w[:, kc, :], h2T[:, kc, g * 512:(g + 1) * 512], start=(kc == 0), stop=(kc == 15)),
                                        reads=[wR, R_h2], writes=[pr])
                            s_ = k % 2; k += 1
                            fw.op(A, lambda pg=pg, s_=s_: nc.scalar.activation(sgb[s_][:], pg[:], AF.Silu), reads=[pgR], writes=[sgbR[s_]])
                            fw.op(V, lambda pu=pu, fi=fi, g=g, s_=s_: nc.vector.tensor_tensor(
                                act[:, fi, g * 512:(g + 1) * 512], pu[:], sgb[s_][:], op=ALU.mult), reads=[puR, sgbR[s_]], writes=[R_act])
                    for cg in range(4):
                        s2 = wdc[0] % 4; wdc[0] += 1
                        fw.dma(P, wd[s2][:], w_dn[part * FPP * 128:(part + 1) * FPP * 128, cg * 512:(cg + 1) * 512].rearrange("(f p) c -> p f c", p=128),
                               writes=[wdR[s2]])
                        fw.op(V, lambda s2=s2, cg=cg: nc.vector.tensor_tensor(
                            wd[s2][:], wd[s2][:], g2bc[:, cg * 512:(cg + 1) * 512].unsqueeze(1).to_broadcast([128, FPP, 512]), op=ALU.mult),
                            reads=[R_mod, wdR[s2]], writes=[wdR[s2]])
                        for tt in range(8):
                            pt, pr = fw.ps()
                            for fi in range(FPP):
                                fw.op(T, lambda pt=pt, fi=fi, tt=tt, s2=s2: nc.tensor.matmul(
                                    pt[:], act[:, fi, tt * 128:(tt + 1) * 128], wd[s2][:, fi, :], start=(fi == 0), stop=(fi == FPP - 1)),
                                    reads=[R_act, wdR[s2]], writes=[pr])
                            fw.op(V, lambda pt=pt, tt=tt, cg=cg: nc.vector.tensor_tensor(
                                xres[:, tt, cg * 512:(cg + 1) * 512], pt[:], xres[:, tt, cg * 512:(cg + 1) * 512], op=ALU.add),
                                reads=[pr, R_xc[tt][cg]], writes=[R_xc[tt][cg]])
                fw.barrier()
            with contextlib.ExitStack() as ph:
                fnw = sb(ph, "fnw_sb", [128, D]); junk = sb(ph, "junkf", [128, D], BF16); ssq = sb(ph, "ssqf", [128, 8]); R_f = fw.R("fnw"); R_sq = fw.R("ssqf"); R_jf = fw.R("junkf")
                ob = [sb(ph, f"ob{i}", [128, D]) for i in range(2)]; obR = [fw.R(f"ob{i}") for i in range(2)]
                fw.dma(SP, fnw[:], fnw_d[0:1, :].partition_broadcast(128), writes=[R_f])
                for tt in range(8):
                    fw.op(A, lambda tt=tt: nc.scalar.activation(junk[:], xres[:, tt, :], AF.Square, accum_out=ssq[:, tt:tt + 1]), reads=R_x[tt], writes=[R_jf, R_sq])
                    fw.op(V, lambda tt=tt: nc.vector.tensor_scalar(ssq[:, tt:tt + 1], ssq[:, tt:tt + 1], 1.0 / D, EPS, op0=ALU.mult, op1=ALU.add), reads=[R_sq], writes=[R_sq])
                    fw.op(A, lambda tt=tt: nc.scalar.sqrt(ssq[:, tt:tt + 1], ssq[:, tt:tt + 1]), reads=[R_sq], writes=[R_sq])
                    fw.op(V, lambda tt=tt: nc.vector.reciprocal(ssq[:, tt:tt + 1], ssq[:, tt:tt + 1]), reads=[R_sq], writes=[R_sq])
                    s_ = tt % 2
                    fw.op(V, lambda tt=tt, s_=s_: nc.vector.scalar_tensor_tensor(ob[s_][:], xres[:, tt, :], ssq[:, tt:tt + 1], fnw[:], op0=ALU.mult, op1=ALU.mult),
                          reads=R_x[tt] + [R_sq, R_f], writes=[obR[s_]])
                    fw.dma(SP, out_d[tt * 128:(tt + 1) * 128, :], ob[s_][:], reads=[obR[s_]], is_out=True)
                fw.barrier()
        for stk in reversed(open_stacks):
            stk.close()
        for ev in fw.out_events:
            fw._wait(SP, ev)
    return nc, dbg_out


def _bf(x):
    return np.ascontiguousarray(x.astype(np.float32))


def make_in_maps(inp):
    f32 = np.float32
    x = np.asarray(inp["x"], f32); c = np.asarray(inp["c"], f32)
    g = lambda k: np.asarray(inp[k], f32)
    def pl(v, n):
        return np.ascontiguousarray(v.reshape(n, 128).T)
    hd = np.arange(4, dtype=np.float64)
    lg = np.log1p(-np.exp2(-5.0 - hd))
    idx = np.arange(128, dtype=np.float64)
    diff = idx[None, :] - idx[:, None]
    maskT = np.zeros((128, 4, 128), f32)
    for h in range(4):
        maskT[:, h, :] = np.where(diff >= 0, np.exp(lg[h] * np.maximum(diff, 0.0)), 0.0) / 16.0
    qdec = np.zeros((128, 4, 128), f32)
    for h in range(4):
        qdec[:, h, :] = np.exp(lg[h] * (idx + 1.0))[None, :]
    kdec = np.zeros((128, 4), f32)
    for h in range(4):
        kdec[:, h] = np.exp(lg[h] * (127.0 - idx)) / 16.0
    freqs = (np.float32(10000.0) ** (-np.arange(128, dtype=f32) / np.float32(128))).astype(f32)
    pos = np.arange(2048, dtype=f32)
    ang = (pos[None, :] * freqs[:, None]).astype(f32)
    cos_all = np.cos(ang).astype(f32); sin_all = np.sin(ang).astype(f32)
    ident = np.eye(128, dtype=f32)
    iota = np.tile(np.arange(256, dtype=f32)[None, :], (128, 1))
    E9 = np.tile(np.repeat(np.arange(9, dtype=f32), 32)[None, :], (128, 1))
    hm = np.zeros((128, 2), f32); hm[:64, 0] = 1; hm[64:, 1] = 1
    bmask = np.kron(np.eye(4, dtype=f32), np.ones((32, 32), f32))
    rm3 = np.zeros((128, 1), f32); rm3[96:] = 1
    def pair2(a):
        return np.ascontiguousarray(a.reshape(32, 2, 64).transpose(1, 2, 0).reshape(128, 32))
    A_re = pair2(g("s5_a_re")[0]); A_im = pair2(g("s5_a_im")[0])
    LS = pair2(np.repeat(g("s5_log_step")[0][:, None], 64, axis=1))
    def pairB(bm):
        return np.ascontiguousarray(bm.reshape(32, 2, 64, 16).transpose(1, 2, 0, 3).reshape(128, 512))
    def pairC(cm):
        return np.ascontiguousarray(cm.reshape(32, 2, 16, 64).transpose(1, 3, 0, 2).reshape(128, 512))
    common = dict(
        w_ada=g("w_ada")[0], b_ada=g("b_ada")[0][None, :], n1w=pl(g("norm1_w")[0], 16), n2w=pl(g("norm2_w")[0], 16),
        fnw=g("final_norm_w")[None, :], w_in=g("w_in")[0], rnw=pl(g("ret_norm_w")[0], 8),
        A_re=A_re, A_im=A_im, LS=LS, B_re=pairB(g("s5_b_re")[0]), B_im=pairB(g("s5_b_im")[0]),
        C_re=pairC(g("s5_c_re")[0]), C_im=pairC(g("s5_c_im")[0]), Dq=pl(g("s5_d")[0].reshape(-1), 8),
        w_glu=g("w_glu")[0], bglu=pl(g("b_glu")[0], 8), w_out=g("w_out")[0], w_gu=g("w_gate_up")[0], w_dn=g("w_down")[0],
        ident=ident, maskT=maskT.reshape(128, 512), qdec=qdec.reshape(128, 512), kdec=kdec,
        cosp=np.ascontiguousarray(cos_all[:, :1024]), sinp=np.ascontiguousarray(sin_all[:, :1024]),
        iota=iota, E9=E9, hm=hm, bmask=bmask, rm3=rm3, iota5=np.tile(np.arange(512, dtype=f32)[None, :], (128, 1)),
    )
    maps = []
    for r in range(8):
        b, half = r // 2, r % 2
        m = dict(common)
        m["xm"] = np.ascontiguousarray(x[b, half * 1024:(half + 1) * 1024])
        m["xp"] = np.ascontiguousarray(x[b, 0:1024])
        m["pmask"] = np.full((128, 1), float(half), f32)
        m["cT"] = pl(c[b], 16)
        m["cosm"] = np.ascontiguousarray(cos_all[:, half * 1024:(half + 1) * 1024])
        m["sinm"] = np.ascontiguousarray(sin_all[:, half * 1024:(half + 1) * 1024])
        maps.append(m)
    return maps


def kernel(**inputs):
    nc, _ = build()
    maps = make_in_maps(inputs)
    res = run_bass_kernel_spmd(nc, maps, core_ids=list(range(8)))
    out = np.zeros((4, 2048, 2048), np.float32)
    for r in range(8):
        b, half = r // 2, r % 2
        out[b, half * 1024:(half + 1) * 1024] = res.results[r]["out"]
    return out
```

```python
import contextlib
import numpy as np
import ml_dtypes
import concourse.bass as bass
import concourse.mybir as mybir
from concourse.bass_utils import run_bass_kernel_spmd

F32 = mybir.dt.float32
BF16 = mybir.dt.bfloat16
I32 = mybir.dt.int32
AF = mybir.ActivationFunctionType
ALU = mybir.AluOpType

D = 2048
NT = 1024
NP = 1024
DFF = 5632
EPS = 1e-6
TWO_PI = 6.283185307179586


class Res:
    __slots__ = ("name", "w", "r")

    def __init__(self, name):
        self.name = name
        self.w = None
        self.r = {}


class FW:
    NDS = 6

    def __init__(self, nc, es):
        self.nc = nc
        self.engs = {"pe": nc.tensor, "act": nc.scalar, "dve": nc.vector, "pool": nc.gpsimd, "sp": nc.sync}
        self.sem = {k: es.enter_context(nc.semaphore("s_" + k)) for k in ["pe", "act", "dve", "pool"]}
        self.cnt = {k: 0 for k in self.sem}
        self.seen = {e: {} for e in self.engs}
        self.dsem = {q: [es.enter_context(nc.semaphore(f"d_{q}{i}")) for i in range(self.NDS)] for q in ["sp", "pool"]}
        self.dcnt = {q: [0] * self.NDS for q in self.dsem}
        self.drr = {q: 0 for q in self.dsem}
        self.psum = []
        for i in range(6):
            t = es.enter_context(nc.psum_tensor(f"psum{i}", [128, 512], F32))
            self.psum.append((t, Res(f"psum{i}")))
        self.y2 = (es.enter_context(nc.psum_tensor("psum_y2", [128, 1024], F32)), Res("psum_y2"))
        self.prr = 0
        self.out_events = []

    def R(self, name):
        return Res(name)

    def ps(self):
        self.prr = (self.prr + 1) % len(self.psum)
        t, r = self.psum[self.prr]
        return t, r

    def reserve(self, k):
        out = [self.psum.pop() for _ in range(k)]
        self.prr = 0
        return out

    def release(self, banks):
        self.psum.extend(banks)

    def _wait(self, e, ev):
        key, h, val = ev
        if self.seen[e].get(key, 0) >= val:
            return
        self.engs[e].wait_ge(h, val)
        self.seen[e][key] = val

    def _deps(self, e, reads, writes):
        skip = "pe" if e == "pe" else None
        for r in reads:
            if r.w is not None and r.w[0] != skip:
                self._wait(e, r.w)
        for w in writes:
            if w.w is not None and w.w[0] != skip:
                self._wait(e, w.w)
            for ev in w.r.values():
                if ev[0] != skip:
                    self._wait(e, ev)

    def _record(self, ev, reads, writes):
        for r in reads:
            r.r[ev[0]] = ev
        for w in writes:
            w.w = ev
            w.r = {}

    def op(self, e, fn, reads=(), writes=()):
        self._deps(e, reads, writes)
        ins = fn()
        self.cnt[e] += 1
        ins.then_inc(self.sem[e], 1)
        ev = (e, self.sem[e], self.cnt[e])
        self._record(ev, reads, writes)
        return ev

    def dma(self, q, out, in_, reads=(), writes=(), is_out=False):
        i = self.drr[q]
        self.drr[q] = (i + 1) % self.NDS
        key = f"d_{q}{i}"
        h = self.dsem[q][i]
        if self.dcnt[q][i] > 0:
            self._wait(q, (key, h, self.dcnt[q][i]))
        self._deps(q, reads, writes)
        ins = self.engs[q].dma_start(out=out, in_=in_)
        self.dcnt[q][i] += 16
        ins.then_inc(h, 16)
        ev = (key, h, self.dcnt[q][i])
        self._record(ev, reads, writes)
        if is_out:
            self.out_events.append(ev)
        return ev

    def barrier(self):
        evs = [(k, self.sem[k], self.cnt[k]) for k in self.sem if self.cnt[k] > 0]
        for q in self.dsem:
            for i in range(self.NDS):
                if self.dcnt[q][i] > 0:
                    evs.append((f"d_{q}{i}", self.dsem[q][i], self.dcnt[q][i]))
        for e in self.engs:
            for ev in evs:
                if not (e == "pe" and ev[0] == "pe"):
                    self._wait(e, ev)


def build(dbg=None, stop_after=99):
    nc = bass.Bass("TRN2", target_bir_lowering=False)
    dbg_out = {}

    def din(name, shape, dt=F32):
        return nc.dram_tensor(name, list(shape), dt, kind="ExternalInput").ap()

    xm = din("xm", [NT, D]); xp = din("xp", [NP, D])
    pmask_d = din("pmask", [128, 1]); cT_d = din("cT", [128, 16])
    w_ada = din("w_ada", [D, 6 * D]); b_ada = din("b_ada", [1, 6 * D])
    n1w_d = din("n1w", [128, 16]); n2w_d = din("n2w", [128, 16]); fnw_d = din("fnw", [1, D])
    w_in = din("w_in", [D, 5120]); rnw_d = din("rnw", [128, 8])
    A_re_d = din("A_re", [128, 32]); A_im_d = din("A_im", [128, 32]); LS_d = din("LS", [128, 32])
    B_re_d = din("B_re", [128, 512]); B_im_d = din("B_im", [128, 512])
    C_re_d = din("C_re", [128, 512]); C_im_d = din("C_im", [128, 512])
    Dq_d = din("Dq", [128, 8])
    w_glu = din("w_glu", [1024, 1024]); bglu_d = din("bglu", [128, 8])
    w_out = din("w_out", [D, D]); w_gu = din("w_gu", [D, 2 * DFF]); w_dn = din("w_dn", [DFF, D])
    ident_d = din("ident", [128, 128]); maskT_d = din("maskT", [128, 512]); qdec_d = din("qdec", [128, 512])
    kdec_d = din("kdec", [128, 4]); cosm_d = din("cosm", [128, NT]); sinm_d = din("sinm", [128, NT])
    cosp_d = din("cosp", [128, NP]); sinp_d = din("sinp", [128, NP])
    iota_d = din("iota", [128, 256]); E9_d = din("E9", [128, 288]); hm_d = din("hm", [128, 2])
    bmask_d = din("bmask", [128, 128]); rm3_d = din("rm3", [128, 1])
    out_d = nc.dram_tensor("out", [NT, D], F32, kind="ExternalOutput").ap()

    def dbg_tensor(name, shape):
        dbg_out[name] = nc.dram_tensor("dbg_" + name, list(shape), F32, kind="ExternalOutput").ap()
        return dbg_out[name]

    CD = [float(np.float32(np.exp(np.float32(128.0) * np.log1p(-np.exp2(np.float32(-5.0 - h)))))) for h in range(4)]

    with contextlib.ExitStack() as es:
        fw = FW(nc, es)
        V, A, P, T, SP = "dve", "act", "pool", "pe", "sp"

        def sb(stack, name, shape, dt=F32):
            return stack.enter_context(nc.sbuf_tensor("sb_" + name, list(shape), dt))

        def dump(name, ap, shape, reads, stack):
            t = sb(stack, "dmp_" + name, shape, F32)
            r = fw.R("dmp_" + name)
            fw.op(V, lambda: nc.vector.tensor_copy(t[:], ap), reads=reads, writes=[r])
            fw.dma(SP, dbg_tensor(name, shape)[:], t[:], reads=[r], is_out=True)

        identb = sb(es, "identb", [128, 128], BF16); identf = sb(es, "identf", [128, 128])
        pmask = sb(es, "pmask_sb", [128, 1])
        a1 = sb(es, "a1", [128, 16]); sh1 = sb(es, "sh1", [128, 16]); a2 = sb(es, "a2", [128, 16]); sh2 = sb(es, "sh2", [128, 16])
        g1bc = sb(es, "g1bc", [128, D]); g2bc = sb(es, "g2bc", [128, D])
        rnw = sb(es, "rnw_sb", [128, 8]); bglu = sb(es, "bglu_sb", [128, 8])
        R_const = fw.R("const")
        R_mod = fw.R("mod")
        for t, d in [(identf, ident_d), (pmask, pmask_d), (rnw, rnw_d), (bglu, bglu_d)]:
            fw.dma(SP, t[:], d[:], writes=[R_const])
        fw.dma(P, identb[:], ident_d[:], writes=[R_const])

        open_stacks = []
        ph25 = contextlib.ExitStack(); open_stacks.append(ph25)
        mixT = sb(ph25, "mixR", [128, 8, NT], BF16)
        ssmg = sb(ph25, "ssmg", [128, 8, NT], BF16)
        R_mix = fw.R("mixT"); R_u = fw.R("uT"); R_ssmg = fw.R("ssmg")
        ph13 = contextlib.ExitStack(); open_stacks.append(ph13)
        h1T = sb(ph13, "h1T", [128, 16, NP + NT], BF16)
        R_h1 = fw.R("h1T")
        R_mod1 = fw.R("mod1")

        def rms_norm_T(ph, tiles, avec, shvec, hT, hR, tag, bg=None, Rm=None):
            xn4 = sb(ph, "xn4" + tag, [128, 4, D], BF16); junk = sb(ph, "junk" + tag, [128, D], BF16)
            ssq = sb(ph, "ssq" + tag, [128, 1]); rstd = sb(ph, "rstd" + tag, [128, 1])
            R_xn = [fw.R(f"xn{i}") for i in range(4)]; R_j = fw.R("junk"); R_s = fw.R("ssq")
            for gi in range(len(tiles) // 4):
                for t4 in range(4):
                    xap, xR = tiles[gi * 4 + t4]()
                    xRl = xR if isinstance(xR, list) else [xR]
                    fw.op(A, lambda xap=xap: nc.scalar.activation(junk[:], xap, AF.Square, accum_out=ssq[:]),
                          reads=xRl, writes=[R_j, R_s])
                    fw.op(V, lambda: nc.vector.tensor_scalar(rstd[:], ssq[:], 1.0 / D, EPS, op0=ALU.mult, op1=ALU.add),
                          reads=[R_s], writes=[R_s])
                    fw.op(A, lambda: nc.scalar.sqrt(rstd[:], rstd[:]), reads=[R_s], writes=[R_s])
                    fw.op(V, lambda: nc.vector.reciprocal(rstd[:], rstd[:]), reads=[R_s], writes=[R_s])
                    fw.op(V, lambda xap=xap, t4=t4: nc.vector.tensor_scalar(
                        xn4[:, t4, :], xap, rstd[:, 0:1], None, op0=ALU.mult), reads=xRl + [R_s], writes=[R_xn[t4]])
                for fc in range(16):
                    pt, pr = fw.ps()
                    ptb = pt[:].bitcast(BF16)
                    for t4 in range(4):
                        fw.op(T, lambda ptb=ptb, t4=t4, fc=fc: nc.tensor.transpose(
                            ptb[:, t4 * 128:(t4 + 1) * 128], xn4[:, t4, fc * 128:(fc + 1) * 128], identb[:]),
                            reads=[R_xn[t4], R_const], writes=[pr])
                    fw.op(A, lambda ptb=ptb, fc=fc, gi=gi: nc.scalar.activation(
                        hT[:, fc, gi * 512:(gi + 1) * 512], ptb[:, 0:512], AF.Identity,
                        bias=shvec[:, fc:fc + 1], scale=avec[:, fc:fc + 1]), reads=[pr, Rm if Rm is not None else R_mod], writes=[hR])
                    if bg is not None:
                        next(bg, None)


        with contextlib.ExitStack() as ph:
            cT = sb(ph, "cT_sb", [128, 16]); condb = sb(ph, "condb", [128, 16], BF16)
            crep = sb(ph, "crep", [128, 16, 128], BF16)
            n1w = sb(ph, "n1w_sb", [128, 16]); n2w = sb(ph, "n2w_sb", [128, 16])
            seg_sb = sb(ph, "seg_sb", [128, D]); bbc = sb(ph, "bbc", [128, D])
            tmpd = sb(ph, "tmpd", [128, 16, 128])
            wsl = [sb(ph, f"wada{i}", [128, D], BF16) for i in range(4)]
            wslR = [fw.R(f"wada{i}") for i in range(4)]
            R_c = fw.R("c"); R_seg = fw.R("seg"); R_bbc = fw.R("bbc"); R_tmpd = fw.R("tmpd")
            fw.dma(SP, cT[:], cT_d[:], writes=[R_c])
            fw.dma(SP, n1w[:], n1w_d[:], writes=[R_c])
            fw.dma(SP, n2w[:], n2w_d[:], writes=[R_c])
            fw.op(A, lambda: nc.scalar.activation(condb[:], cT[:], AF.Silu), reads=[R_c], writes=[R_c])
            fw.op(V, lambda: nc.vector.tensor_copy(crep[:], condb[:].unsqueeze(2).to_broadcast([128, 16, 128])),
                  reads=[R_c], writes=[R_c])
            wi = [0]

            def mod_segs(segs, banks_fn):
                for seg in segs:
                    Rm = R_mod1 if seg in (0, 1) else R_mod
                    fw.dma(SP, bbc[:], b_ada[0:1, seg * D:(seg + 1) * D].partition_broadcast(128), writes=[R_bbc])
                    pss = banks_fn()
                    for kc in range(16):
                        s = wi[0] % 4; wi[0] += 1
                        fw.dma(P, wsl[s][:], w_ada[kc * 128:(kc + 1) * 128, seg * D:(seg + 1) * D], writes=[wslR[s]])
                        for cg in range(4):
                            pt, pr = pss[cg]
                            fw.op(T, lambda pt=pt, s=s, cg=cg, kc=kc: nc.tensor.matmul(
                                pt, crep[:, kc, :], wsl[s][:, cg * 512:(cg + 1) * 512], start=(kc == 0), stop=(kc == 15)),
                                reads=[R_c, wslR[s]], writes=[pr])
                        yield
                    dst = g1bc if seg == 2 else (g2bc if seg == 5 else seg_sb)
                    Rd = R_mod if seg in (2, 5) else R_seg
                    for cg in range(4):
                        pt, pr = pss[cg]
                        fw.op(V, lambda pt=pt, cg=cg, dst=dst: nc.vector.tensor_tensor(
                            dst[:, cg * 512:(cg + 1) * 512], pt, bbc[:, cg * 512:(cg + 1) * 512], op=ALU.add),
                            reads=[pr, R_bbc], writes=[Rd])
                    if seg in (0, 1, 3, 4):
                        vec = {0: sh1, 1: a1, 3: sh2, 4: a2}[seg]
                        fw.op(V, lambda: nc.vector.tensor_tensor(
                            tmpd[:], seg_sb[:].rearrange("p (f m) -> p f m", m=128),
                            identf[:].unsqueeze(1).to_broadcast([128, 16, 128]), op=ALU.mult),
                            reads=[R_seg, R_const], writes=[R_tmpd])
                        fw.op(V, lambda vec=vec: nc.vector.tensor_reduce(
                            vec[:], tmpd[:], axis=mybir.AxisListType.X, op=ALU.add), reads=[R_tmpd], writes=[Rm])
                        if seg in (1, 4):
                            nw = n1w if seg == 1 else n2w
                            fw.op(V, lambda vec=vec: nc.vector.tensor_scalar(vec[:], vec[:], 1.0, None, op0=ALU.add),
                                  reads=[Rm], writes=[Rm])
                            fw.op(V, lambda vec=vec, nw=nw: nc.vector.tensor_tensor(vec[:], vec[:], nw[:], op=ALU.mult),
                                  reads=[R_c, Rm], writes=[Rm])
                    yield

            def banks_rot():
                return [(t[:], r) for (t, r) in [fw.ps() for _ in range(4)]]
            for _ in mod_segs([1, 0], banks_rot):
                pass
            if dbg == 0:
                for _ in mod_segs([2, 4, 3, 5], banks_rot):
                    pass
                dump("a1", a1[:], [128, 16], [R_mod1], ph); dump("sh1", sh1[:], [128, 16], [R_mod1], ph)
                dump("a2", a2[:], [128, 16], [R_mod], ph); dump("g1", g1bc[:], [128, D], [R_mod], ph); dump("n1w", n1w[:], [128, 16], [R_c], ph)
            if stop_after >= 1:
                resv = fw.reserve(2)
                y2t, y2r = fw.y2
                fixed = [(y2t[:, 0:512], y2r), (y2t[:, 512:1024], fw.R("y2b")), (resv[0][0][:], resv[0][1]), (resv[1][0][:], resv[1][1])]
                bg = mod_segs([2, 4, 3, 5], lambda: fixed)
                with contextlib.ExitStack() as ph1:
                    xs = [sb(ph1, f"xs{i}", [128, D]) for i in range(3)]
                    xsR = [fw.R(f"xs{i}") for i in range(3)]
                    cnt = [0]

                    def mk_loader(src, t):
                        def f():
                            s = cnt[0] % 3; cnt[0] += 1
                            fw.dma(SP, xs[s][:], src[t * 128:(t + 1) * 128, :], writes=[xsR[s]])
                            return xs[s][:], xsR[s]
                        return f
                    tiles = [mk_loader(xp, t) for t in range(8)] + [mk_loader(xm, t) for t in range(8)]
                    rms_norm_T(ph1, tiles, a1, sh1, h1T, R_h1, "_n1", bg=bg, Rm=R_mod1)
                    for _ in bg:
                        pass
                    fw.release(resv)
                    if dbg == 1:
                        dump("h1T0", h1T[:, 0, :], [128, 2048], [R_h1], ph1); dump("h1T15", h1T[:, 15, :], [128, 2048], [R_h1], ph1)
                    fw.barrier()
            else:
                fw.barrier()

        wcnt = [0]

        def load_w(slots, slotR, src_ap):
            s = wcnt[0] % len(slots); wcnt[0] += 1
            fw.dma(P, slots[s][:], src_ap, writes=[slotR[s]])
            return slots[s], slotR[s]

        w_in_v = w_in.rearrange("(kc p) c -> p kc c", p=128)

        if stop_after >= 2:
            with contextlib.ExitStack() as ph:
                wv = [sb(ph, f"wv{i}", [128, 16, 256], BF16) for i in range(3)]
                wvR = [fw.R(f"wv{i}") for i in range(3)]
                maskT = sb(ph, "maskT", [128, 4, 128]); qdec = sb(ph, "qdec", [128, 4, 128]); kdec = sb(ph, "kdec", [128, 4])
                kdecm = sb(ph, "kdecm", [128, 4])
                cosm = sb(ph, "cosm", [128, NT]); sinm = sb(ph, "sinm", [128, NT])
                cosp = sb(ph, "cosp", [128, NP]); sinp = sb(ph, "sinp", [128, NP])
                R_tab = fw.R("tab")
                fw.dma(SP, maskT[:], maskT_d[:].rearrange("p (h i) -> p h i", h=4), writes=[R_tab])
                fw.dma(SP, qdec[:], qdec_d[:].rearrange("p (h i) -> p h i", h=4), writes=[R_tab])
                fw.dma(SP, kdec[:], kdec_d[:], writes=[R_tab])
                for t, d in [(cosm, cosm_d), (sinm, sinm_d), (cosp, cosp_d), (sinp, sinp_d)]:
                    fw.dma(SP, t[:], d[:], writes=[R_tab])
                fw.op(V, lambda: nc.vector.tensor_scalar(kdecm[:], kdec[:], pmask[:, 0:1], None, op0=ALU.mult),
                      reads=[R_tab, R_const], writes=[R_tab])
                qT = sb(ph, "qT", [128, 2, NT], BF16); qdT = sb(ph, "qdT", [128, 2, NT], BF16)
                kT = sb(ph, "kT", [128, 2, NP + NT], BF16)
                ktok = sb(ph, "ktok", [128, 16, 256], BF16); vtok = sb(ph, "vtok", [128, 16, 256], BF16)
                gtok = sb(ph, "gtok", [128, 8, 256], BF16)
                state = sb(ph, "state", [128, 2, 256])
                rt = [sb(ph, f"rt{i}", [128, 512]) for i in range(4)]
                ssflat = ssmg[:].rearrange("p a b -> p (a b)")
                stateb8 = ssflat[:, 0:4096].rearrange("p (n d v) -> p n d v", n=8, d=2)
                yn4L = [ssflat[:, 4096:6144].bitcast(F32).rearrange("p (a b) -> p a b", a=4)] * 2
                sq2L = [ssflat[:, 6144:8192].bitcast(F32).rearrange("p (k b) -> p k b", k=2)] * 2
                msk4L = [sb(ph, f"msk4{i}", [128, 4, 128], BF16) for i in range(2)]
                rtok4L = [sb(ph, "rtok4", [128, 4, 256], BF16)] * 2
                stL = [sb(ph, f"st{i}", [128, 16]) for i in range(2)]
                R_q = fw.R("qT"); R_qd = fw.R("qdT"); R_k = fw.R("kT"); R_kt = fw.R("ktok"); R_v = fw.R("vtok"); R_g = fw.R("gtok")
                R_state = fw.R("state"); R_sb = fw.R("stateb"); R_rt = [fw.R(f"rt{i}") for i in range(4)]
                R_mskL = [fw.R(f"msk{i}") for i in range(2)]; R_rtokL = [fw.R("rtok4")] * 2
                _ryn = fw.R("yn4"); _rsq = fw.R("sq2"); R_ynL = [_ryn, _ryn]; R_sqL = [_rsq, _rsq]
                R_stL = [fw.R(f"st{i}") for i in range(2)]; R_sb8 = [fw.R(f"stateb8_{i}") for i in range(8)]

                def projT_rot(wt, tok0, ntok, dstT, Rdst, ctab, stab, toff):
                    w_, wR_ = wt
                    for g in range(ntok // 512):
                        p1, p1R = fw.ps(); p2, p2R = fw.ps()
                        for (w, wR, pt, pr) in [(w_[:, :, 0:128], wR_, p1, p1R), (w_[:, :, 128:256], wR_, p2, p2R)]:
                            for kc in range(16):
                                fw.op(T, lambda w=w, pt=pt, kc=kc, g=g: nc.tensor.matmul(
                                    pt[:], w[:, kc, :], h1T[:, kc, tok0 + g * 512: tok0 + (g + 1) * 512],
                                    start=(kc == 0), stop=(kc == 15)), reads=[wR, R_h1], writes=[pr])
                        cs = ctab[:, toff + g * 512: toff + (g + 1) * 512]; sn = stab[:, toff + g * 512: toff + (g + 1) * 512]
                        fw.op(V, lambda p1=p1, cs=cs: nc.vector.tensor_tensor(rt[0][:], p1[:], cs, op=ALU.mult), reads=[p1R, R_tab], writes=[R_rt[0]])
                        fw.op(V, lambda p2=p2, sn=sn: nc.vector.tensor_tensor(rt[1][:], p2[:], sn, op=ALU.mult), reads=[p2R, R_tab], writes=[R_rt[1]])
                        fw.op(V, lambda p2=p2, cs=cs: nc.vector.tensor_tensor(rt[2][:], p2[:], cs, op=ALU.mult), reads=[p2R, R_tab], writes=[R_rt[2]])
                        fw.op(V, lambda p1=p1, sn=sn: nc.vector.tensor_tensor(rt[3][:], p1[:], sn, op=ALU.mult), reads=[p1R, R_tab], writes=[R_rt[3]])
                        o0 = dstT[:, 0, tok0 - (0 if dstT is kT else NP) + g * 512: tok0 - (0 if dstT is kT else NP) + (g + 1) * 512]
                        o1 = dstT[:, 1, tok0 - (0 if dstT is kT else NP) + g * 512: tok0 - (0 if dstT is kT else NP) + (g + 1) * 512]
                        fw.op(P, lambda o0=o0: nc.gpsimd.tensor_tensor(o0, rt[0][:], rt[1][:], op=ALU.subtract),
                              reads=[R_rt[0], R_rt[1]], writes=[Rdst])
                        fw.op(P, lambda o1=o1: nc.gpsimd.tensor_tensor(o1, rt[2][:], rt[3][:], op=ALU.add),
                              reads=[R_rt[2], R_rt[3]], writes=[Rdst])

                deferred = []
                for h in range(4):
                    wq_ = load_w(wv, wvR, w_in_v[:, :, h * 256:(h + 1) * 256])
                    wk_ = load_w(wv, wvR, w_in_v[:, :, 1024 + h * 256: 1024 + (h + 1) * 256])
                    projT_rot(wq_, NP, NT, qT, R_q, cosm, sinm, 0)
                    projT_rot(wk_, 0, NP, kT, R_k, cosp, sinp, 0)
                    projT_rot(wk_, NP, NT, kT, R_k, cosm, sinm, 0)
                    while deferred:
                        deferred.pop(0)()
                    for dc in range(2):
                        fw.op(P, lambda dc=dc, h=h: nc.gpsimd.tensor_tensor(
                            qdT[:, dc, :].rearrange("p (t i) -> p t i", i=128), qT[:, dc, :].rearrange("p (t i) -> p t i", i=128),
                            qdec[:, h, :].unsqueeze(1).to_broadcast([128, 8, 128]), op=ALU.mult),
                            reads=[R_q, R_tab], writes=[R_qd])
                    wvt, wvtR = load_w(wv, wvR, w_in_v[:, :, 2048 + h * 256: 2048 + (h + 1) * 256])
                    for tt in range(0, 16, 2):
                        pt, pr = fw.ps()
                        for u2 in range(2):
                            for kc in range(16):
                                fw.op(T, lambda pt=pt, u2=u2, kc=kc, tt=tt: nc.tensor.matmul(
                                    pt[:, u2 * 256:(u2 + 1) * 256], h1T[:, kc, (tt + u2) * 128:(tt + u2 + 1) * 128], wvt[:, kc, :],
                                    start=(kc == 0), stop=(kc == 15)), reads=[wvtR, R_h1], writes=[pr])
                        fw.op(A, lambda pt=pt, tt=tt: nc.scalar.copy(vtok[:, tt:tt + 2, :], pt[:].rearrange("p (a b) -> p a b", a=2)),
                              reads=[pr], writes=[R_v])
                    wgt, wgtR = load_w(wv, wvR, w_in_v[:, :, 3072 + h * 256: 3072 + (h + 1) * 256])
                    for tt in range(0, 8, 2):
                        pt, pr = fw.ps()
                        for u2 in range(2):
                            for kc in range(16):
                                fw.op(T, lambda pt=pt, u2=u2, kc=kc, tt=tt: nc.tensor.matmul(
                                    pt[:, u2 * 256:(u2 + 1) * 256], h1T[:, kc, NP + (tt + u2) * 128: NP + (tt + u2 + 1) * 128], wgt[:, kc, :],
                                    start=(kc == 0), stop=(kc == 15)), reads=[wgtR, R_h1], writes=[pr])
                        fw.op(A, lambda pt=pt, tt=tt: nc.scalar.activation(
                            gtok[:, tt:tt + 2, :], pt[:].rearrange("p (a b) -> p a b", a=2), AF.Silu), reads=[pr], writes=[R_g])
                    for n in range(0, 16, 2):
                        pt, pr = fw.ps(); ptb = pt[:].bitcast(BF16)
                        for u2 in range(2):
                            for dc in range(2):
                                fw.op(T, lambda ptb=ptb, u2=u2, dc=dc, n=n: nc.tensor.transpose(
                                    ptb[:, u2 * 256 + dc * 128: u2 * 256 + (dc + 1) * 128], kT[:, dc, (n + u2) * 128:(n + u2 + 1) * 128], identb[:]),
                                    reads=[R_k, R_const], writes=[pr])
                        kd = kdecm if n < 8 else kdec
                        fw.op(A, lambda ptb=ptb, n=n, kd=kd, h=h: nc.scalar.activation(
                            ktok[:, n:n + 2, :], ptb[:, 0:512].rearrange("p (a b) -> p a b", a=2), AF.Copy, scale=kd[:, h:h + 1]),
                            reads=[pr, R_tab], writes=[R_kt])
                    for bt in range(2):
                        i0_ = bt * 4
                        msk4 = msk4L[bt]; R_msk = R_mskL[bt]
                        ps_s, ps_sR = fw.ps()
                        for ci in range(4):
                            i = i0_ + ci; n = 8 + i
                            for dc in range(2):
                                fw.op(T, lambda dc=dc, n=n, i=i, ci=ci, ps_s=ps_s: nc.tensor.matmul(
                                    ps_s[:, ci * 128:(ci + 1) * 128], kT[:, dc, n * 128:(n + 1) * 128], qT[:, dc, i * 128:(i + 1) * 128],
                                    start=(dc == 0), stop=(dc == 1)), reads=[R_k, R_q], writes=[ps_sR])
                        fw.op(V, lambda ps_s=ps_s, h=h: nc.vector.tensor_tensor(
                            msk4[:], ps_s[:].rearrange("p (a b) -> p a b", a=4), maskT[:, h, :].unsqueeze(1).to_broadcast([128, 4, 128]), op=ALU.mult),
                            reads=[ps_sR, R_tab], writes=[R_msk])
                    fw.op(V, lambda: nc.vector.memset(state[:], 0.0), writes=[R_state])
                    for n in range(15):
                        ps_k, ps_kR = fw.ps()
                        for dc in range(2):
                            fw.op(T, lambda ps_k=ps_k, dc=dc, n=n: nc.tensor.matmul(
                                ps_k[:, dc * 256:(dc + 1) * 256], ktok[:, n, dc * 128:(dc + 1) * 128], vtok[:, n, :], start=True, stop=True),
                                reads=[R_kt, R_v], writes=[ps_kR])
                        fw.op(V, lambda ps_k=ps_k, h=h: nc.vector.scalar_tensor_tensor(
                            state[:], state[:], CD[h], ps_k[:].rearrange("p (a b) -> p a b", a=2), op0=ALU.mult, op1=ALU.add),
                            reads=[ps_kR], writes=[R_state])
                        if n >= 7:
                            fw.op(A, lambda n=n: nc.scalar.copy(stateb8[:, n - 7, :, :], state[:]), reads=[R_state], writes=[R_sb8[n - 7]])
                    for bt in range(2):
                        i0_ = bt * 4
                        msk4, yn4, rtok4, sq2, st = msk4L[bt], yn4L[bt], rtok4L[bt], sq2L[bt], stL[bt]
                        R_msk, R_yn, R_rtok, R_sq, R_st = R_mskL[bt], R_ynL[bt], R_rtokL[bt], R_sqL[bt], R_stL[bt]
                        pos_ = [fw.ps(), fw.ps()]
                        for ci in range(4):
                            i = i0_ + ci; n = 8 + i
                            ps_o, ps_oR = pos_[ci // 2]
                            reg = ps_o[:, (ci % 2) * 256:(ci % 2 + 1) * 256]
                            fw.op(T, lambda reg=reg, n=n, ci=ci: nc.tensor.matmul(reg, msk4[:, ci, :], vtok[:, n, :], start=True, stop=False),
                                  reads=[R_msk, R_v], writes=[ps_oR])
                            if i > 0 or True:
                                for dc in range(2):
                                    fw.op(T, lambda reg=reg, dc=dc, i=i: nc.tensor.matmul(
                                        reg, qdT[:, dc, i * 128:(i + 1) * 128], stateb8[:, i, dc, :], start=False, stop=(dc == 1)),
                                        reads=[R_qd, R_sb8[i]], writes=[ps_oR])
                        if bt == 1:
                            deferred.pop(0)()
                        for ci in range(4):
                            ps_o, ps_oR = pos_[ci // 2]
                            reg = ps_o[:, (ci % 2) * 256:(ci % 2 + 1) * 256]
                            fw.op(A, lambda reg=reg, ci=ci: nc.scalar.activation(sq2[:, 0, 0:256], reg, AF.Copy, accum_out=st[:, ci:ci + 1]),
                                  reads=[ps_oR], writes=[R_sq, R_st])
                            fw.op(A, lambda reg=reg, ci=ci: nc.scalar.activation(sq2[:, 1, 0:256], reg, AF.Square, accum_out=st[:, 4 + ci:5 + ci]),
                                  reads=[ps_oR], writes=[R_sq, R_st])
                        fw.op(V, lambda: nc.vector.tensor_scalar(st[:, 0:8], st[:, 0:8], 1.0 / 256, None, op0=ALU.mult), reads=[R_st], writes=[R_st])
                        fw.op(V, lambda: nc.vector.tensor_tensor(st[:, 8:12], st[:, 0:4], st[:, 0:4], op=ALU.mult), reads=[R_st], writes=[R_st])
                        fw.op(V, lambda: nc.vector.scalar_tensor_tensor(st[:, 12:16], st[:, 4:8], EPS, st[:, 8:12], op0=ALU.add, op1=ALU.subtract),
                              reads=[R_st], writes=[R_st])
                        fw.op(A, lambda: nc.scalar.sqrt(st[:, 12:16], st[:, 12:16]), reads=[R_st], writes=[R_st])
                        fw.op(V, lambda: nc.vector.reciprocal(st[:, 12:16], st[:, 12:16]), reads=[R_st], writes=[R_st])
                        for ci in range(4):
                            ps_o, ps_oR = pos_[ci // 2]
                            reg = ps_o[:, (ci % 2) * 256:(ci % 2 + 1) * 256]
                            fw.op(V, lambda reg=reg, ci=ci: nc.vector.tensor_scalar(
                                yn4[:, ci, :], reg, st[:, ci:ci + 1], st[:, 12 + ci:13 + ci], op0=ALU.subtract, op1=ALU.mult),
                                reads=[ps_oR, R_st], writes=[R_yn])
                        fw.op(P, lambda i0_=i0_: nc.gpsimd.tensor_tensor(rtok4[:], yn4[:], gtok[:, i0_:i0_ + 4, :], op=ALU.mult),
                              reads=[R_yn, R_g], writes=[R_rtok])
                        def _tr_evac(rtok4=rtok4, R_rtok=R_rtok, i0_=i0_, h=h):
                            ps_t, ps_tR = fw.ps(); ptb = ps_t[:].bitcast(BF16)
                            for dc in range(2):
                                for ci in range(4):
                                    fw.op(T, lambda ptb=ptb, dc=dc, ci=ci: nc.tensor.transpose(
                                        ptb[:, dc * 512 + ci * 128: dc * 512 + (ci + 1) * 128], rtok4[:, ci, dc * 128:(dc + 1) * 128], identb[:]),
                                        reads=[R_rtok, R_const], writes=[ps_tR])
                            for dc in range(2):
                                fw.op(A, lambda ptb=ptb, dc=dc, i0_=i0_, h=h: nc.scalar.activation(
                                    mixT[:, h * 2 + dc, i0_ * 128:(i0_ + 4) * 128], ptb[:, dc * 512:(dc + 1) * 512], AF.Copy,
                                    scale=rnw[:, h * 2 + dc: h * 2 + dc + 1]), reads=[ps_tR, R_const], writes=[R_mix])
                        deferred.append(_tr_evac)
                while deferred:
                    deferred.pop(0)()
                fw.barrier()
            if dbg == 2:
                with contextlib.ExitStack() as dph:
                    dump("mixT0", mixT[:, 0, :], [128, NT], [R_mix], dph); dump("mixT7", mixT[:, 7, :], [128, NT], [R_mix], dph)
                    fw.barrier()

        iota5_d = din("iota5", [128, 512])
        if stop_after >= 3:
            with contextlib.ExitStack() as ph:
                uT = sb(ph, "uT", [128, 8, NP + NT], BF16)
                wsl = [sb(ph, f"winu{i}", [128, 16, 128], BF16) for i in range(2)]
                wslR = [fw.R(f"winu{i}") for i in range(2)]
                for cc in range(8):
                    w, wR = load_w(wsl, wslR, w_in_v[:, :, 4096 + cc * 128: 4096 + (cc + 1) * 128])
                    for g in range(4):
                        pt, pr = fw.ps()
                        for kc in range(16):
                            fw.op(T, lambda pt=pt, w=w, kc=kc, g=g: nc.tensor.matmul(
                                pt[:], w[:, kc, :], h1T[:, kc, g * 512:(g + 1) * 512], start=(kc == 0), stop=(kc == 15)),
                                reads=[wR, R_h1], writes=[pr])
                        if g < 2:
                            fw.op(A, lambda pt=pt, cc=cc, g=g: nc.scalar.activation(
                                uT[:, cc, g * 512:(g + 1) * 512], pt[:], AF.Copy, scale=pmask[:, 0:1]), reads=[pr, R_const], writes=[R_u])
                        else:
                            fw.op(A, lambda pt=pt, cc=cc, g=g: nc.scalar.copy(uT[:, cc, g * 512:(g + 1) * 512], pt[:]), reads=[pr], writes=[R_u])
                fw.barrier()
                scr_off = [0]

                h1flat = h1T[:].rearrange("p a b -> p (a b)")

                def scr(shape, dt=F32):
                    nel = int(np.prod(shape[1:]))
                    esz = 4 if dt in (F32, I32) else 2
                    nby = -(-(nel * esz) // 64) * 64
                    o = scr_off[0]; scr_off[0] += nby
                    assert scr_off[0] <= 65536, "out of h1T scratch"
                    flat = h1flat[:, o // 2:(o + nel * esz) // 2]
                    if dt != BF16:
                        flat = flat.bitcast(dt)
                    v = flat
                    if len(shape) == 3:
                        v = v.rearrange("p (a b) -> p a b", a=shape[1])
                    elif len(shape) == 4:
                        v = v.rearrange("p (a b c) -> p a b c", a=shape[1], b=shape[2])
                    elif len(shape) == 5:
                        v = v.rearrange("p (a b c d) -> p a b c d", a=shape[1], b=shape[2], c=shape[3])
                    return v

                class _T:
                    def __init__(self, v):
                        self.v = v

                    def __getitem__(self, k):
                        return self.v[k] if not (isinstance(k, slice) and k == slice(None)) else self.v

                def scrT(shape, dt=F32):
                    return _T(scr(shape, dt))
                def t32(name):
                    return sb(ph, name, [128, 32])[:]
                Are = t32("Are"); Aim = t32("Aim"); dtt = t32("dtt"); ar = t32("ar"); ph2 = t32("ph2"); ph8 = t32("ph8")
                cre = t32("cre"); cim = t32("cim"); tA = t32("tA"); tB = t32("tB"); tC = t32("tC")
                hm = sb(ph, "hm_sb", [128, 2]); bmask = sb(ph, "bmask_sb", [128, 128]); Dq = sb(ph, "Dq_sb", [128, 8])
                rm3 = sb(ph, "rm3_sb", [128, 1]); iota = sb(ph, "iota_sb", [128, 256])
                E9 = sb(ph, "E9_sb", [128, 9, 32]); magE = sb(ph, "magE", [128, 9, 32]); PWre = sb(ph, "PWre", [128, 9, 32]); PWim = sb(ph, "PWim", [128, 9, 32])
                eF = sb(ph, "eF", [128, 9, 32]); eF2 = sb(ph, "eF2", [128, 9, 32]); eA = sb(ph, "eA", [128, 9, 32]); eI = sb(ph, "eI", [128, 9, 32], I32)
                small7 = scr([128, 7, 32, 16])
                Bre, Bim, Cre, Cim, bbre, bbim, tb1 = [small7[:, i_] for i_ in range(7)]
                big_off = scr_off[0]
                big1 = scr([128, 9, 32, 16]); big2 = scr([128, 9, 32, 16])
                CL = [sb(ph, "CLre", [128, 9, 32, 16], BF16), sb(ph, "CLim", [128, 9, 32, 16], BF16)]
                BstA = [sb(ph, "BstAre", [128, 8, 32, 16], BF16), sb(ph, "BstAim", [128, 8, 32, 16], BF16)]
                R_p = fw.R("s5p")
                for t, d in [(Are, A_re_d), (Aim, A_im_d), (dtt, LS_d), (hm[:], hm_d), (bmask[:], bmask_d), (Dq[:], Dq_d), (iota[:], iota_d), (rm3[:], rm3_d)]:
                    fw.dma(SP, t, d[:], writes=[R_p])
                fw.dma(SP, E9[:], E9_d[:].rearrange("p (a b) -> p a b", b=32), writes=[R_p])
                for t, d in [(Bre, B_re_d), (Bim, B_im_d), (Cre, C_re_d), (Cim, C_im_d)]:
                    fw.dma(SP, t, d[:].rearrange("p (a b) -> p a b", b=16), writes=[R_p])
                RW = [R_p]

                def vop(f):
                    fw.op(V, f, reads=RW, writes=RW)

                def aop(f):
                    fw.op(A, f, reads=RW, writes=RW)

                def sincos(ang_t, sin_out, cos_out, tmpF, tmpF2, tmpI):
                    vop(lambda: nc.vector.tensor_copy(tmpI, ang_t))
                    vop(lambda: nc.vector.tensor_copy(tmpF, tmpI))
                    vop(lambda: nc.vector.tensor_tensor(tmpF, ang_t, tmpF, op=ALU.subtract))
                    aop(lambda: nc.scalar.activation(sin_out, tmpF, AF.Sin, scale=TWO_PI))
                    vop(lambda: nc.vector.tensor_scalar(tmpF2, ang_t, 0.25, None, op0=ALU.add))
                    vop(lambda: nc.vector.tensor_copy(tmpI, tmpF2))
                    vop(lambda: nc.vector.tensor_copy(tmpF, tmpI))
                    vop(lambda: nc.vector.tensor_tensor(tmpF, tmpF2, tmpF, op=ALU.subtract))
                    aop(lambda: nc.scalar.activation(cos_out, tmpF, AF.Sin, scale=TWO_PI))

                zlhs = sb(ph, "zlhs", [128, 128], BF16)
                vop(lambda: nc.vector.memset(zlhs[:], 0.0))
                aop(lambda: nc.scalar.activation(dtt, dtt, AF.Exp))
                vop(lambda: nc.vector.tensor_tensor(ar, Are, dtt, op=ALU.mult))
                vop(lambda: nc.vector.tensor_tensor(ph2, Aim, dtt, op=ALU.mult))
                vop(lambda: nc.vector.tensor_scalar(ph2, ph2, 1.0 / TWO_PI, None, op0=ALU.mult))
                vop(lambda: nc.vector.tensor_scalar(ph8, ph2, 8.0, None, op0=ALU.mult))
                b9 = lambda t: t.unsqueeze(1).to_broadcast([128, 9, 32])
                vop(lambda: nc.vector.tensor_tensor(eA[:], E9[:], b9(ar), op=ALU.mult))
                aop(lambda: nc.scalar.activation(magE[:], eA[:], AF.Exp))
                vop(lambda: nc.vector.tensor_tensor(eA[:], E9[:], b9(ph2), op=ALU.mult))
                sincos(eA[:], PWim[:], PWre[:], eF[:], eF2[:], eI[:])
                vop(lambda: nc.vector.tensor_tensor(PWre[:], PWre[:], magE[:], op=ALU.mult))
                vop(lambda: nc.vector.tensor_tensor(PWim[:], PWim[:], magE[:], op=ALU.mult))
                lre = PWre[:, 1, :]; lim = PWim[:, 1, :]
                vop(lambda: nc.vector.tensor_scalar(tA, lre, -1.0, None, op0=ALU.add))
                vop(lambda: nc.vector.tensor_tensor(tB, Are, Are, op=ALU.mult))
                vop(lambda: nc.vector.tensor_tensor(tC, Aim, Aim, op=ALU.mult))
                vop(lambda: nc.vector.tensor_tensor(tB, tB, tC, op=ALU.add))
                vop(lambda: nc.vector.reciprocal(tB, tB))
                vop(lambda: nc.vector.tensor_tensor(cre, tA, Are, op=ALU.mult))
                vop(lambda: nc.vector.tensor_tensor(tC, lim, Aim, op=ALU.mult))
                vop(lambda: nc.vector.tensor_tensor(cre, cre, tC, op=ALU.add))
                vop(lambda: nc.vector.tensor_tensor(cre, cre, tB, op=ALU.mult))
                vop(lambda: nc.vector.tensor_tensor(cim, lim, Are, op=ALU.mult))
                vop(lambda: nc.vector.tensor_tensor(tC, tA, Aim, op=ALU.mult))
                vop(lambda: nc.vector.tensor_tensor(cim, cim, tC, op=ALU.subtract))
                vop(lambda: nc.vector.tensor_tensor(cim, cim, tB, op=ALU.mult))
                bc = lambda t: t.unsqueeze(2).to_broadcast([128, 32, 16])
                vop(lambda: nc.vector.tensor_tensor(bbre, Bre, bc(cre), op=ALU.mult))
                vop(lambda: nc.vector.tensor_tensor(tb1, Bim, bc(cim), op=ALU.mult))
                vop(lambda: nc.vector.tensor_tensor(bbre, bbre, tb1, op=ALU.subtract))
                vop(lambda: nc.vector.tensor_tensor(bbim, Bim, bc(cre), op=ALU.mult))
                vop(lambda: nc.vector.tensor_tensor(tb1, Bre, bc(cim), op=ALU.mult))
                vop(lambda: nc.vector.tensor_tensor(bbim, bbim, tb1, op=ALU.add))
                X9 = lambda t: t.unsqueeze(1).to_broadcast([128, 9, 32, 16])
                PW9 = lambda t: t.unsqueeze(3).to_broadcast([128, 9, 32, 16])
                vop(lambda: nc.vector.tensor_tensor(big1, X9(Cre), PW9(PWre[:]), op=ALU.mult))
                vop(lambda: nc.vector.tensor_tensor(big2, X9(Cim), PW9(PWim[:]), op=ALU.mult))
                vop(lambda: nc.vector.tensor_tensor(CL[0][:], big1, big2, op=ALU.subtract))
                vop(lambda: nc.vector.tensor_tensor(big1, X9(Cre), PW9(PWim[:]), op=ALU.mult))
                vop(lambda: nc.vector.tensor_tensor(big2, X9(Cim), PW9(PWre[:]), op=ALU.mult))
                vop(lambda: nc.vector.tensor_tensor(big1, big1, big2, op=ALU.add))
                vop(lambda: nc.vector.tensor_scalar(CL[1][:], big1, -1.0, None, op0=ALU.mult))
                X8 = lambda t: t.unsqueeze(1).to_broadcast([128, 8, 32, 16])
                PW8 = lambda t: t[:, 0:8, :].unsqueeze(3).to_broadcast([128, 8, 32, 16])
                vop(lambda: nc.vector.tensor_tensor(big1[:, 0:8], X8(bbre), PW8(PWre), op=ALU.mult))
                vop(lambda: nc.vector.tensor_tensor(big2[:, 0:8], X8(bbim), PW8(PWim), op=ALU.mult))
                vop(lambda: nc.vector.tensor_tensor(BstA[0][:], big1[:, 0:8], big2[:, 0:8], op=ALU.subtract))
                vop(lambda: nc.vector.tensor_tensor(big1[:, 0:8], X8(bbim), PW8(PWre), op=ALU.mult))
                vop(lambda: nc.vector.tensor_tensor(big2[:, 0:8], X8(bbre), PW8(PWim), op=ALU.mult))
                vop(lambda: nc.vector.tensor_tensor(BstA[1][:], big1[:, 0:8], big2[:, 0:8], op=ALU.add))
                fw.barrier()
                scr_off[0] = big_off
                XEc = [scrT([128, 8, 4, 2, 16], BF16) for r in range(2)]
                CLXc = [scrT([128, 9, 4, 2, 16], BF16) for r in range(2)]
                CLX3 = [scrT([128, 9, 64], BF16) for r in range(2)]
                LBT = [scrT([128, 8, 128], BF16) for r in range(2)]
                LBT3 = [scrT([128, 8, 128], BF16) for r in range(2)]
                BD = scrT([128, 8, 128], BF16)
                cosC2 = [scrT([128, 256]) for _ in range(2)]; sinC2 = [scrT([128, 256]) for _ in range(2)]; angC = scrT([128, 256])
                tF = scrT([128, 256]); tF2 = scrT([128, 256]); tIl = scrT([128, 256], I32)
                tF3 = sb(ph, "tF3", [128, 256]); tIl2 = sb(ph, "tIl2", [128, 256], I32)
                R_tab2 = [fw.R("tab2a"), fw.R("tab2b")]; R_tmpL = fw.R("tmpL"); R_tmpL2 = fw.R("tmpL2"); nonlocal_RW = [None]
                wa = [scrT([128, 256]) for i in range(4)]
                rr = scrT([128, 256]); rim = scrT([128, 256])
                sbf = [[scrT([128, 128], BF16) for b_ in range(2)] for r in range(2)]
                ysb = [scrT([128, 512]) for i in range(2)]
                R_xe = fw.R("xec"); R_clx = fw.R("clxc"); R_lbt = fw.R("lbt"); R_bd = fw.R("bd")
                R_wa = [fw.R(f"wa{i}") for i in range(4)]; R_rr = fw.R("rr"); R_ri = fw.R("ri")
                R_sbf = [[fw.R(f"sbf{r}{b_}") for b_ in range(2)] for r in range(2)]; R_y = [fw.R(f"ysb{i}") for i in range(2)]
                for r in range(2):
                    fw.op(V, lambda r=r: nc.vector.memset(CLX3[r][:], 0.0), writes=[R_clx])
                y2, y2R = fw.y2
                pairn = 0

                def prep_xe(cc_):
                    for r in range(2):
                        for g2 in range(2):
                            fw.op(V, lambda r=r, g2=g2: nc.vector.tensor_scalar(
                                XEc[r][:, :, :, g2, :], BstA[r][:, :, cc_ * 4:(cc_ + 1) * 4, :], hm[:, g2:g2 + 1], None, op0=ALU.mult),
                                reads=[R_p], writes=[R_xe])
                for cc in range(8):
                    if cc == 0:
                        prep_xe(0)
                    for r in range(2):
                        for g2 in range(2):
                            fw.op(V, lambda r=r, g2=g2, cc=cc: nc.vector.tensor_scalar(
                                CLXc[r][:, :, :, g2, :], CL[r][:, :, cc * 4:(cc + 1) * 4, :], hm[:, g2:g2 + 1], None, op0=ALU.mult),
                                reads=[R_p], writes=[R_clx])
                        fw.op(V, lambda r=r: nc.vector.tensor_copy(
                            CLX3[r][:, :, 32:64], CLXc[r][:, :, 3, :, :].rearrange("p e a b -> p e (a b)")), reads=[R_clx], writes=[R_clx])
                    for r in range(2):
                        for eh in range(2):
                            pt, pr = fw.ps(); ptb = pt[:].bitcast(BF16)
                            for e4 in range(4):
                                e = eh * 4 + e4
                                fw.op(T, lambda ptb=ptb, r=r, e=e, e4=e4: nc.tensor.transpose(
                                    ptb[:, e4 * 128:(e4 + 1) * 128], XEc[r][:, e, :, :, :].rearrange("p a b c -> p (a b c)"), identb[:]),
                                    reads=[R_xe, R_const], writes=[pr])
                            fw.op(A, lambda ptb=ptb, r=r, eh=eh: nc.scalar.copy(
                                LBT[r][:, eh * 4:(eh + 1) * 4, :], ptb[:, 0:512].rearrange("p (a b) -> p a b", a=4)), reads=[pr], writes=[R_lbt])
                            fw.op(V, lambda ptb=ptb, r=r, eh=eh: nc.vector.tensor_scalar(
                                LBT3[r][64:128, eh * 4:(eh + 1) * 4, :], ptb[64:128, 0:512].rearrange("p (a b) -> p a b", a=4), rm3[64:128, 0:1], None, op0=ALU.mult),
                                reads=[pr, R_p], writes=[R_lbt])
                    for dh in range(2):
                        pt, pr = fw.ps()
                        for d4 in range(4):
                            d_ = dh * 4 + d4
                            for r in range(2):
                                fw.op(T, lambda pt=pt, d4=d4, d_=d_, r=r: nc.tensor.matmul(
                                    pt[:, d4 * 128:(d4 + 1) * 128], XEc[r][:, 0, :, :, :].rearrange("p a b c -> p (a b c)"),
                                    CLXc[r][:, d_, :, :, :].rearrange("p a b c -> p (a b c)"), start=(r == 0), stop=(r == 1)),
                                    reads=[R_xe, R_clx], writes=[pr])
                        fw.op(V, lambda pt=pt, dh=dh: nc.vector.tensor_tensor(
                            BD[:, dh * 4:(dh + 1) * 4, :], pt[:].rearrange("p (a b) -> p a b", a=4), bmask[:].unsqueeze(1).to_broadcast([128, 4, 128]), op=ALU.mult),
                            reads=[pr, R_p], writes=[R_bd])
                    for bk in range(2):
                        fw.op(T, lambda bk=bk, cc=cc: nc.tensor.matmul(
                            y2[:, bk * 512:(bk + 1) * 512], zlhs[:], uT[:, cc, 0:512], start=True, stop=False),
                            reads=[R_p, R_u], writes=[y2R])
                    for i in range(8):
                        for j in range(i + 1):
                            fw.op(T, lambda i=i, j=j, cc=cc: nc.tensor.matmul(
                                y2[:, i * 128:(i + 1) * 128], BD[:, i - j, :], uT[:, cc, NP + j::8], start=False, stop=False),
                                reads=[R_bd, R_u], writes=[y2R])
                    if cc + 1 < 8:
                        prep_xe(cc + 1)

                    def stageA(gpl):
                        Pp = cc * 4 + gpl
                        tb = Pp % 2
                        if gpl < 3:
                            rows = slice(32 * gpl, 32 * gpl + 32); Ls = LBT
                        else:
                            rows = slice(64, 128); Ls = LBT3
                        pS, pSR = fw.ps()
                        for r in range(2):
                            for j in range(8):
                                fw.op(T, lambda pS=pS, r=r, j=j, rows=rows, Ls=Ls: nc.tensor.matmul(
                                    pS[:, r * 256:(r + 1) * 256], Ls[r][rows, 7 - j, :], uT[rows, cc, j::8], start=(j == 0), stop=(j == 7)),
                                    reads=[R_lbt, R_u], writes=[pSR])
                        nonlocal_RW[0] = [R_tmpL]
                        fw.op(V, lambda Pp=Pp: nc.vector.tensor_scalar(angC[:], iota[:], 1.0, ph8[:, Pp:Pp + 1], op0=ALU.add, op1=ALU.mult),
                              reads=[R_p, R_tmpL], writes=[R_tmpL])
                        cT_, sT_ = cosC2[tb], sinC2[tb]
                        fw.op(V, lambda: nc.vector.tensor_copy(tIl[:], angC[:]), reads=[R_tmpL], writes=[R_tmpL])
                        fw.op(V, lambda: nc.vector.tensor_copy(tF[:], tIl[:]), reads=[R_tmpL], writes=[R_tmpL])
                        fw.op(V, lambda: nc.vector.tensor_tensor(tF[:], angC[:], tF[:], op=ALU.subtract), reads=[R_tmpL], writes=[R_tmpL])
                        fw.op(A, lambda sT_=sT_: nc.scalar.activation(sT_[:], tF[:], AF.Sin, scale=TWO_PI), reads=[R_tmpL], writes=[R_tab2[tb]])
                        fw.op(V, lambda: nc.vector.tensor_scalar(tF2[:], angC[:], 0.25, None, op0=ALU.add), reads=[R_tmpL], writes=[R_tmpL2])
                        fw.op(V, lambda: nc.vector.tensor_copy(tIl2[:], tF2[:]), reads=[R_tmpL2], writes=[R_tmpL2])
                        fw.op(V, lambda: nc.vector.tensor_copy(tF3[:], tIl2[:]), reads=[R_tmpL2], writes=[R_tmpL2])
                        fw.op(V, lambda: nc.vector.tensor_tensor(tF3[:], tF2[:], tF3[:], op=ALU.subtract), reads=[R_tmpL2], writes=[R_tmpL2])
                        fw.op(A, lambda cT_=cT_: nc.scalar.activation(cT_[:], tF3[:], AF.Sin, scale=TWO_PI), reads=[R_tmpL2], writes=[R_tab2[tb]])
                        return (gpl, Pp, tb, rows, pS, pSR)

                    def stageB(ctx):
                        nonlocal pairn
                        gpl, Pp, tb, rows, pS, pSR = ctx
                        b_ = pairn % 2; pairn += 1
                        cosC, sinC, R_tabL = cosC2[tb], sinC2[tb], R_tab2[tb]
                        Sre = pS[:, 0:256]; Sim = pS[:, 256:512]
                        fw.op(V, lambda: nc.vector.tensor_tensor(wa[0][:], Sre, cosC[:], op=ALU.mult), reads=[pSR, R_tabL], writes=[R_wa[0]])
                        fw.op(V, lambda: nc.vector.tensor_tensor(wa[1][:], Sim, sinC[:], op=ALU.mult), reads=[pSR, R_tabL], writes=[R_wa[1]])
                        fw.op(V, lambda: nc.vector.tensor_tensor(wa[2][:], Sim, cosC[:], op=ALU.mult), reads=[pSR, R_tabL], writes=[R_wa[2]])
                        fw.op(V, lambda: nc.vector.tensor_tensor(wa[3][:], Sre, sinC[:], op=ALU.mult), reads=[pSR, R_tabL], writes=[R_wa[3]])
                        fw.op(P, lambda: nc.gpsimd.tensor_tensor(wa[0][:], wa[0][:], wa[1][:], op=ALU.add), reads=[R_wa[0], R_wa[1]], writes=[R_wa[0]])
                        fw.op(P, lambda: nc.gpsimd.tensor_tensor(wa[2][:], wa[2][:], wa[3][:], op=ALU.subtract), reads=[R_wa[2], R_wa[3]], writes=[R_wa[2]])
                        rho = magE[:, 8, Pp:Pp + 1].to_broadcast([128, 256])
                        fw.op(V, lambda: nc.vector.tensor_tensor_scan(rr[:], rho, wa[0][:], 0.0, ALU.mult, ALU.add),
                              reads=[R_wa[0], R_p], writes=[R_rr])
                        fw.op(V, lambda: nc.vector.tensor_tensor_scan(rim[:], rho, wa[2][:], 0.0, ALU.mult, ALU.add),
                              reads=[R_wa[2], R_p], writes=[R_ri])
                        cs = cosC[:, 127:255]; sn = sinC[:, 127:255]
                        fw.op(P, lambda: nc.gpsimd.tensor_tensor(wa[0][:, 0:128], rr[:, 127:255], cs, op=ALU.mult), reads=[R_rr, R_tabL], writes=[R_wa[0]])
                        fw.op(P, lambda: nc.gpsimd.tensor_tensor(wa[1][:, 0:128], rim[:, 127:255], sn, op=ALU.mult), reads=[R_ri, R_tabL], writes=[R_wa[1]])
                        fw.op(V, lambda: nc.vector.tensor_tensor(wa[2][:, 0:128], rim[:, 127:255], cs, op=ALU.mult), reads=[R_ri, R_tabL], writes=[R_wa[2]])
                        fw.op(V, lambda: nc.vector.tensor_tensor(wa[3][:, 0:128], rr[:, 127:255], sn, op=ALU.mult), reads=[R_rr, R_tabL], writes=[R_wa[3]])
                        fw.op(P, lambda: nc.gpsimd.tensor_tensor(sbf[0][b_][:], wa[0][:, 0:128], wa[1][:, 0:128], op=ALU.subtract),
                              reads=[R_wa[0], R_wa[1]], writes=[R_sbf[0][b_]])
                        fw.op(V, lambda: nc.vector.tensor_tensor(sbf[1][b_][:], wa[2][:, 0:128], wa[3][:, 0:128], op=ALU.add),
                              reads=[R_wa[2], R_wa[3]], writes=[R_sbf[1][b_]])
                        for i in range(8):
                            for r in range(2):
                                if gpl < 3:
                                    lhs = CLXc[r][:, i + 1, gpl, :, :].rearrange("p a b -> p (a b)")
                                else:
                                    lhs = CLX3[r][:, i + 1, :]
                                fw.op(T, lambda i=i, r=r, lhs=lhs: nc.tensor.matmul(
                                    y2[rows, i * 128:(i + 1) * 128], lhs, sbf[r][b_][:], start=False, stop=False),
                                    reads=[R_clx, R_sbf[r][b_]], writes=[y2R])

                    ctxs = [stageA(0), stageA(1)]
                    stageB(ctxs[0]); ctxs.append(stageA(2)); stageB(ctxs[1]); ctxs.append(stageA(3)); stageB(ctxs[2]); stageB(ctxs[3])
                    for bk in range(2):
                        fw.op(T, lambda bk=bk, cc=cc: nc.tensor.matmul(
                            y2[:, bk * 512:(bk + 1) * 512], zlhs[:], uT[:, cc, 0:512], start=False, stop=True),
                            reads=[R_p, R_u], writes=[y2R])
                    y2v = y2[:].rearrange("p (i c) -> p c i", i=8)
                    for g in range(2):
                        fw.op(V, lambda g=g, cc=cc: nc.vector.scalar_tensor_tensor(
                            ysb[g][:].rearrange("p (c i) -> p c i", i=8), uT[:, cc, NP + g * 512: NP + (g + 1) * 512].rearrange("p (c i) -> p c i", i=8),
                            Dq[:, cc:cc + 1], y2v[:, g * 64:(g + 1) * 64, :], op0=ALU.mult, op1=ALU.add),
                            reads=[y2R, R_u, R_p], writes=[R_y[g]])
                        fw.op(A, lambda cc=cc, g=g: nc.scalar.activation(ssmg[:, cc, g * 512:(g + 1) * 512], ysb[g][:], AF.Gelu_apprx_tanh),
                              reads=[R_y[g]], writes=[R_ssmg])
                fw.barrier()
            if dbg == 3:
                with contextlib.ExitStack() as dph:
                    dump("ssmg0", ssmg[:, 0, :], [128, NT], [R_ssmg], dph); dump("ssmg7", ssmg[:, 7, :], [128, NT], [R_ssmg], dph)
                    fw.barrier()
        for stk in reversed(open_stacks[1:]):
            stk.close()
        open_stacks = open_stacks[:1]

        if stop_after >= 5:
            phx = contextlib.ExitStack(); open_stacks.append(phx)
            xres = sb(phx, "xres", [128, 8, D])
            R_xc = [[fw.R(f"xres{t}_{c}") for c in range(4)] for t in range(8)]
            R_x = R_xc
            for tt in range(8):
                fw.dma(SP, xres[:, tt, :], xm[tt * 128:(tt + 1) * 128, :], writes=R_xc[tt])
            with contextlib.ExitStack() as ph:
                mixS = sb(ph, "mixS", [128, 8, NT], BF16); wglu = sb(ph, "wglu", [128, 8, 1024], BF16)
                wo = [sb(ph, f"wo{i}", [128, 16, 512], BF16) for i in range(2)]
                woR = [fw.R(f"wo{i}") for i in range(2)]
                sg = [sb(ph, f"sg{i}", [128, 512]) for i in range(2)]; sgR = [fw.R(f"sg{i}") for i in range(2)]
                R_ms = fw.R("mixS"); R_wg = fw.R("wglu")
                fw.dma(P, wglu[:], w_glu.rearrange("(kc p) c -> p kc c", p=128), writes=[R_wg])
                k = 0
                for co in range(8):
                    for g in range(2):
                        pt, pr = fw.ps()
                        for cc in range(8):
                            fw.op(T, lambda pt=pt, cc=cc, co=co, g=g: nc.tensor.matmul(
                                pt[:], wglu[:, cc, co * 128:(co + 1) * 128], ssmg[:, cc, g * 512:(g + 1) * 512], start=(cc == 0), stop=(cc == 7)),
                                reads=[R_wg, R_ssmg], writes=[pr])
                        s_ = k % 2; k += 1
                        fw.op(A, lambda pt=pt, co=co, s_=s_: nc.scalar.activation(sg[s_][:], pt[:], AF.Sigmoid, bias=bglu[:, co:co + 1]),
                              reads=[pr, R_const], writes=[sgR[s_]])
                        fw.op(V, lambda co=co, g=g, s_=s_: nc.vector.tensor_tensor(
                            mixS[:, co, g * 512:(g + 1) * 512], ssmg[:, co, g * 512:(g + 1) * 512], sg[s_][:], op=ALU.mult),
                            reads=[sgR[s_], R_ssmg], writes=[R_ms])
                w_out_v = w_out.rearrange("(kc p) c -> p kc c", p=128)
                for cg in range(4):
                    w, wR = load_w(wo, woR, w_out_v[:, :, cg * 512:(cg + 1) * 512])
                    fw.op(V, lambda w=w, cg=cg: nc.vector.tensor_tensor(
                        w[:], w[:], g1bc[:, cg * 512:(cg + 1) * 512].unsqueeze(1).to_broadcast([128, 16, 512]), op=ALU.mult),
                        reads=[R_mod, wR], writes=[wR])
                    for tt in range(8):
                        pt, pr = fw.ps()
                        for fc in range(16):
                            src = mixT[:, fc, tt * 128:(tt + 1) * 128] if fc < 8 else mixS[:, fc - 8, tt * 128:(tt + 1) * 128]
                            fw.op(T, lambda pt=pt, src=src, w=w, fc=fc: nc.tensor.matmul(pt[:], src, w[:, fc, :], start=(fc == 0), stop=(fc == 15)),
                                  reads=[R_mix, R_ms, wR], writes=[pr])
                        fw.op(V, lambda pt=pt, tt=tt, cg=cg: nc.vector.tensor_tensor(
                            xres[:, tt, cg * 512:(cg + 1) * 512], pt[:], xres[:, tt, cg * 512:(cg + 1) * 512], op=ALU.add),
                            reads=[pr, R_xc[tt][cg]], writes=[R_xc[tt][cg]])
                fw.barrier()
            if dbg == 5:
                with contextlib.ExitStack() as dph:
                    dump("x1_0", xres[:, 0, :], [128, D], R_x[0], dph); dump("x1_7", xres[:, 7, :], [128, D], R_x[7], dph)
                    fw.barrier()

        if stop_after >= 6:
            with contextlib.ExitStack() as ph:
                h2T = sb(ph, "h2T", [128, 16, NT], BF16); R_h2 = fw.R("h2T")
                with contextlib.ExitStack() as ph_n:
                    tiles = [(lambda t=t: (xres[:, t, :], R_x[t])) for t in range(8)]
                    rms_norm_T(ph_n, tiles, a2, sh2, h2T, R_h2, "_n2")
                    fw.barrier()
                NPART = 11; FPP = 4
                actL = [sb(ph, f"act{i}", [128, FPP, NT], BF16) for i in range(2)]; R_actL = [fw.R(f"act{i}") for i in range(2)]
                wgu = [sb(ph, f"wgu{i}", [128, 16, 128], BF16) for i in range(4)]; wguR = [fw.R(f"wgu{i}") for i in range(4)]
                wd = [sb(ph, f"wd{i}", [128, FPP, 512], BF16) for i in range(4)]; wdR = [fw.R(f"wd{i}") for i in range(4)]
                sg = [sb(ph, f"sgf{i}", [128, 512]) for i in range(2)]; sgR = [fw.R(f"sgf{i}") for i in range(2)]
                sgb = [sb(ph, f"sgb{i}", [128, 512], BF16) for i in range(2)]; sgbR = [fw.R(f"sgb{i}") for i in range(2)]
                w_gu_v = w_gu.rearrange("(kc p) c -> p kc c", p=128)
                k = 0; wdc = [0]
                for part in range(NPART):
                    act = actL[part % 2]; R_act = R_actL[part % 2]
                    for fi in range(FPP):
                        f = part * FPP + fi
                        wg_, wgR_ = load_w(wgu, wguR, w_gu_v[:, :, f * 128:(f + 1) * 128])
                        wu_, wuR_ = load_w(wgu, wguR, w_gu_v[:, :, DFF + f * 128: DFF + (f + 1) * 128])
                        for g in range(2):
                            pg, pgR = fw.ps(); pu, puR = fw.ps()
                            for (w, wR, pt, pr) in [(wg_, wgR_, pg, pgR), (wu_, wuR_, pu, puR)]:
                                for kc in range(16):
                                    fw.op(T, lambda w=w, pt=pt, kc=kc, g=g: nc.tensor.matmul(
                                        pt[:], w[:, kc, :], h2T[:, kc, g * 512:(g + 1) * 512], start=(kc == 0), stop=(kc == 15)),
                                        reads=[wR, R_h2], writes=[pr])
                            s_ = k % 2; k += 1
                            fw.op(A, lambda pg=pg, s_=s_: nc.scalar.activation(sgb[s_][:], pg[:], AF.Silu), reads=[pgR], writes=[sgbR[s_]])
                            fw.op(V, lambda pu=pu, fi=fi, g=g, s_=s_: nc.vector.tensor_tensor(
                                act[:, fi, g * 512:(g + 1) * 512], pu[:], sgb[s_][:], op=ALU.mult), reads=[puR, sgbR[s_]], writes=[R_act])
                    for cg in range(4):
                        s2 = wdc[0] % 4; wdc[0] += 1
                        fw.dma(P, wd[s2][:], w_dn[part * FPP * 128:(part + 1) * FPP * 128, cg * 512:(cg + 1) * 512].rearrange("(f p) c -> p f c", p=128),
                               writes=[wdR[s2]])
                        fw.op(V, lambda s2=s2, cg=cg: nc.vector.tensor_tensor(
                            wd[s2][:], wd[s2][:], g2bc[:, cg * 512:(cg + 1) * 512].unsqueeze(1).to_broadcast([128, FPP, 512]), op=ALU.mult),
                            reads=[R_mod, wdR[s2]], writes=[wdR[s2]])
                        for tt in range(8):
                            pt, pr = fw.ps()
                            for fi in range(FPP):
                                fw.op(T, lambda pt=pt, fi=fi, tt=tt, s2=s2: nc.tensor.matmul(
                                    pt[:], act[:, fi, tt * 128:(tt + 1) * 128], wd[s2][:, fi, :], start=(fi == 0), stop=(fi == FPP - 1)),
                                    reads=[R_act, wdR[s2]], writes=[pr])
                            fw.op(V, lambda pt=pt, tt=tt, cg=cg: nc.vector.tensor_tensor(
                                xres[:, tt, cg * 512:(cg + 1) * 512], pt[:], xres[:, tt, cg * 512:(cg + 1) * 512], op=ALU.add),
                                reads=[pr, R_xc[tt][cg]], writes=[R_xc[tt][cg]])
                fw.barrier()
            with contextlib.ExitStack() as ph:
                fnw = sb(ph, "fnw_sb", [128, D]); junk = sb(ph, "junkf", [128, D], BF16); ssq = sb(ph, "ssqf", [128, 8]); R_f = fw.R("fnw"); R_sq = fw.R("ssqf"); R_jf = fw.R("junkf")
                ob = [sb(ph, f"ob{i}", [128, D]) for i in range(2)]; obR = [fw.R(f"ob{i}") for i in range(2)]
                fw.dma(SP, fnw[:], fnw_d[0:1, :].partition_broadcast(128), writes=[R_f])
                for tt in range(8):
                    fw.op(A, lambda tt=tt: nc.scalar.activation(junk[:], xres[:, tt, :], AF.Square, accum_out=ssq[:, tt:tt + 1]), reads=R_x[tt], writes=[R_jf, R_sq])
                    fw.op(V, lambda tt=tt: nc.vector.tensor_scalar(ssq[:, tt:tt + 1], ssq[:, tt:tt + 1], 1.0 / D, EPS, op0=ALU.mult, op1=ALU.add), reads=[R_sq], writes=[R_sq])
                    fw.op(A, lambda tt=tt: nc.scalar.sqrt(ssq[:, tt:tt + 1], ssq[:, tt:tt + 1]), reads=[R_sq], writes=[R_sq])
                    fw.op(V, lambda tt=tt: nc.vector.reciprocal(ssq[:, tt:tt + 1], ssq[:, tt:tt + 1]), reads=[R_sq], writes=[R_sq])
                    s_ = tt % 2
                    fw.op(V, lambda tt=tt, s_=s_: nc.vector.scalar_tensor_tensor(ob[s_][:], xres[:, tt, :], ssq[:, tt:tt + 1], fnw[:], op0=ALU.mult, op1=ALU.mult),
                          reads=R_x[tt] + [R_sq, R_f], writes=[obR[s_]])
                    fw.dma(SP, out_d[tt * 128:(tt + 1) * 128, :], ob[s_][:], reads=[obR[s_]], is_out=True)
                fw.barrier()
        for stk in reversed(open_stacks):
            stk.close()
        for ev in fw.out_events:
            fw._wait(SP, ev)
    return nc, dbg_out


def _bf(x):
    return np.ascontiguousarray(x.astype(np.float32))


def make_in_maps(inp):
    f32 = np.float32
    x = np.asarray(inp["x"], f32); c = np.asarray(inp["c"], f32)
    g = lambda k: np.asarray(inp[k], f32)
    def pl(v, n):
        return np.ascontiguousarray(v.reshape(n, 128).T)
    hd = np.arange(4, dtype=np.float64)
    lg = np.log1p(-np.exp2(-5.0 - hd))
    idx = np.arange(128, dtype=np.float64)
    diff = idx[None, :] - idx[:, None]
    maskT = np.zeros((128, 4, 128), f32)
    for h in range(4):
        maskT[:, h, :] = np.where(diff >= 0, np.exp(lg[h] * np.maximum(diff, 0.0)), 0.0) / 16.0
    qdec = np.zeros((128, 4, 128), f32)
    for h in range(4):
        qdec[:, h, :] = np.exp(lg[h] * (idx + 1.0))[None, :]
    kdec = np.zeros((128, 4), f32)
    for h in range(4):
        kdec[:, h] = np.exp(lg[h] * (127.0 - idx)) / 16.0
    freqs = (np.float32(10000.0) ** (-np.arange(128, dtype=f32) / np.float32(128))).astype(f32)
    pos = np.arange(2048, dtype=f32)
    ang = (pos[None, :] * freqs[:, None]).astype(f32)
    cos_all = np.cos(ang).astype(f32); sin_all = np.sin(ang).astype(f32)
    ident = np.eye(128, dtype=f32)
    iota = np.tile(np.arange(256, dtype=f32)[None, :], (128, 1))
    E9 = np.tile(np.repeat(np.arange(9, dtype=f32), 32)[None, :], (128, 1))
    hm = np.zeros((128, 2), f32); hm[:64, 0] = 1; hm[64:, 1] = 1
    bmask = np.kron(np.eye(4, dtype=f32), np.ones((32, 32), f32))
    rm3 = np.zeros((128, 1), f32); rm3[96:] = 1
    def pair2(a):
        return np.ascontiguousarray(a.reshape(32, 2, 64).transpose(1, 2, 0).reshape(128, 32))
    A_re = pair2(g("s5_a_re")[0]); A_im = pair2(g("s5_a_im")[0])
    LS = pair2(np.repeat(g("s5_log_step")[0][:, None], 64, axis=1))
    def pairB(bm):
        return np.ascontiguousarray(bm.reshape(32, 2, 64, 16).transpose(1, 2, 0, 3).reshape(128, 512))
    def pairC(cm):
        return np.ascontiguousarray(cm.reshape(32, 2, 16, 64).transpose(1, 3, 0, 2).reshape(128, 512))
    common = dict(
        w_ada=g("w_ada")[0], b_ada=g("b_ada")[0][None, :], n1w=pl(g("norm1_w")[0], 16), n2w=pl(g("norm2_w")[0], 16),
        fnw=g("final_norm_w")[None, :], w_in=g("w_in")[0], rnw=pl(g("ret_norm_w")[0], 8),
        A_re=A_re, A_im=A_im, LS=LS, B_re=pairB(g("s5_b_re")[0]), B_im=pairB(g("s5_b_im")[0]),
        C_re=pairC(g("s5_c_re")[0]), C_im=pairC(g("s5_c_im")[0]), Dq=pl(g("s5_d")[0].reshape(-1), 8),
        w_glu=g("w_glu")[0], bglu=pl(g("b_glu")[0], 8), w_out=g("w_out")[0], w_gu=g("w_gate_up")[0], w_dn=g("w_down")[0],
        ident=ident, maskT=maskT.reshape(128, 512), qdec=qdec.reshape(128, 512), kdec=kdec,
        cosp=np.ascontiguousarray(cos_all[:, :1024]), sinp=np.ascontiguousarray(sin_all[:, :1024]),
        iota=iota, E9=E9, hm=hm, bmask=bmask, rm3=rm3, iota5=np.tile(np.arange(512, dtype=f32)[None, :], (128, 1)),
    )
    maps = []
    for r in range(8):
        b, half = r // 2, r % 2
        m = dict(common)
        m["xm"] = np.ascontiguousarray(x[b, half * 1024:(half + 1) * 1024])
        m["xp"] = np.ascontiguousarray(x[b, 0:1024])
        m["pmask"] = np.full((128, 1), float(half), f32)
        m["cT"] = pl(c[b], 16)
        m["cosm"] = np.ascontiguousarray(cos_all[:, half * 1024:(half + 1) * 1024])
        m["sinm"] = np.ascontiguousarray(sin_all[:, half * 1024:(half + 1) * 1024])
        maps.append(m)
    return maps


def kernel(**inputs):
    nc, _ = build()
    maps = make_in_maps(inputs)
    res = run_bass_kernel_spmd(nc, maps, core_ids=list(range(8)))
    out = np.zeros((4, 2048, 2048), np.float32)
    for r in range(8):
        b, half = r // 2, r % 2
        out[b, half * 1024:(half + 1) * 1024] = res.results[r]["out"]
    return out
```

```python
import contextlib
import numpy as np
import ml_dtypes
import concourse.bass as bass
import concourse.mybir as mybir
from concourse.bass_utils import run_bass_kernel_spmd

F32 = mybir.dt.float32
BF16 = mybir.dt.bfloat16
I32 = mybir.dt.int32
AF = mybir.ActivationFunctionType
ALU = mybir.AluOpType

D = 2048
NT = 1024
NP = 1024
DFF = 5632
EPS = 1e-6
TWO_PI = 6.283185307179586


class Res:
    __slots__ = ("name", "w", "r")

    def __init__(self, name):
        self.name = name
        self.w = None
        self.r = {}


class FW:
    NDS = 6

    def __init__(self, nc, es):
        self.nc = nc
        self.engs = {"pe": nc.tensor, "act": nc.scalar, "dve": nc.vector, "pool": nc.gpsimd, "sp": nc.sync}
        self.sem = {k: es.enter_context(nc.semaphore("s_" + k)) for k in ["pe", "act", "dve", "pool"]}
        self.cnt = {k: 0 for k in self.sem}
        self.seen = {e: {} for e in self.engs}
        self.dsem = {q: [es.enter_context(nc.semaphore(f"d_{q}{i}")) for i in range(self.NDS)] for q in ["sp", "pool"]}
        self.dcnt = {q: [0] * self.NDS for q in self.dsem}
        self.drr = {q: 0 for q in self.dsem}
        self.psum = []
        for i in range(6):
            t = es.enter_context(nc.psum_tensor(f"psum{i}", [128, 512], F32))
            self.psum.append((t, Res(f"psum{i}")))
        self.y2 = (es.enter_context(nc.psum_tensor("psum_y2", [128, 1024], F32)), Res("psum_y2"))
        self.prr = 0
        self.out_events = []

    def R(self, name):
        return Res(name)

    def ps(self):
        self.prr = (self.prr + 1) % len(self.psum)
        t, r = self.psum[self.prr]
        return t, r

    def reserve(self, k):
        out = [self.psum.pop() for _ in range(k)]
        self.prr = 0
        return out

    def release(self, banks):
        self.psum.extend(banks)

    def _wait(self, e, ev):
        key, h, val = ev
        if self.seen[e].get(key, 0) >= val:
            return
        self.engs[e].wait_ge(h, val)
        self.seen[e][key] = val

    def _deps(self, e, reads, writes):
        skip = "pe" if e == "pe" else None
        for r in reads:
            if r.w is not None and r.w[0] != skip:
                self._wait(e, r.w)
        for w in writes:
            if w.w is not None and w.w[0] != skip:
                self._wait(e, w.w)
            for ev in w.r.values():
                if ev[0] != skip:
                    self._wait(e, ev)

    def _record(self, ev, reads, writes):
        for r in reads:
            r.r[ev[0]] = ev
        for w in writes:
            w.w = ev
            w.r = {}

    def op(self, e, fn, reads=(), writes=()):
        self._deps(e, reads, writes)
        ins = fn()
        self.cnt[e] += 1
        ins.then_inc(self.sem[e], 1)
        ev = (e, self.sem[e], self.cnt[e])
        self._record(ev, reads, writes)
        return ev

    def dma(self, q, out, in_, reads=(), writes=(), is_out=False):
        i = self.drr[q]
        self.drr[q] = (i + 1) % self.NDS
        key = f"d_{q}{i}"
        h = self.dsem[q][i]
        if self.dcnt[q][i] > 0:
            self._wait(q, (key, h, self.dcnt[q][i]))
        self._deps(q, reads, writes)
        ins = self.engs[q].dma_start(out=out, in_=in_)
        self.dcnt[q][i] += 16
        ins.then_inc(h, 16)
        ev = (key, h, self.dcnt[q][i])
        self._record(ev, reads, writes)
        if is_out:
            self.out_events.append(ev)
        return ev

    def barrier(self):
        evs = [(k, self.sem[k], self.cnt[k]) for k in self.sem if self.cnt[k] > 0]
        for q in self.dsem:
            for i in range(self.NDS):
                if self.dcnt[q][i] > 0:
                    evs.append((f"d_{q}{i}", self.dsem[q][i], self.dcnt[q][i]))
        for e in self.engs:
            for ev in evs:
                if not (e == "pe" and ev[0] == "pe"):
                    self._wait(e, ev)


def build(dbg=None, stop_after=99):
    nc = bass.Bass("TRN2", target_bir_lowering=False)
    dbg_out = {}

    def din(name, shape, dt=F32):
        return nc.dram_tensor(name, list(shape), dt, kind="ExternalInput").ap()

    xm = din("xm", [NT, D]); xp = din("xp", [NP, D])
    pmask_d = din("pmask", [128, 1]); cT_d = din("cT", [128, 16])
    w_ada = din("w_ada", [D, 6 * D]); b_ada = din("b_ada", [1, 6 * D])
    n1w_d = din("n1w", [128, 16]); n2w_d = din("n2w", [128, 16]); fnw_d = din("fnw", [1, D])
    w_in = din("w_in", [D, 5120]); rnw_d = din("rnw", [128, 8])
    A_re_d = din("A_re", [128, 32]); A_im_d = din("A_im", [128, 32]); LS_d = din("LS", [128, 32])
    B_re_d = din("B_re", [128, 512]); B_im_d = din("B_im", [128, 512])
    C_re_d = din("C_re", [128, 512]); C_im_d = din("C_im", [128, 512])
    Dq_d = din("Dq", [128, 8])
    w_glu = din("w_glu", [1024, 1024]); bglu_d = din("bglu", [128, 8])
    w_out = din("w_out", [D, D]); w_gu = din("w_gu", [D, 2 * DFF]); w_dn = din("w_dn", [DFF, D])
    ident_d = din("ident", [128, 128]); maskT_d = din("maskT", [128, 512]); qdec_d = din("qdec", [128, 512])
    kdec_d = din("kdec", [128, 4]); cosm_d = din("cosm", [128, NT]); sinm_d = din("sinm", [128, NT])
    cosp_d = din("cosp", [128, NP]); sinp_d = din("sinp", [128, NP])
    iota_d = din("iota", [128, 256]); E9_d = din("E9", [128, 288]); hm_d = din("hm", [128, 2])
    bmask_d = din("bmask", [128, 128]); rm3_d = din("rm3", [128, 1])
    out_d = nc.dram_tensor("out", [NT, D], F32, kind="ExternalOutput").ap()

    def dbg_tensor(name, shape):
        dbg_out[name] = nc.dram_tensor("dbg_" + name, list(shape), F32, kind="ExternalOutput").ap()
        return dbg_out[name]

    CD = [float(np.float32(np.exp(np.float32(128.0) * np.log1p(-np.exp2(np.float32(-5.0 - h)))))) for h in range(4)]

    with contextlib.ExitStack() as es:
        fw = FW(nc, es)
        V, A, P, T, SP = "dve", "act", "pool", "pe", "sp"

        def sb(stack, name, shape, dt=F32):
            return stack.enter_context(nc.sbuf_tensor("sb_" + name, list(shape), dt))

        def dump(name, ap, shape, reads, stack):
            t = sb(stack, "dmp_" + name, shape, F32)
            r = fw.R("dmp_" + name)
            fw.op(V, lambda: nc.vector.tensor_copy(t[:], ap), reads=reads, writes=[r])
            fw.dma(SP, dbg_tensor(name, shape)[:], t[:], reads=[r], is_out=True)

        identb = sb(es, "identb", [128, 128], BF16); identf = sb(es, "identf", [128, 128])
        pmask = sb(es, "pmask_sb", [128, 1])
        a1 = sb(es, "a1", [128, 16]); sh1 = sb(es, "sh1", [128, 16]); a2 = sb(es, "a2", [128, 16]); sh2 = sb(es, "sh2", [128, 16])
        g1bc = sb(es, "g1bc", [128, D]); g2bc = sb(es, "g2bc", [128, D])
        rnw = sb(es, "rnw_sb", [128, 8]); bglu = sb(es, "bglu_sb", [128, 8])
        R_const = fw.R("const")
        R_mod = fw.R("mod")
        for t, d in [(identf, ident_d), (pmask, pmask_d), (rnw, rnw_d), (bglu, bglu_d)]:
            fw.dma(SP, t[:], d[:], writes=[R_const])
        fw.dma(P, identb[:], ident_d[:], writes=[R_const])

        open_stacks = []
        ph25 = contextlib.ExitStack(); open_stacks.append(ph25)
        mixT = sb(ph25, "mixR", [128, 8, NT], BF16)
        ssmg = sb(ph25, "ssmg", [128, 8, NT], BF16)
        R_mix = fw.R("mixT"); R_u = fw.R("uT"); R_ssmg = fw.R("ssmg")
        ph13 = contextlib.ExitStack(); open_stacks.append(ph13)
        h1T = sb(ph13, "h1T", [128, 16, NP + NT], BF16)
        R_h1 = fw.R("h1T")
        R_mod1 = fw.R("mod1")

        def rms_norm_T(ph, tiles, avec, shvec, hT, hR, tag, bg=None, Rm=None):
            xn4 = sb(ph, "xn4" + tag, [128, 4, D], BF16); junk = sb(ph, "junk" + tag, [128, D], BF16)
            ssq = sb(ph, "ssq" + tag, [128, 1]); rstd = sb(ph, "rstd" + tag, [128, 1])
            R_xn = [fw.R(f"xn{i}") for i in range(4)]; R_j = fw.R("junk"); R_s = fw.R("ssq")
            for gi in range(len(tiles) // 4):
                for t4 in range(4):
                    xap, xR = tiles[gi * 4 + t4]()
                    xRl = xR if isinstance(xR, list) else [xR]
                    fw.op(A, lambda xap=xap: nc.scalar.activation(junk[:], xap, AF.Square, accum_out=ssq[:]),
                          reads=xRl, writes=[R_j, R_s])
                    fw.op(V, lambda: nc.vector.tensor_scalar(rstd[:], ssq[:], 1.0 / D, EPS, op0=ALU.mult, op1=ALU.add),
                          reads=[R_s], writes=[R_s])
                    fw.op(A, lambda: nc.scalar.sqrt(rstd[:], rstd[:]), reads=[R_s], writes=[R_s])
                    fw.op(V, lambda: nc.vector.reciprocal(rstd[:], rstd[:]), reads=[R_s], writes=[R_s])
                    fw.op(V, lambda xap=xap, t4=t4: nc.vector.tensor_scalar(
                        xn4[:, t4, :], xap, rstd[:, 0:1], None, op0=ALU.mult), reads=xRl + [R_s], writes=[R_xn[t4]])
                for fc in range(16):
                    pt, pr = fw.ps()
                    ptb = pt[:].bitcast(BF16)
                    for t4 in range(4):
                        fw.op(T, lambda ptb=ptb, t4=t4, fc=fc: nc.tensor.transpose(
                            ptb[:, t4 * 128:(t4 + 1) * 128], xn4[:, t4, fc * 128:(fc + 1) * 128], identb[:]),
                            reads=[R_xn[t4], R_const], writes=[pr])
                    fw.op(A, lambda ptb=ptb, fc=fc, gi=gi: nc.scalar.activation(
                        hT[:, fc, gi * 512:(gi + 1) * 512], ptb[:, 0:512], AF.Identity,
                        bias=shvec[:, fc:fc + 1], scale=avec[:, fc:fc + 1]), reads=[pr, Rm if Rm is not None else R_mod], writes=[hR])
                    if bg is not None:
                        next(bg, None)


        with contextlib.ExitStack() as ph:
            cT = sb(ph, "cT_sb", [128, 16]); condb = sb(ph, "condb", [128, 16], BF16)
            crep = sb(ph, "crep", [128, 16, 128], BF16)
            n1w = sb(ph, "n1w_sb", [128, 16]); n2w = sb(ph, "n2w_sb", [128, 16])
            seg_sb = sb(ph, "seg_sb", [128, D]); bbc = sb(ph, "bbc", [128, D])
            tmpd = sb(ph, "tmpd", [128, 16, 128])
            wsl = [sb(ph, f"wada{i}", [128, D], BF16) for i in range(4)]
            wslR = [fw.R(f"wada{i}") for i in range(4)]
            R_c = fw.R("c"); R_seg = fw.R("seg"); R_bbc = fw.R("bbc"); R_tmpd = fw.R("tmpd")
            fw.dma(SP, cT[:], cT_d[:], writes=[R_c])
            fw.dma(SP, n1w[:], n1w_d[:], writes=[R_c])
            fw.dma(SP, n2w[:], n2w_d[:], writes=[R_c])
            fw.op(A, lambda: nc.scalar.activation(condb[:], cT[:], AF.Silu), reads=[R_c], writes=[R_c])
            fw.op(V, lambda: nc.vector.tensor_copy(crep[:], condb[:].unsqueeze(2).to_broadcast([128, 16, 128])),
                  reads=[R_c], writes=[R_c])
            wi = [0]

            def mod_segs(segs, banks_fn):
                for seg in segs:
                    Rm = R_mod1 if seg in (0, 1) else R_mod
                    fw.dma(SP, bbc[:], b_ada[0:1, seg * D:(seg + 1) * D].partition_broadcast(128), writes=[R_bbc])
                    pss = banks_fn()
                    for kc in range(16):
                        s = wi[0] % 4; wi[0] += 1
                        fw.dma(P, wsl[s][:], w_ada[kc * 128:(kc + 1) * 128, seg * D:(seg + 1) * D], writes=[wslR[s]])
                        for cg in range(4):
                            pt, pr = pss[cg]
                            fw.op(T, lambda pt=pt, s=s, cg=cg, kc=kc: nc.tensor.matmul(
                                pt, crep[:, kc, :], wsl[s][:, cg * 512:(cg + 1) * 512], start=(kc == 0), stop=(kc == 15)),
                                reads=[R_c, wslR[s]], writes=[pr])
                        yield
                    dst = g1bc if seg == 2 else (g2bc if seg == 5 else seg_sb)
                    Rd = R_mod if seg in (2, 5) else R_seg
                    for cg in range(4):
                        pt, pr = pss[cg]
                        fw.op(V, lambda pt=pt, cg=cg, dst=dst: nc.vector.tensor_tensor(
                            dst[:, cg * 512:(cg + 1) * 512], pt, bbc[:, cg * 512:(cg + 1) * 512], op=ALU.add),
                            reads=[pr, R_bbc], writes=[Rd])
                    if seg in (0, 1, 3, 4):
                        vec = {0: sh1, 1: a1, 3: sh2, 4: a2}[seg]
                        fw.op(V, lambda: nc.vector.tensor_tensor(
                            tmpd[:], seg_sb[:].rearrange("p (f m) -> p f m", m=128),
                            identf[:].unsqueeze(1).to_broadcast([128, 16, 128]), op=ALU.mult),
                            reads=[R_seg, R_const], writes=[R_tmpd])
                        fw.op(V, lambda vec=vec: nc.vector.tensor_reduce(
                            vec[:], tmpd[:], axis=mybir.AxisListType.X, op=ALU.add), reads=[R_tmpd], writes=[Rm])
                        if seg in (1, 4):
                            nw = n1w if seg == 1 else n2w
                            fw.op(V, lambda vec=vec: nc.vector.tensor_scalar(vec[:], vec[:], 1.0, None, op0=ALU.add),
                                  reads=[Rm], writes=[Rm])
                            fw.op(V, lambda vec=vec, nw=nw: nc.vector.tensor_tensor(vec[:], vec[:], nw[:], op=ALU.mult),
                                  reads=[R_c, Rm], writes=[Rm])
                    yield

            def banks_rot():
                return [(t[:], r) for (t, r) in [fw.ps() for _ in range(4)]]
            for _ in mod_segs([1, 0], banks_rot):
                pass
            if dbg == 0:
                for _ in mod_segs([2, 4, 3, 5], banks_rot):
                    pass
                dump("a1", a1[:], [128, 16], [R_mod1], ph); dump("sh1", sh1[:], [128, 16], [R_mod1], ph)
                dump("a2", a2[:], [128, 16], [R_mod], ph); dump("g1", g1bc[:], [128, D], [R_mod], ph); dump("n1w", n1w[:], [128, 16], [R_c], ph)
            if stop_after >= 1:
                resv = fw.reserve(2)
                y2t, y2r = fw.y2
                fixed = [(y2t[:, 0:512], y2r), (y2t[:, 512:1024], fw.R("y2b")), (resv[0][0][:], resv[0][1]), (resv[1][0][:], resv[1][1])]
                bg = mod_segs([2, 4, 3, 5], lambda: fixed)
                with contextlib.ExitStack() as ph1:
                    xs = [sb(ph1, f"xs{i}", [128, D]) for i in range(3)]
                    xsR = [fw.R(f"xs{i}") for i in range(3)]
                    cnt = [0]

                    def mk_loader(src, t):
                        def f():
                            s = cnt[0] % 3; cnt[0] += 1
                            fw.dma(SP, xs[s][:], src[t * 128:(t + 1) * 128, :], writes=[xsR[s]])
                            return xs[s][:], xsR[s]
                        return f
                    tiles = [mk_loader(xp, t) for t in range(8)] + [mk_loader(xm, t) for t in range(8)]
                    rms_norm_T(ph1, tiles, a1, sh1, h1T, R_h1, "_n1", bg=bg, Rm=R_mod1)
                    for _ in bg:
                        pass
                    fw.release(resv)
                    if dbg == 1:
                        dump("h1T0", h1T[:, 0, :], [128, 2048], [R_h1], ph1); dump("h1T15", h1T[:, 15, :], [128, 2048], [R_h1], ph1)
                    fw.barrier()
            else:
                fw.barrier()

        wcnt = [0]

        def load_w(slots, slotR, src_ap):
            s = wcnt[0] % len(slots); wcnt[0] += 1
            fw.dma(P, slots[s][:], src_ap, writes=[slotR[s]])
            return slots[s], slotR[s]

        w_in_v = w_in.rearrange("(kc p) c -> p kc c", p=128)

        if stop_after >= 2:
            with contextlib.ExitStack() as ph:
                wv = [sb(ph, f"wv{i}", [128, 16, 256], BF16) for i in range(3)]
                wvR = [fw.R(f"wv{i}") for i in range(3)]
                maskT = sb(ph, "maskT", [128, 4, 128]); qdec = sb(ph, "qdec", [128, 4, 128]); kdec = sb(ph, "kdec", [128, 4])
                kdecm = sb(ph, "kdecm", [128, 4])
                cosm = sb(ph, "cosm", [128, NT]); sinm = sb(ph, "sinm", [128, NT])
                cosp = sb(ph, "cosp", [128, NP]); sinp = sb(ph, "sinp", [128, NP])
                R_tab = fw.R("tab")
                fw.dma(SP, maskT[:], maskT_d[:].rearrange("p (h i) -> p h i", h=4), writes=[R_tab])
                fw.dma(SP, qdec[:], qdec_d[:].rearrange("p (h i) -> p h i", h=4), writes=[R_tab])
                fw.dma(SP, kdec[:], kdec_d[:], writes=[R_tab])
                for t, d in [(cosm, cosm_d), (sinm, sinm_d), (cosp, cosp_d), (sinp, sinp_d)]:
                    fw.dma(SP, t[:], d[:], writes=[R_tab])
                fw.op(V, lambda: nc.vector.tensor_scalar(kdecm[:], kdec[:], pmask[:, 0:1], None, op0=ALU.mult),
                      reads=[R_tab, R_const], writes=[R_tab])
                qT = sb(ph, "qT", [128, 2, NT], BF16); qdT = sb(ph, "qdT", [128, 2, NT], BF16)
                kT = sb(ph, "kT", [128, 2, NP + NT], BF16)
                ktok = sb(ph, "ktok", [128, 16, 256], BF16); vtok = sb(ph, "vtok", [128, 16, 256], BF16)
                gtok = sb(ph, "gtok", [128, 8, 256], BF16)
                state = sb(ph, "state", [128, 2, 256])
                rt = [sb(ph, f"rt{i}", [128, 512]) for i in range(4)]
                ssflat = ssmg[:].rearrange("p a b -> p (a b)")
                stateb8 = ssflat[:, 0:4096].rearrange("p (n d v) -> p n d v", n=8, d=2)
                yn4L = [ssflat[:, 4096:6144].bitcast(F32).rearrange("p (a b) -> p a b", a=4)] * 2
                sq2L = [ssflat[:, 6144:8192].bitcast(F32).rearrange("p (k b) -> p k b", k=2)] * 2
                msk4L = [sb(ph, f"msk4{i}", [128, 4, 128], BF16) for i in range(2)]
                rtok4L = [sb(ph, "rtok4", [128, 4, 256], BF16)] * 2
                stL = [sb(ph, f"st{i}", [128, 16]) for i in range(2)]
                R_q = fw.R("qT"); R_qd = fw.R("qdT"); R_k = fw.R("kT"); R_kt = fw.R("ktok"); R_v = fw.R("vtok"); R_g = fw.R("gtok")
                R_state = fw.R("state"); R_sb = fw.R("stateb"); R_rt = [fw.R(f"rt{i}") for i in range(4)]
                R_mskL = [fw.R(f"msk{i}") for i in range(2)]; R_rtokL = [fw.R("rtok4")] * 2
                _ryn = fw.R("yn4"); _rsq = fw.R("sq2"); R_ynL = [_ryn, _ryn]; R_sqL = [_rsq, _rsq]
                R_stL = [fw.R(f"st{i}") for i in range(2)]; R_sb8 = [fw.R(f"stateb8_{i}") for i in range(8)]

                def projT_rot(wt, tok0, ntok, dstT, Rdst, ctab, stab, toff):
                    w_, wR_ = wt
                    for g in range(ntok // 512):
                        p1, p1R = fw.ps(); p2, p2R = fw.ps()
                        for (w, wR, pt, pr) in [(w_[:, :, 0:128], wR_, p1, p1R), (w_[:, :, 128:256], wR_, p2, p2R)]:
                            for kc in range(16):
                                fw.op(T, lambda w=w, pt=pt, kc=kc, g=g: nc.tensor.matmul(
                                    pt[:], w[:, kc, :], h1T[:, kc, tok0 + g * 512: tok0 + (g + 1) * 512],
                                    start=(kc == 0), stop=(kc == 15)), reads=[wR, R_h1], writes=[pr])
                        cs = ctab[:, toff + g * 512: toff + (g + 1) * 512]; sn = stab[:, toff + g * 512: toff + (g + 1) * 512]
                        fw.op(V, lambda p1=p1, cs=cs: nc.vector.tensor_tensor(rt[0][:], p1[:], cs, op=ALU.mult), reads=[p1R, R_tab], writes=[R_rt[0]])
                        fw.op(V, lambda p2=p2, sn=sn: nc.vector.tensor_tensor(rt[1][:], p2[:], sn, op=ALU.mult), reads=[p2R, R_tab], writes=[R_rt[1]])
                        fw.op(V, lambda p2=p2, cs=cs: nc.vector.tensor_tensor(rt[2][:], p2[:], cs, op=ALU.mult), reads=[p2R, R_tab], writes=[R_rt[2]])
                        fw.op(V, lambda p1=p1, sn=sn: nc.vector.tensor_tensor(rt[3][:], p1[:], sn, op=ALU.mult), reads=[p1R, R_tab], writes=[R_rt[3]])
                        o0 = dstT[:, 0, tok0 - (0 if dstT is kT else NP) + g * 512: tok0 - (0 if dstT is kT else NP) + (g + 1) * 512]
                        o1 = dstT[:, 1, tok0 - (0 if dstT is kT else NP) + g * 512: tok0 - (0 if dstT is kT else NP) + (g + 1) * 512]
                        fw.op(P, lambda o0=o0: nc.gpsimd.tensor_tensor(o0, rt[0][:], rt[1][:], op=ALU.subtract),
                              reads=[R_rt[0], R_rt[1]], writes=[Rdst])
                        fw.op(P, lambda o1=o1: nc.gpsimd.tensor_tensor(o1, rt[2][:], rt[3][:], op=ALU.add),
                              reads=[R_rt[2], R_rt[3]], writes=[Rdst])

                deferred = []
                for h in range(4):
                    wq_ = load_w(wv, wvR, w_in_v[:, :, h * 256:(h + 1) * 256])
                    wk_ = load_w(wv, wvR, w_in_v[:, :, 1024 + h * 256: 1024 + (h + 1) * 256])
                    projT_rot(wq_, NP, NT, qT, R_q, cosm, sinm, 0)
                    projT_rot(wk_, 0, NP, kT, R_k, cosp, sinp, 0)
                    projT_rot(wk_, NP, NT, kT, R_k, cosm, sinm, 0)
                    while deferred:
                        deferred.pop(0)()
                    for dc in range(2):
                        fw.op(P, lambda dc=dc, h=h: nc.gpsimd.tensor_tensor(
                            qdT[:, dc, :].rearrange("p (t i) -> p t i", i=128), qT[:, dc, :].rearrange("p (t i) -> p t i", i=128),
                            qdec[:, h, :].unsqueeze(1).to_broadcast([128, 8, 128]), op=ALU.mult),
                            reads=[R_q, R_tab], writes=[R_qd])
                    wvt, wvtR = load_w(wv, wvR, w_in_v[:, :, 2048 + h * 256: 2048 + (h + 1) * 256])
                    for tt in range(0, 16, 2):
                        pt, pr = fw.ps()
                        for u2 in range(2):
                            for kc in range(16):
                                fw.op(T, lambda pt=pt, u2=u2, kc=kc, tt=tt: nc.tensor.matmul(
                                    pt[:, u2 * 256:(u2 + 1) * 256], h1T[:, kc, (tt + u2) * 128:(tt + u2 + 1) * 128], wvt[:, kc, :],
                                    start=(kc == 0), stop=(kc == 15)), reads=[wvtR, R_h1], writes=[pr])
                        fw.op(A, lambda pt=pt, tt=tt: nc.scalar.copy(vtok[:, tt:tt + 2, :], pt[:].rearrange("p (a b) -> p a b", a=2)),
                              reads=[pr], writes=[R_v])
                    wgt, wgtR = load_w(wv, wvR, w_in_v[:, :, 3072 + h * 256: 3072 + (h + 1) * 256])
                    for tt in range(0, 8, 2):
                        pt, pr = fw.ps()
                        for u2 in range(2):
                            for kc in range(16):
                                fw.op(T, lambda pt=pt, u2=u2, kc=kc, tt=tt: nc.tensor.matmul(
                                    pt[:, u2 * 256:(u2 + 1) * 256], h1T[:, kc, NP + (tt + u2) * 128: NP + (tt + u2 + 1) * 128], wgt[:, kc, :],
                                    start=(kc == 0), stop=(kc == 15)), reads=[wgtR, R_h1], writes=[pr])
                        fw.op(A, lambda pt=pt, tt=tt: nc.scalar.activation(
                            gtok[:, tt:tt + 2, :], pt[:].rearrange("p (a b) -> p a b", a=2), AF.Silu), reads=[pr], writes=[R_g])
                    for n in range(0, 16, 2):
                        pt, pr = fw.ps(); ptb = pt[:].bitcast(BF16)
                        for u2 in range(2):
                            for dc in range(2):
                                fw.op(T, lambda ptb=ptb, u2=u2, dc=dc, n=n: nc.tensor.transpose(
                                    ptb[:, u2 * 256 + dc * 128: u2 * 256 + (dc + 1) * 128], kT[:, dc, (n + u2) * 128:(n + u2 + 1) * 128], identb[:]),
                                    reads=[R_k, R_const], writes=[pr])
                        kd = kdecm if n < 8 else kdec
                        fw.op(A, lambda ptb=ptb, n=n, kd=kd, h=h: nc.scalar.activation(
                            ktok[:, n:n + 2, :], ptb[:, 0:512].rearrange("p (a b) -> p a b", a=2), AF.Copy, scale=kd[:, h:h + 1]),
                            reads=[pr, R_tab], writes=[R_kt])
                    for bt in range(2):
                        i0_ = bt * 4
                        msk4 = msk4L[bt]; R_msk = R_mskL[bt]
                        ps_s, ps_sR = fw.ps()
                        for ci in range(4):
                            i = i0_ + ci; n = 8 + i
                            for dc in range(2):
                                fw.op(T, lambda dc=dc, n=n, i=i, ci=ci, ps_s=ps_s: nc.tensor.matmul(
                                    ps_s[:, ci * 128:(ci + 1) * 128], kT[:, dc, n * 128:(n + 1) * 128], qT[:, dc, i * 128:(i + 1) * 128],
                                    start=(dc == 0), stop=(dc == 1)), reads=[R_k, R_q], writes=[ps_sR])
                        fw.op(V, lambda ps_s=ps_s, h=h: nc.vector.tensor_tensor(
                            msk4[:], ps_s[:].rearrange("p (a b) -> p a b", a=4), maskT[:, h, :].unsqueeze(1).to_broadcast([128, 4, 128]), op=ALU.mult),
                            reads=[ps_sR, R_tab], writes=[R_msk])
                    fw.op(V, lambda: nc.vector.memset(state[:], 0.0), writes=[R_state])
                    for n in range(15):
                        ps_k, ps_kR = fw.ps()
                        for dc in range(2):
                            fw.op(T, lambda ps_k=ps_k, dc=dc, n=n: nc.tensor.matmul(
                                ps_k[:, dc * 256:(dc + 1) * 256], ktok[:, n, dc * 128:(dc + 1) * 128], vtok[:, n, :], start=True, stop=True),
                                reads=[R_kt, R_v], writes=[ps_kR])
                        fw.op(V, lambda ps_k=ps_k, h=h: nc.vector.scalar_tensor_tensor(
                            state[:], state[:], CD[h], ps_k[:].rearrange("p (a b) -> p a b", a=2), op0=ALU.mult, op1=ALU.add),
                            reads=[ps_kR], writes=[R_state])
                        if n >= 7:
                            fw.op(A, lambda n=n: nc.scalar.copy(stateb8[:, n - 7, :, :], state[:]), reads=[R_state], writes=[R_sb8[n - 7]])
                    for bt in range(2):
                        i0_ = bt * 4
                        msk4, yn4, rtok4, sq2, st = msk4L[bt], yn4L[bt], rtok4L[bt], sq2L[bt], stL[bt]
                        R_msk, R_yn, R_rtok, R_sq, R_st = R_mskL[bt], R_ynL[bt], R_rtokL[bt], R_sqL[bt], R_stL[bt]
                        pos_ = [fw.ps(), fw.ps()]
                        for ci in range(4):
                            i = i0_ + ci; n = 8 + i
                            ps_o, ps_oR = pos_[ci // 2]
                            reg = ps_o[:, (ci % 2) * 256:(ci % 2 + 1) * 256]
                            fw.op(T, lambda reg=reg, n=n, ci=ci: nc.tensor.matmul(reg, msk4[:, ci, :], vtok[:, n, :], start=True, stop=False),
                                  reads=[R_msk, R_v], writes=[ps_oR])
                            if i > 0 or True:
                                for dc in range(2):
                                    fw.op(T, lambda reg=reg, dc=dc, i=i: nc.tensor.matmul(
                                        reg, qdT[:, dc, i * 128:(i + 1) * 128], stateb8[:, i, dc, :], start=False, stop=(dc == 1)),
                                        reads=[R_qd, R_sb8[i]], writes=[ps_oR])
                        if bt == 1:
                            deferred.pop(0)()
                        for ci in range(4):
                            ps_o, ps_oR = pos_[ci // 2]
                            reg = ps_o[:, (ci % 2) * 256:(ci % 2 + 1) * 256]
                            fw.op(A, lambda reg=reg, ci=ci: nc.scalar.activation(sq2[:, 0, 0:256], reg, AF.Copy, accum_out=st[:, ci:ci + 1]),
                                  reads=[ps_oR], writes=[R_sq, R_st])
                            fw.op(A, lambda reg=reg, ci=ci: nc.scalar.activation(sq2[:, 1, 0:256], reg, AF.Square, accum_out=st[:, 4 + ci:5 + ci]),
                                  reads=[ps_oR], writes=[R_sq, R_st])
                        fw.op(V, lambda: nc.vector.tensor_scalar(st[:, 0:8], st[:, 0:8], 1.0 / 256, None, op0=ALU.mult), reads=[R_st], writes=[R_st])
                        fw.op(V, lambda: nc.vector.tensor_tensor(st[:, 8:12], st[:, 0:4], st[:, 0:4], op=ALU.mult), reads=[R_st], writes=[R_st])
                        fw.op(V, lambda: nc.vector.scalar_tensor_tensor(st[:, 12:16], st[:, 4:8], EPS, st[:, 8:12], op0=ALU.add, op1=ALU.subtract),
                              reads=[R_st], writes=[R_st])
                        fw.op(A, lambda: nc.scalar.sqrt(st[:, 12:16], st[:, 12:16]), reads=[R_st], writes=[R_st])
                        fw.op(V, lambda: nc.vector.reciprocal(st[:, 12:16], st[:, 12:16]), reads=[R_st], writes=[R_st])
                        for ci in range(4):
                            ps_o, ps_oR = pos_[ci // 2]
                            reg = ps_o[:, (ci % 2) * 256:(ci % 2 + 1) * 256]
                            fw.op(V, lambda reg=reg, ci=ci: nc.vector.tensor_scalar(
                                yn4[:, ci, :], reg, st[:, ci:ci + 1], st[:, 12 + ci:13 + ci], op0=ALU.subtract, op1=ALU.mult),
                                reads=[ps_oR, R_st], writes=[R_yn])
                        fw.op(P, lambda i0_=i0_: nc.gpsimd.tensor_tensor(rtok4[:], yn4[:], gtok[:, i0_:i0_ + 4, :], op=ALU.mult),
                              reads=[R_yn, R_g], writes=[R_rtok])
                        def _tr_evac(rtok4=rtok4, R_rtok=R_rtok, i0_=i0_, h=h):
                            ps_t, ps_tR = fw.ps(); ptb = ps_t[:].bitcast(BF16)
                            for dc in range(2):
                                for ci in range(4):
                                    fw.op(T, lambda ptb=ptb, dc=dc, ci=ci: nc.tensor.transpose(
                                        ptb[:, dc * 512 + ci * 128: dc * 512 + (ci + 1) * 128], rtok4[:, ci, dc * 128:(dc + 1) * 128], identb[:]),
                                        reads=[R_rtok, R_const], writes=[ps_tR])
                            for dc in range(2):
                                fw.op(A, lambda ptb=ptb, dc=dc, i0_=i0_, h=h: nc.scalar.activation(
                                    mixT[:, h * 2 + dc, i0_ * 128:(i0_ + 4) * 128], ptb[:, dc * 512:(dc + 1) * 512], AF.Copy,
                                    scale=rnw[:, h * 2 + dc: h * 2 + dc + 1]), reads=[ps_tR, R_const], writes=[R_mix])
                        deferred.append(_tr_evac)
                while deferred:
                    deferred.pop(0)()
                fw.barrier()
            if dbg == 2:
                with contextlib.ExitStack() as dph:
                    dump("mixT0", mixT[:, 0, :], [128, NT], [R_mix], dph); dump("mixT7", mixT[:, 7, :], [128, NT], [R_mix], dph)
                    fw.barrier()

        iota5_d = din("iota5", [128, 512])
        if stop_after >= 3:
            with contextlib.ExitStack() as ph:
                uT = sb(ph, "uT", [128, 8, NP + NT], BF16)
                wsl = [sb(ph, f"winu{i}", [128, 16, 128], BF16) for i in range(2)]
                wslR = [fw.R(f"winu{i}") for i in range(2)]
                for cc in range(8):
                    w, wR = load_w(wsl, wslR, w_in_v[:, :, 4096 + cc * 128: 4096 + (cc + 1) * 128])
                    for g in range(4):
                        pt, pr = fw.ps()
                        for kc in range(16):
                            fw.op(T, lambda pt=pt, w=w, kc=kc, g=g: nc.tensor.matmul(
                                pt[:], w[:, kc, :], h1T[:, kc, g * 512:(g + 1) * 512], start=(kc == 0), stop=(kc == 15)),
                                reads=[wR, R_h1], writes=[pr])
                        if g < 2:
                            fw.op(A, lambda pt=pt, cc=cc, g=g: nc.scalar.activation(
                                uT[:, cc, g * 512:(g + 1) * 512], pt[:], AF.Copy, scale=pmask[:, 0:1]), reads=[pr, R_const], writes=[R_u])
                        else:
                            fw.op(A, lambda pt=pt, cc=cc, g=g: nc.scalar.copy(uT[:, cc, g * 512:(g + 1) * 512], pt[:]), reads=[pr], writes=[R_u])
                fw.barrier()
                scr_off = [0]

                h1flat = h1T[:].rearrange("p a b -> p (a b)")

                def scr(shape, dt=F32):
                    nel = int(np.prod(shape[1:]))
                    esz = 4 if dt in (F32, I32) else 2
                    nby = -(-(nel * esz) // 64) * 64
                    o = scr_off[0]; scr_off[0] += nby
                    assert scr_off[0] <= 65536, "out of h1T scratch"
                    flat = h1flat[:, o // 2:(o + nel * esz) // 2]
                    if dt != BF16:
                        flat = flat.bitcast(dt)
                    v = flat
                    if len(shape) == 3:
                        v = v.rearrange("p (a b) -> p a b", a=shape[1])
                    elif len(shape) == 4:
                        v = v.rearrange("p (a b c) -> p a b c", a=shape[1], b=shape[2])
                    elif len(shape) == 5:
                        v = v.rearrange("p (a b c d) -> p a b c d", a=shape[1], b=shape[2], c=shape[3])
                    return v

                class _T:
                    def __init__(self, v):
                        self.v = v

                    def __getitem__(self, k):
                        return self.v[k] if not (isinstance(k, slice) and k == slice(None)) else self.v

                def scrT(shape, dt=F32):
                    return _T(scr(shape, dt))
                def t32(name):
                    return sb(ph, name, [128, 32])[:]
                Are = t32("Are"); Aim = t32("Aim"); dtt = t32("dtt"); ar = t32("ar"); ph2 = t32("ph2"); ph8 = t32("ph8")
                cre = t32("cre"); cim = t32("cim"); tA = t32("tA"); tB = t32("tB"); tC = t32("tC")
                hm = sb(ph, "hm_sb", [128, 2]); bmask = sb(ph, "bmask_sb", [128, 128]); Dq = sb(ph, "Dq_sb", [128, 8])
                rm3 = sb(ph, "rm3_sb", [128, 1]); iota = sb(ph, "iota_sb", [128, 256])
                E9 = sb(ph, "E9_sb", [128, 9, 32]); magE = sb(ph, "magE", [128, 9, 32]); PWre = sb(ph, "PWre", [128, 9, 32]); PWim = sb(ph, "PWim", [128, 9, 32])
                eF = sb(ph, "eF", [128, 9, 32]); eF2 = sb(ph, "eF2", [128, 9, 32]); eA = sb(ph, "eA", [128, 9, 32]); eI = sb(ph, "eI", [128, 9, 32], I32)
                small7 = scr([128, 7, 32, 16])
                Bre, Bim, Cre, Cim, bbre, bbim, tb1 = [small7[:, i_] for i_ in range(7)]
                big_off = scr_off[0]
                big1 = scr([128, 9, 32, 16]); big2 = scr([128, 9, 32, 16])
                CL = [sb(ph, "CLre", [128, 9, 32, 16], BF16), sb(ph, "CLim", [128, 9, 32, 16], BF16)]
                BstA = [sb(ph, "BstAre", [128, 8, 32, 16], BF16), sb(ph, "BstAim", [128, 8, 32, 16], BF16)]
                R_p = fw.R("s5p")
                for t, d in [(Are, A_re_d), (Aim, A_im_d), (dtt, LS_d), (hm[:], hm_d), (bmask[:], bmask_d), (Dq[:], Dq_d), (iota[:], iota_d), (rm3[:], rm3_d)]:
                    fw.dma(SP, t, d[:], writes=[R_p])
                fw.dma(SP, E9[:], E9_d[:].rearrange("p (a b) -> p a b", b=32), writes=[R_p])
                for t, d in [(Bre, B_re_d), (Bim, B_im_d), (Cre, C_re_d), (Cim, C_im_d)]:
                    fw.dma(SP, t, d[:].rearrange("p (a b) -> p a b", b=16), writes=[R_p])
                RW = [R_p]

                def vop(f):
                    fw.op(V, f, reads=RW, writes=RW)

                def aop(f):
                    fw.op(A, f, reads=RW, writes=RW)

                def sincos(ang_t, sin_out, cos_out, tmpF, tmpF2, tmpI):
                    vop(lambda: nc.vector.tensor_copy(tmpI, ang_t))
                    vop(lambda: nc.vector.tensor_copy(tmpF, tmpI))
                    vop(lambda: nc.vector.tensor_tensor(tmpF, ang_t, tmpF, op=ALU.subtract))
                    aop(lambda: nc.scalar.activation(sin_out, tmpF, AF.Sin, scale=TWO_PI))
                    vop(lambda: nc.vector.tensor_scalar(tmpF2, ang_t, 0.25, None, op0=ALU.add))
                    vop(lambda: nc.vector.tensor_copy(tmpI, tmpF2))
                    vop(lambda: nc.vector.tensor_copy(tmpF, tmpI))
                    vop(lambda: nc.vector.tensor_tensor(tmpF, tmpF2, tmpF, op=ALU.subtract))
                    aop(lambda: nc.scalar.activation(cos_out, tmpF, AF.Sin, scale=TWO_PI))

                zlhs = sb(ph, "zlhs", [128, 128], BF16)
                vop(lambda: nc.vector.memset(zlhs[:], 0.0))
                aop(lambda: nc.scalar.activation(dtt, dtt, AF.Exp))
                vop(lambda: nc.vector.tensor_tensor(ar, Are, dtt, op=ALU.mult))
                vop(lambda: nc.vector.tensor_tensor(ph2, Aim, dtt, op=ALU.mult))
                vop(lambda: nc.vector.tensor_scalar(ph2, ph2, 1.0 / TWO_PI, None, op0=ALU.mult))
                vop(lambda: nc.vector.tensor_scalar(ph8, ph2, 8.0, None, op0=ALU.mult))
                b9 = lambda t: t.unsqueeze(1).to_broadcast([128, 9, 32])
                vop(lambda: nc.vector.tensor_tensor(eA[:], E9[:], b9(ar), op=ALU.mult))
                aop(lambda: nc.scalar.activation(magE[:], eA[:], AF.Exp))
                vop(lambda: nc.vector.tensor_tensor(eA[:], E9[:], b9(ph2), op=ALU.mult))
                sincos(eA[:], PWim[:], PWre[:], eF[:], eF2[:], eI[:])
                vop(lambda: nc.vector.tensor_tensor(PWre[:], PWre[:], magE[:], op=ALU.mult))
                vop(lambda: nc.vector.tensor_tensor(PWim[:], PWim[:], magE[:], op=ALU.mult))
                lre = PWre[:, 1, :]; lim = PWim[:, 1, :]
                vop(lambda: nc.vector.tensor_scalar(tA, lre, -1.0, None, op0=ALU.add))
                vop(lambda: nc.vector.tensor_tensor(tB, Are, Are, op=ALU.mult))
                vop(lambda: nc.vector.tensor_tensor(tC, Aim, Aim, op=ALU.mult))
                vop(lambda: nc.vector.tensor_tensor(tB, tB, tC, op=ALU.add))
                vop(lambda: nc.vector.reciprocal(tB, tB))
                vop(lambda: nc.vector.tensor_tensor(cre, tA, Are, op=ALU.mult))
                vop(lambda: nc.vector.tensor_tensor(tC, lim, Aim, op=ALU.mult))
                vop(lambda: nc.vector.tensor_tensor(cre, cre, tC, op=ALU.add))
                vop(lambda: nc.vector.tensor_tensor(cre, cre, tB, op=ALU.mult))
                vop(lambda: nc.vector.tensor_tensor(cim, lim, Are, op=ALU.mult))
                vop(lambda: nc.vector.tensor_tensor(tC, tA, Aim, op=ALU.mult))
                vop(lambda: nc.vector.tensor_tensor(cim, cim, tC, op=ALU.subtract))
                vop(lambda: nc.vector.tensor_tensor(cim, cim, tB, op=ALU.mult))
                bc = lambda t: t.unsqueeze(2).to_broadcast([128, 32, 16])
                vop(lambda: nc.vector.tensor_tensor(bbre, Bre, bc(cre), op=ALU.mult))
                vop(lambda: nc.vector.tensor_tensor(tb1, Bim, bc(cim), op=ALU.mult))
                vop(lambda: nc.vector.tensor_tensor(bbre, bbre, tb1, op=ALU.subtract))
                vop(lambda: nc.vector.tensor_tensor(bbim, Bim, bc(cre), op=ALU.mult))
                vop(lambda: nc.vector.tensor_tensor(tb1, Bre, bc(cim), op=ALU.mult))
                vop(lambda: nc.vector.tensor_tensor(bbim, bbim, tb1, op=ALU.add))
                X9 = lambda t: t.unsqueeze(1).to_broadcast([128, 9, 32, 16])
                PW9 = lambda t: t.unsqueeze(3).to_broadcast([128, 9, 32, 16])
                vop(lambda: nc.vector.tensor_tensor(big1, X9(Cre), PW9(PWre[:]), op=ALU.mult))
                vop(lambda: nc.vector.tensor_tensor(big2, X9(Cim), PW9(PWim[:]), op=ALU.mult))
                vop(lambda: nc.vector.tensor_tensor(CL[0][:], big1, big2, op=ALU.subtract))
                vop(lambda: nc.vector.tensor_tensor(big1, X9(Cre), PW9(PWim[:]), op=ALU.mult))
                vop(lambda: nc.vector.tensor_tensor(big2, X9(Cim), PW9(PWre[:]), op=ALU.mult))
                vop(lambda: nc.vector.tensor_tensor(big1, big1, big2, op=ALU.add))
                vop(lambda: nc.vector.tensor_scalar(CL[1][:], big1, -1.0, None, op0=ALU.mult))
                X8 = lambda t: t.unsqueeze(1).to_broadcast([128, 8, 32, 16])
                PW8 = lambda t: t[:, 0:8, :].unsqueeze(3).to_broadcast([128, 8, 32, 16])
                vop(lambda: nc.vector.tensor_tensor(big1[:, 0:8], X8(bbre), PW8(PWre), op=ALU.mult))
                vop(lambda: nc.vector.tensor_tensor(big2[:, 0:8], X8(bbim), PW8(PWim), op=ALU.mult))
                vop(lambda: nc.vector.tensor_tensor(BstA[0][:], big1[:, 0:8], big2[:, 0:8], op=ALU.subtract))
                vop(lambda: nc.vector.tensor_tensor(big1[:, 0:8], X8(bbim), PW8(PWre), op=ALU.mult))
                vop(lambda: nc.vector.tensor_tensor(big2[:, 0:8], X8(bbre), PW8(PWim), op=ALU.mult))
                vop(lambda: nc.vector.tensor_tensor(BstA[1][:], big1[:, 0:8], big2[:, 0:8], op=ALU.add))
                fw.barrier()
                scr_off[0] = big_off
                XEc = [scrT([128, 8, 4, 2, 16], BF16) for r in range(2)]
                CLXc = [scrT([128, 9, 4, 2, 16], BF16) for r in range(2)]
                CLX3 = [scrT([128, 9, 64], BF16) for r in range(2)]
                LBT = [scrT([128, 8, 128], BF16) for r in range(2)]
                LBT3 = [scrT([128, 8, 128], BF16) for r in range(2)]
                BD = scrT([128, 8, 128], BF16)
                cosC2 = [scrT([128, 256]) for _ in range(2)]; sinC2 = [scrT([128, 256]) for _ in range(2)]; angC = scrT([128, 256])
                tF = scrT([128, 256]); tF2 = scrT([128, 256]); tIl = scrT([128, 256], I32)
                tF3 = sb(ph, "tF3", [128, 256]); tIl2 = sb(ph, "tIl2", [128, 256], I32)
                R_tab2 = [fw.R("tab2a"), fw.R("tab2b")]; R_tmpL = fw.R("tmpL"); R_tmpL2 = fw.R("tmpL2"); nonlocal_RW = [None]
                wa = [scrT([128, 256]) for i in range(4)]
                rr = scrT([128, 256]); rim = scrT([128, 256])
                sbf = [[scrT([128, 128], BF16) for b_ in range(2)] for r in range(2)]
                ysb = [scrT([128, 512]) for i in range(2)]
                R_xe = fw.R("xec"); R_clx = fw.R("clxc"); R_lbt = fw.R("lbt"); R_bd = fw.R("bd")
                R_wa = [fw.R(f"wa{i}") for i in range(4)]; R_rr = fw.R("rr"); R_ri = fw.R("ri")
                R_sbf = [[fw.R(f"sbf{r}{b_}") for b_ in range(2)] for r in range(2)]; R_y = [fw.R(f"ysb{i}") for i in range(2)]
                for r in range(2):
                    fw.op(V, lambda r=r: nc.vector.memset(CLX3[r][:], 0.0), writes=[R_clx])
                y2, y2R = fw.y2
                pairn = 0

                def prep_xe(cc_):
                    for r in range(2):
                        for g2 in range(2):
                            fw.op(V, lambda r=r, g2=g2: nc.vector.tensor_scalar(
                                XEc[r][:, :, :, g2, :], BstA[r][:, :, cc_ * 4:(cc_ + 1) * 4, :], hm[:, g2:g2 + 1], None, op0=ALU.mult),
                                reads=[R_p], writes=[R_xe])
                for cc in range(8):
                    if cc == 0:
                        prep_xe(0)
                    for r in range(2):
                        for g2 in range(2):
                            fw.op(V, lambda r=r, g2=g2, cc=cc: nc.vector.tensor_scalar(
                                CLXc[r][:, :, :, g2, :], CL[r][:, :, cc * 4:(cc + 1) * 4, :], hm[:, g2:g2 + 1], None, op0=ALU.mult),
                                reads=[R_p], writes=[R_clx])
                        fw.op(V, lambda r=r: nc.vector.tensor_copy(
                            CLX3[r][:, :, 32:64], CLXc[r][:, :, 3, :, :].rearrange("p e a b -> p e (a b)")), reads=[R_clx], writes=[R_clx])
                    for r in range(2):
                        for eh in range(2):
                            pt, pr = fw.ps(); ptb = pt[:].bitcast(BF16)
                            for e4 in range(4):
                                e = eh * 4 + e4
                                fw.op(T, lambda ptb=ptb, r=r, e=e, e4=e4: nc.tensor.transpose(
                                    ptb[:, e4 * 128:(e4 + 1) * 128], XEc[r][:, e, :, :, :].rearrange("p a b c -> p (a b c)"), identb[:]),
                                    reads=[R_xe, R_const], writes=[pr])
                            fw.op(A, lambda ptb=ptb, r=r, eh=eh: nc.scalar.copy(
                                LBT[r][:, eh * 4:(eh + 1) * 4, :], ptb[:, 0:512].rearrange("p (a b) -> p a b", a=4)), reads=[pr], writes=[R_lbt])
                            fw.op(V, lambda ptb=ptb, r=r, eh=eh: nc.vector.tensor_scalar(
                                LBT3[r][64:128, eh * 4:(eh + 1) * 4, :], ptb[64:128, 0:512].rearrange("p (a b) -> p a b", a=4), rm3[64:128, 0:1], None, op0=ALU.mult),
                                reads=[pr, R_p], writes=[R_lbt])
                    for dh in range(2):
                        pt, pr = fw.ps()
                        for d4 in range(4):
                            d_ = dh * 4 + d4
                            for r in range(2):
                                fw.op(T, lambda pt=pt, d4=d4, d_=d_, r=r: nc.tensor.matmul(
                                    pt[:, d4 * 128:(d4 + 1) * 128], XEc[r][:, 0, :, :, :].rearrange("p a b c -> p (a b c)"),
                                    CLXc[r][:, d_, :, :, :].rearrange("p a b c -> p (a b c)"), start=(r == 0), stop=(r == 1)),
                                    reads=[R_xe, R_clx], writes=[pr])
                        fw.op(V, lambda pt=pt, dh=dh: nc.vector.tensor_tensor(
                            BD[:, dh * 4:(dh + 1) * 4, :], pt[:].rearrange("p (a b) -> p a b", a=4), bmask[:].unsqueeze(1).to_broadcast([128, 4, 128]), op=ALU.mult),
                            reads=[pr, R_p], writes=[R_bd])
                    for bk in range(2):
                        fw.op(T, lambda bk=bk, cc=cc: nc.tensor.matmul(
                            y2[:, bk * 512:(bk + 1) * 512], zlhs[:], uT[:, cc, 0:512], start=True, stop=False),
                            reads=[R_p, R_u], writes=[y2R])
                    for i in range(8):
                        for j in range(i + 1):
                            fw.op(T, lambda i=i, j=j, cc=cc: nc.tensor.matmul(
                                y2[:, i * 128:(i + 1) * 128], BD[:, i - j, :], uT[:, cc, NP + j::8], start=False, stop=False),
                                reads=[R_bd, R_u], writes=[y2R])
                    if cc + 1 < 8:
                        prep_xe(cc + 1)

                    def stageA(gpl):
                        Pp = cc * 4 + gpl
                        tb = Pp % 2
                        if gpl < 3:
                            rows = slice(32 * gpl, 32 * gpl + 32); Ls = LBT
                        else:
                            rows = slice(64, 128); Ls = LBT3
                        pS, pSR = fw.ps()
                        for r in range(2):
                            for j in range(8):
                                fw.op(T, lambda pS=pS, r=r, j=j, rows=rows, Ls=Ls: nc.tensor.matmul(
                                    pS[:, r * 256:(r + 1) * 256], Ls[r][rows, 7 - j, :], uT[rows, cc, j::8], start=(j == 0), stop=(j == 7)),
                                    reads=[R_lbt, R_u], writes=[pSR])
                        nonlocal_RW[0] = [R_tmpL]
                        fw.op(V, lambda Pp=Pp: nc.vector.tensor_scalar(angC[:], iota[:], 1.0, ph8[:, Pp:Pp + 1], op0=ALU.add, op1=ALU.mult),
                              reads=[R_p, R_tmpL], writes=[R_tmpL])
                        cT_, sT_ = cosC2[tb], sinC2[tb]
                        fw.op(V, lambda: nc.vector.tensor_copy(tIl[:], angC[:]), reads=[R_tmpL], writes=[R_tmpL])
                        fw.op(V, lambda: nc.vector.tensor_copy(tF[:], tIl[:]), reads=[R_tmpL], writes=[R_tmpL])
                        fw.op(V, lambda: nc.vector.tensor_tensor(tF[:], angC[:], tF[:], op=ALU.subtract), reads=[R_tmpL], writes=[R_tmpL])
                        fw.op(A, lambda sT_=sT_: nc.scalar.activation(sT_[:], tF[:], AF.Sin, scale=TWO_PI), reads=[R_tmpL], writes=[R_tab2[tb]])
                        fw.op(V, lambda: nc.vector.tensor_scalar(tF2[:], angC[:], 0.25, None, op0=ALU.add), reads=[R_tmpL], writes=[R_tmpL2])
                        fw.op(V, lambda: nc.vector.tensor_copy(tIl2[:], tF2[:]), reads=[R_tmpL2], writes=[R_tmpL2])
                        fw.op(V, lambda: nc.vector.tensor_copy(tF3[:], tIl2[:]), reads=[R_tmpL2], writes=[R_tmpL2])
                        fw.op(V, lambda: nc.vector.tensor_tensor(tF3[:], tF2[:], tF3[:], op=ALU.subtract), reads=[R_tmpL2], writes=[R_tmpL2])
                        fw.op(A, lambda cT_=cT_: nc.scalar.activation(cT_[:], tF3[:], AF.Sin, scale=TWO_PI), reads=[R_tmpL2], writes=[R_tab2[tb]])
                        return (gpl, Pp, tb, rows, pS, pSR)

                    def stageB(ctx):
                        nonlocal pairn
                        gpl, Pp, tb, rows, pS, pSR = ctx
                        b_ = pairn % 2; pairn += 1
                        cosC, sinC, R_tabL = cosC2[tb], sinC2[tb], R_tab2[tb]
                        Sre = pS[:, 0:256]; Sim = pS[:, 256:512]
                        fw.op(V, lambda: nc.vector.tensor_tensor(wa[0][:], Sre, cosC[:], op=ALU.mult), reads=[pSR, R_tabL], writes=[R_wa[0]])
                        fw.op(V, lambda: nc.vector.tensor_tensor(wa[1][:], Sim, sinC[:], op=ALU.mult), reads=[pSR, R_tabL], writes=[R_wa[1]])
                        fw.op(V, lambda: nc.vector.tensor_tensor(wa[2][:], Sim, cosC[:], op=ALU.mult), reads=[pSR, R_tabL], writes=[R_wa[2]])
                        fw.op(V, lambda: nc.vector.tensor_tensor(wa[3][:], Sre, sinC[:], op=ALU.mult), reads=[pSR, R_tabL], writes=[R_wa[3]])
                        fw.op(P, lambda: nc.gpsimd.tensor_tensor(wa[0][:], wa[0][:], wa[1][:], op=ALU.add), reads=[R_wa[0], R_wa[1]], writes=[R_wa[0]])
                        fw.op(P, lambda: nc.gpsimd.tensor_tensor(wa[2][:], wa[2][:], wa[3][:], op=ALU.subtract), reads=[R_wa[2], R_wa[3]], writes=[R_wa[2]])
                        rho = magE[:, 8, Pp:Pp + 1].to_broadcast([128, 256])
                        fw.op(V, lambda: nc.vector.tensor_tensor_scan(rr[:], rho, wa[0][:], 0.0, ALU.mult, ALU.add),
                              reads=[R_wa[0], R_p], writes=[R_rr])
                        fw.op(V, lambda: nc.vector.tensor_tensor_scan(rim[:], rho, wa[2][:], 0.0, ALU.mult, ALU.add),
                              reads=[R_wa[2], R_p], writes=[R_ri])
                        cs = cosC[:, 127:255]; sn = sinC[:, 127:255]
                        fw.op(P, lambda: nc.gpsimd.tensor_tensor(wa[0][:, 0:128], rr[:, 127:255], cs, op=ALU.mult), reads=[R_rr, R_tabL], writes=[R_wa[0]])
                        fw.op(P, lambda: nc.gpsimd.tensor_tensor(wa[1][:, 0:128], rim[:, 127:255], sn, op=ALU.mult), reads=[R_ri, R_tabL], writes=[R_wa[1]])
                        fw.op(V, lambda: nc.vector.tensor_tensor(wa[2][:, 0:128], rim[:, 127:255], cs, op=ALU.mult), reads=[R_ri, R_tabL], writes=[R_wa[2]])
                        fw.op(V, lambda: nc.vector.tensor_tensor(wa[3][:, 0:128], rr[:, 127:255], sn, op=ALU.mult), reads=[R_rr, R_tabL], writes=[R_wa[3]])
                        fw.op(P, lambda: nc.gpsimd.tensor_tensor(sbf[0][b_][:], wa[0][:, 0:128], wa[1][:, 0:128], op=ALU.subtract),
                              reads=[R_wa[0], R_wa[1]], writes=[R_sbf[0][b_]])
                        fw.op(V, lambda: nc.vector.tensor_tensor(sbf[1][b_][:], wa[2][:, 0:128], wa[3][:, 0:128], op=ALU.add),
                              reads=[R_wa[2], R_wa[3]], writes=[R_sbf[1][b_]])
                        for i in range(8):
                            for r in range(2):
                                if gpl < 3:
                                    lhs = CLXc[r][:, i + 1, gpl, :, :].rearrange("p a b -> p (a b)")
                                else:
                                    lhs = CLX3[r][:, i + 1, :]
                                fw.op(T, lambda i=i, r=r, lhs=lhs: nc.tensor.matmul(
                                    y2[rows, i * 128:(i + 1) * 128], lhs, sbf[r][b_][:], start=False, stop=False),
                                    reads=[R_clx, R_sbf[r][b_]], writes=[y2R])

                    ctxs = [stageA(0), stageA(1)]
                    stageB(ctxs[0]); ctxs.append(stageA(2)); stageB(ctxs[1]); ctxs.append(stageA(3)); stageB(ctxs[2]); stageB(ctxs[3])
                    for bk in range(2):
                        fw.op(T, lambda bk=bk, cc=cc: nc.tensor.matmul(
                            y2[:, bk * 512:(bk + 1) * 512], zlhs[:], uT[:, cc, 0:512], start=False, stop=True),
                            reads=[R_p, R_u], writes=[y2R])
                    y2v = y2[:].rearrange("p (i c) -> p c i", i=8)
                    for g in range(2):
                        fw.op(V, lambda g=g, cc=cc: nc.vector.scalar_tensor_tensor(
                            ysb[g][:].rearrange("p (c i) -> p c i", i=8), uT[:, cc, NP + g * 512: NP + (g + 1) * 512].rearrange("p (c i) -> p c i", i=8),
                            Dq[:, cc:cc + 1], y2v[:, g * 64:(g + 1) * 64, :], op0=ALU.mult, op1=ALU.add),
                            reads=[y2R, R_u, R_p], writes=[R_y[g]])
                        fw.op(A, lambda cc=cc, g=g: nc.scalar.activation(ssmg[:, cc, g * 512:(g + 1) * 512], ysb[g][:], AF.Gelu_apprx_tanh),
                              reads=[R_y[g]], writes=[R_ssmg])
                fw.barrier()
            if dbg == 3:
                with contextlib.ExitStack() as dph:
                    dump("ssmg0", ssmg[:, 0, :], [128, NT], [R_ssmg], dph); dump("ssmg7", ssmg[:, 7, :], [128, NT], [R_ssmg], dph)
                    fw.barrier()
        for stk in reversed(open_stacks[1:]):
            stk.close()
        open_stacks = open_stacks[:1]

        if stop_after >= 5:
            phx = contextlib.ExitStack(); open_stacks.append(phx)
            xres = sb(phx, "xres", [128, 8, D])
            R_xc = [[fw.R(f"xres{t}_{c}") for c in range(4)] for t in range(8)]
            R_x = R_xc
            for tt in range(8):
                fw.dma(SP, xres[:, tt, :], xm[tt * 128:(tt + 1) * 128, :], writes=R_xc[tt])
            with contextlib.ExitStack() as ph:
                mixS = sb(ph, "mixS", [128, 8, NT], BF16); wglu = sb(ph, "wglu", [128, 8, 1024], BF16)
                wo = [sb(ph, f"wo{i}", [128, 16, 512], BF16) for i in range(2)]
                woR = [fw.R(f"wo{i}") for i in range(2)]
                sg = [sb(ph, f"sg{i}", [128, 512]) for i in range(2)]; sgR = [fw.R(f"sg{i}") for i in range(2)]
                R_ms = fw.R("mixS"); R_wg = fw.R("wglu")
                fw.dma(P, wglu[:], w_glu.rearrange("(kc p) c -> p kc c", p=128), writes=[R_wg])
                k = 0
                for co in range(8):
                    for g in range(2):
                        pt, pr = fw.ps()
                        for cc in range(8):
                            fw.op(T, lambda pt=pt, cc=cc, co=co, g=g: nc.tensor.matmul(
                                pt[:], wglu[:, cc, co * 128:(co + 1) * 128], ssmg[:, cc, g * 512:(g + 1) * 512], start=(cc == 0), stop=(cc == 7)),
                                reads=[R_wg, R_ssmg], writes=[pr])
                        s_ = k % 2; k += 1
                        fw.op(A, lambda pt=pt, co=co, s_=s_: nc.scalar.activation(sg[s_][:], pt[:], AF.Sigmoid, bias=bglu[:, co:co + 1]),
                              reads=[pr, R_const], writes=[sgR[s_]])
                        fw.op(V, lambda co=co, g=g, s_=s_: nc.vector.tensor_tensor(
                            mixS[:, co, g * 512:(g + 1) * 512], ssmg[:, co, g * 512:(g + 1) * 512], sg[s_][:], op=ALU.mult),
                            reads=[sgR[s_], R_ssmg], writes=[R_ms])
                w_out_v = w_out.rearrange("(kc p) c -> p kc c", p=128)
                for cg in range(4):
                    w, wR = load_w(wo, woR, w_out_v[:, :, cg * 512:(cg + 1) * 512])
                    fw.op(V, lambda w=w, cg=cg: nc.vector.tensor_tensor(
                        w[:], w[:], g1bc[:, cg * 512:(cg + 1) * 512].unsqueeze(1).to_broadcast([128, 16, 512]), op=ALU.mult),
                        reads=[R_mod, wR], writes=[wR])
                    for tt in range(8):
                        pt, pr = fw.ps()
                        for fc in range(16):
                            src = mixT[:, fc, tt * 128:(tt + 1) * 128] if fc < 8 else mixS[:, fc - 8, tt * 128:(tt + 1) * 128]
                            fw.op(T, lambda pt=pt, src=src, w=w, fc=fc: nc.tensor.matmul(pt[:], src, w[:, fc, :], start=(fc == 0), stop=(fc == 15)),
                                  reads=[R_mix, R_ms, wR], writes=[pr])
                        fw.op(V, lambda pt=pt, tt=tt, cg=cg: nc.vector.tensor_tensor(
                            xres[:, tt, cg * 512:(cg + 1) * 512], pt[:], xres[:, tt, cg * 512:(cg + 1) * 512], op=ALU.add),
                            reads=[pr, R_xc[tt][cg]], writes=[R_xc[tt][cg]])
                fw.barrier()
            if dbg == 5:
                with contextlib.ExitStack() as dph:
                    dump("x1_0", xres[:, 0, :], [128, D], R_x[0], dph); dump("x1_7", xres[:, 7, :], [128, D], R_x[7], dph)
                    fw.barrier()

        if stop_after >= 6:
            with contextlib.ExitStack() as ph:
                h2T = sb(ph, "h2T", [128, 16, NT], BF16); R_h2 = fw.R("h2T")
                with contextlib.ExitStack() as ph_n:
                    tiles = [(lambda t=t: (xres[:, t, :], R_x[t])) for t in range(8)]
                    rms_norm_T(ph_n, tiles, a2, sh2, h2T, R_h2, "_n2")
                    fw.barrier()
                def final_tile(tt, fnw, junk, ssq, ob, obR, R_f, R_sq, R_jf):
                    fw.op(A, lambda: nc.scalar.activation(junk[:], xres[:, tt, :], AF.Square, accum_out=ssq[:, tt:tt + 1]), reads=R_x[tt], writes=[R_jf, R_sq])
                    fw.op(V, lambda: nc.vector.tensor_scalar(ssq[:, tt:tt + 1], ssq[:, tt:tt + 1], 1.0 / D, EPS, op0=ALU.mult, op1=ALU.add), reads=[R_sq], writes=[R_sq])
                    fw.op(A, lambda: nc.scalar.sqrt(ssq[:, tt:tt + 1], ssq[:, tt:tt + 1]), reads=[R_sq], writes=[R_sq])
                    fw.op(V, lambda: nc.vector.reciprocal(ssq[:, tt:tt + 1], ssq[:, tt:tt + 1]), reads=[R_sq], writes=[R_sq])
                    s_ = tt % 2
                    fw.op(V, lambda: nc.vector.scalar_tensor_tensor(ob[s_][:], xres[:, tt, :], ssq[:, tt:tt + 1], fnw[:], op0=ALU.mult, op1=ALU.mult),
                          reads=R_x[tt] + [R_sq, R_f], writes=[obR[s_], R_h2])
                    fw.dma(SP, out_d[tt * 128:(tt + 1) * 128, :], ob[s_][:], reads=[obR[s_]], is_out=True)
                NPART = 11; FPP = 4
                actL = [sb(ph, f"act{i}", [128, FPP, NT], BF16) for i in range(2)]; R_actL = [fw.R(f"act{i}") for i in range(2)]
                wgu = [sb(ph, f"wgu{i}", [128, 16, 128], BF16) for i in range(4)]; wguR = [fw.R(f"wgu{i}") for i in range(4)]
                wd = [sb(ph, f"wd{i}", [128, FPP, 512], BF16) for i in range(4)]; wdR = [fw.R(f"wd{i}") for i in range(4)]
                sg = [sb(ph, f"sgf{i}", [128, 512]) for i in range(2)]; sgR = [fw.R(f"sgf{i}") for i in range(2)]
                sgb = [sb(ph, f"sgb{i}", [128, 512], BF16) for i in range(2)]; sgbR = [fw.R(f"sgb{i}") for i in range(2)]
                w_gu_v = w_gu.rearrange("(kc p) c -> p kc c", p=128)
                k = 0; wdc = [0]
                for part in range(NPART):
                    act = actL[part % 2]; R_act = R_actL[part % 2]
                    for fi in range(FPP):
                        f = part * FPP + fi
                        wg_, wgR_ = load_w(wgu, wguR, w_gu_v[:, :, f * 128:(f + 1) * 128])
                        wu_, wuR_ = load_w(wgu, wguR, w_gu_v[:, :, DFF + f * 128: DFF + (f + 1) * 128])
                        for g in range(2):
                            pg, pgR = fw.ps(); pu, puR = fw.ps()
                            for (w, wR, pt, pr) in [(wg_, wgR_, pg, pgR), (wu_, wuR_, pu, puR)]:
                                for kc in range(16):
                                    fw.op(T, lambda w=w, pt=pt, kc=kc, g=g: nc.tensor.matmul(
                                        pt[:], w[:, kc, :], h2T[:, kc, g * 512:(g + 1) * 512], start=(kc == 0), stop=(kc == 15)),
                                        reads=[wR, R_h2], writes=[pr])
                            s_ = k % 2; k += 1
                            fw.op(A, lambda pg=pg, s_=s_: nc.scalar.activation(sgb[s_][:], pg[:], AF.Silu), reads=[pgR], writes=[sgbR[s_]])
                            fw.op(V, lambda pu=pu, fi=fi, g=g, s_=s_: nc.vector.tensor_tensor(
                                act[:, fi, g * 512:(g + 1) * 512], pu[:], sgb[s_][:], op=ALU.mult), reads=[puR, sgbR[s_]], writes=[R_act])
                    last = (part == NPART - 1)
                    slots = []
                    for cg in range(4):
                        s2 = wdc[0] % 4; wdc[0] += 1
                        slots.append(s2)
                        fw.dma(P, wd[s2][:], w_dn[part * FPP * 128:(part + 1) * FPP * 128, cg * 512:(cg + 1) * 512].rearrange("(f p) c -> p f c", p=128),
                               writes=[wdR[s2]])
                        fw.op(V, lambda s2=s2, cg=cg: nc.vector.tensor_tensor(
                            wd[s2][:], wd[s2][:], g2bc[:, cg * 512:(cg + 1) * 512].unsqueeze(1).to_broadcast([128, FPP, 512]), op=ALU.mult),
                            reads=[R_mod, wdR[s2]], writes=[wdR[s2]])
                        if last:
                            continue
                        for tt in range(8):
                            pt, pr = fw.ps()
                            for fi in range(FPP):
                                fw.op(T, lambda pt=pt, fi=fi, tt=tt, s2=s2: nc.tensor.matmul(
                                    pt[:], act[:, fi, tt * 128:(tt + 1) * 128], wd[s2][:, fi, :], start=(fi == 0), stop=(fi == FPP - 1)),
                                    reads=[R_act, wdR[s2]], writes=[pr])
                            fw.op(V, lambda pt=pt, tt=tt, cg=cg: nc.vector.tensor_tensor(
                                xres[:, tt, cg * 512:(cg + 1) * 512], pt[:], xres[:, tt, cg * 512:(cg + 1) * 512], op=ALU.add),
                                reads=[pr, R_xc[tt][cg]], writes=[R_xc[tt][cg]])
                    if last:
                        class _V:
                            def __init__(self, v):
                                self.v = v

                            def __getitem__(self, k):
                                return self.v
                        h2v = lambda a, b_: h2T[:, a:b_, :].rearrange("p a b -> p (a b)")
                        fnw = _V(h2v(0, 4).bitcast(F32)); ob = [_V(h2v(4, 8).bitcast(F32)), _V(h2v(8, 12).bitcast(F32))]
                        junk = _V(h2v(12, 14)); ssq = sb(ph, "ssqf", [128, 8])
                        R_f = fw.R("fnw"); R_sq = fw.R("ssqf"); R_jf = R_h2
                        obR = [fw.R(f"ob{i}") for i in range(2)]
                        fw.dma(SP, fnw[:], fnw_d[0:1, :].partition_broadcast(128), writes=[R_f, R_h2])
                        for tt in range(8):
                            for cg in range(4):
                                s2 = slots[cg]
                                pt, pr = fw.ps()
                                for fi in range(FPP):
                                    fw.op(T, lambda pt=pt, fi=fi, tt=tt, s2=s2: nc.tensor.matmul(
                                        pt[:], act[:, fi, tt * 128:(tt + 1) * 128], wd[s2][:, fi, :], start=(fi == 0), stop=(fi == FPP - 1)),
                                        reads=[R_act, wdR[s2]], writes=[pr])
                                fw.op(V, lambda pt=pt, tt=tt, cg=cg: nc.vector.tensor_tensor(
                                    xres[:, tt, cg * 512:(cg + 1) * 512], pt[:], xres[:, tt, cg * 512:(cg + 1) * 512], op=ALU.add),
                                    reads=[pr, R_xc[tt][cg]], writes=[R_xc[tt][cg]])
                            final_tile(tt, fnw, junk, ssq, ob, obR, R_f, R_sq, R_jf)
                fw.barrier()
            if False:
                for tt in range(8):
                    fw.op(A, lambda tt=tt: nc.scalar.activation(junk[:], xres[:, tt, :], AF.Square, accum_out=ssq[:, tt:tt + 1]), reads=R_x[tt], writes=[R_jf, R_sq])
                    fw.op(V, lambda tt=tt: nc.vector.tensor_scalar(ssq[:, tt:tt + 1], ssq[:, tt:tt + 1], 1.0 / D, EPS, op0=ALU.mult, op1=ALU.add), reads=[R_sq], writes=[R_sq])
                    fw.op(A, lambda tt=tt: nc.scalar.sqrt(ssq[:, tt:tt + 1], ssq[:, tt:tt + 1]), reads=[R_sq], writes=[R_sq])
                    fw.op(V, lambda tt=tt: nc.vector.reciprocal(ssq[:, tt:tt + 1], ssq[:, tt:tt + 1]), reads=[R_sq], writes=[R_sq])
                    s_ = tt % 2
                    fw.op(V, lambda tt=tt, s_=s_: nc.vector.scalar_tensor_tensor(ob[s_][:], xres[:, tt, :], ssq[:, tt:tt + 1], fnw[:], op0=ALU.mult, op1=ALU.mult),
                          reads=R_x[tt] + [R_sq, R_f], writes=[obR[s_]])
                    fw.dma(SP, out_d[tt * 128:(tt + 1) * 128, :], ob[s_][:], reads=[obR[s_]], is_out=True)
                fw.barrier()
        for stk in reversed(open_stacks):
            stk.close()
        for ev in fw.out_events:
            fw._wait(SP, ev)
    return nc, dbg_out


def _bf(x):
    return np.ascontiguousarray(x.astype(np.float32))


def make_in_maps(inp):
    f32 = np.float32
    x = np.asarray(inp["x"], f32); c = np.asarray(inp["c"], f32)
    g = lambda k: np.asarray(inp[k], f32)
    def pl(v, n):
        return np.ascontiguousarray(v.reshape(n, 128).T)
    hd = np.arange(4, dtype=np.float64)
    lg = np.log1p(-np.exp2(-5.0 - hd))
    idx = np.arange(128, dtype=np.float64)
    diff = idx[None, :] - idx[:, None]
    maskT = np.zeros((128, 4, 128), f32)
    for h in range(4):
        maskT[:, h, :] = np.where(diff >= 0, np.exp(lg[h] * np.maximum(diff, 0.0)), 0.0) / 16.0
    qdec = np.zeros((128, 4, 128), f32)
    for h in range(4):
        qdec[:, h, :] = np.exp(lg[h] * (idx + 1.0))[None, :]
    kdec = np.zeros((128, 4), f32)
    for h in range(4):
        kdec[:, h] = np.exp(lg[h] * (127.0 - idx)) / 16.0
    freqs = (np.float32(10000.0) ** (-np.arange(128, dtype=f32) / np.float32(128))).astype(f32)
    pos = np.arange(2048, dtype=f32)
    ang = (pos[None, :] * freqs[:, None]).astype(f32)
    cos_all = np.cos(ang).astype(f32); sin_all = np.sin(ang).astype(f32)
    ident = np.eye(128, dtype=f32)
    iota = np.tile(np.arange(256, dtype=f32)[None, :], (128, 1))
    E9 = np.tile(np.repeat(np.arange(9, dtype=f32), 32)[None, :], (128, 1))
    hm = np.zeros((128, 2), f32); hm[:64, 0] = 1; hm[64:, 1] = 1
    bmask = np.kron(np.eye(4, dtype=f32), np.ones((32, 32), f32))
    rm3 = np.zeros((128, 1), f32); rm3[96:] = 1
    def pair2(a):
        return np.ascontiguousarray(a.reshape(32, 2, 64).transpose(1, 2, 0).reshape(128, 32))
    A_re = pair2(g("s5_a_re")[0]); A_im = pair2(g("s5_a_im")[0])
    LS = pair2(np.repeat(g("s5_log_step")[0][:, None], 64, axis=1))
    def pairB(bm):
        return np.ascontiguousarray(bm.reshape(32, 2, 64, 16).transpose(1, 2, 0, 3).reshape(128, 512))
    def pairC(cm):
        return np.ascontiguousarray(cm.reshape(32, 2, 16, 64).transpose(1, 3, 0, 2).reshape(128, 512))
    common = dict(
        w_ada=g("w_ada")[0], b_ada=g("b_ada")[0][None, :], n1w=pl(g("norm1_w")[0], 16), n2w=pl(g("norm2_w")[0], 16),
        fnw=g("final_norm_w")[None, :], w_in=g("w_in")[0], rnw=pl(g("ret_norm_w")[0], 8),
        A_re=A_re, A_im=A_im, LS=LS, B_re=pairB(g("s5_b_re")[0]), B_im=pairB(g("s5_b_im")[0]),
        C_re=pairC(g("s5_c_re")[0]), C_im=pairC(g("s5_c_im")[0]), Dq=pl(g("s5_d")[0].reshape(-1), 8),
        w_glu=g("w_glu")[0], bglu=pl(g("b_glu")[0], 8), w_out=g("w_out")[0], w_gu=g("w_gate_up")[0], w_dn=g("w_down")[0],
        ident=ident, maskT=maskT.reshape(128, 512), qdec=qdec.reshape(128, 512), kdec=kdec,
        cosp=np.ascontiguousarray(cos_all[:, :1024]), sinp=np.ascontiguousarray(sin_all[:, :1024]),
        iota=iota, E9=E9, hm=hm, bmask=bmask, rm3=rm3, iota5=np.tile(np.arange(512, dtype=f32)[None, :], (128, 1)),
    )
    maps = []
    for r in range(8):
        b, half = r // 2, r % 2
        m = dict(common)
        m["xm"] = np.ascontiguousarray(x[b, half * 1024:(half + 1) * 1024])
        m["xp"] = np.ascontiguousarray(x[b, 0:1024])
        m["pmask"] = np.full((128, 1), float(half), f32)
        m["cT"] = pl(c[b], 16)
        m["cosm"] = np.ascontiguousarray(cos_all[:, half * 1024:(half + 1) * 1024])
        m["sinm"] = np.ascontiguousarray(sin_all[:, half * 1024:(half + 1) * 1024])
        maps.append(m)
    return maps


def kernel(**inputs):
    nc, _ = build()
    maps = make_in_maps(inputs)
    res = run_bass_kernel_spmd(nc, maps, core_ids=list(range(8)))
    out = np.zeros((4, 2048, 2048), np.float32)
    for r in range(8):
        b, half = r // 2, r % 2
        out[b, half * 1024:(half + 1) * 1024] = res.results[r]["out"]
    return out
```

```python
import contextlib
import numpy as np
import ml_dtypes
import concourse.bass as bass
import concourse.mybir as mybir
from concourse.bass_utils import run_bass_kernel_spmd

F32 = mybir.dt.float32
BF16 = mybir.dt.bfloat16
I32 = mybir.dt.int32
AF = mybir.ActivationFunctionType
ALU = mybir.AluOpType

D = 2048
NT = 1024
NP = 1024
DFF = 5632
EPS = 1e-6
TWO_PI = 6.283185307179586


class Res:
    __slots__ = ("name", "w", "r")

    def __init__(self, name):
        self.name = name
        self.w = None
        self.r = {}


class FW:
    NDS = 6

    def __init__(self, nc, es):
        self.nc = nc
        self.engs = {"pe": nc.tensor, "act": nc.scalar, "dve": nc.vector, "pool": nc.gpsimd, "sp": nc.sync}
        self.sem = {k: es.enter_context(nc.semaphore("s_" + k)) for k in ["pe", "act", "dve", "pool"]}
        self.cnt = {k: 0 for k in self.sem}
        self.seen = {e: {} for e in self.engs}
        self.dsem = {q: [es.enter_context(nc.semaphore(f"d_{q}{i}")) for i in range(self.NDS)] for q in ["sp", "pool"]}
        self.dcnt = {q: [0] * self.NDS for q in self.dsem}
        self.drr = {q: 0 for q in self.dsem}
        self.psum = []
        for i in range(6):
            t = es.enter_context(nc.psum_tensor(f"psum{i}", [128, 512], F32))
            self.psum.append((t, Res(f"psum{i}")))
        self.y2 = (es.enter_context(nc.psum_tensor("psum_y2", [128, 1024], F32)), Res("psum_y2"))
        self.prr = 0
        self.out_events = []

    def R(self, name):
        return Res(name)

    def ps(self):
        self.prr = (self.prr + 1) % len(self.psum)
        t, r = self.psum[self.prr]
        return t, r

    def reserve(self, k):
        out = [self.psum.pop() for _ in range(k)]
        self.prr = 0
        return out

    def release(self, banks):
        self.psum.extend(banks)

    def _wait(self, e, ev):
        key, h, val = ev
        if self.seen[e].get(key, 0) >= val:
            return
        self.engs[e].wait_ge(h, val)
        self.seen[e][key] = val

    def _deps(self, e, reads, writes):
        skip = "pe" if e == "pe" else None
        for r in reads:
            if r.w is not None and r.w[0] != skip:
                self._wait(e, r.w)
        for w in writes:
            if w.w is not None and w.w[0] != skip:
                self._wait(e, w.w)
            for ev in w.r.values():
                if ev[0] != skip:
                    self._wait(e, ev)

    def _record(self, ev, reads, writes):
        for r in reads:
            r.r[ev[0]] = ev
        for w in writes:
            w.w = ev
            w.r = {}

    def op(self, e, fn, reads=(), writes=()):
        self._deps(e, reads, writes)
        ins = fn()
        self.cnt[e] += 1
        ins.then_inc(self.sem[e], 1)
        ev = (e, self.sem[e], self.cnt[e])
        self._record(ev, reads, writes)
        return ev

    def dma(self, q, out, in_, reads=(), writes=(), is_out=False):
        i = self.drr[q]
        self.drr[q] = (i + 1) % self.NDS
        key = f"d_{q}{i}"
        h = self.dsem[q][i]
        if self.dcnt[q][i] > 0:
            self._wait(q, (key, h, self.dcnt[q][i]))
        self._deps(q, reads, writes)
        ins = self.engs[q].dma_start(out=out, in_=in_)
        self.dcnt[q][i] += 16
        ins.then_inc(h, 16)
        ev = (key, h, self.dcnt[q][i])
        self._record(ev, reads, writes)
        if is_out:
            self.out_events.append(ev)
        return ev

    def barrier(self):
        evs = [(k, self.sem[k], self.cnt[k]) for k in self.sem if self.cnt[k] > 0]
        for q in self.dsem:
            for i in range(self.NDS):
                if self.dcnt[q][i] > 0:
                    evs.append((f"d_{q}{i}", self.dsem[q][i], self.dcnt[q][i]))
        for e in self.engs:
            for ev in evs:
                if not (e == "pe" and ev[0] == "pe"):
                    self._wait(e, ev)


def build(dbg=None, stop_after=99):
    nc = bass.Bass("TRN2", target_bir_lowering=False)
    dbg_out = {}

    def din(name, shape, dt=F32):
        return nc.dram_tensor(name, list(shape), dt, kind="ExternalInput").ap()

    xm = din("xm", [NT, D]); xp = din("xp", [NP, D])
    pmask_d = din("pmask", [128, 1]); cT_d = din("cT", [128, 16])
    w_ada = din("w_ada", [D, 6 * D]); b_ada = din("b_ada", [1, 6 * D])
    n1w_d = din("n1w", [128, 16]); n2w_d = din("n2w", [128, 16]); fnw_d = din("fnw", [1, D])
    w_in = din("w_in", [D, 5120]); rnw_d = din("rnw", [128, 8])
    A_re_d = din("A_re", [128, 32]); A_im_d = din("A_im", [128, 32]); LS_d = din("LS", [128, 32])
    B_re_d = din("B_re", [128, 512]); B_im_d = din("B_im", [128, 512])
    C_re_d = din("C_re", [128, 512]); C_im_d = din("C_im", [128, 512])
    Dq_d = din("Dq", [128, 8])
    w_glu = din("w_glu", [1024, 1024]); bglu_d = din("bglu", [128, 8])
    w_out = din("w_out", [D, D]); w_gu = din("w_gu", [D, 2 * DFF]); w_dn = din("w_dn", [DFF, D])
    ident_d = din("ident", [128, 128]); maskT_d = din("maskT", [128, 512]); qdec_d = din("qdec", [128, 512])
    kdec_d = din("kdec", [128, 4]); cosm_d = din("cosm", [128, NT]); sinm_d = din("sinm", [128, NT])
    cosp_d = din("cosp", [128, NP]); sinp_d = din("sinp", [128, NP])
    iota_d = din("iota", [128, 256]); E9_d = din("E9", [128, 288]); hm_d = din("hm", [128, 2])
    bmask_d = din("bmask", [128, 128]); rm3_d = din("rm3", [128, 1])
    out_d = nc.dram_tensor("out", [NT, D], F32, kind="ExternalOutput").ap()

    def dbg_tensor(name, shape):
        dbg_out[name] = nc.dram_tensor("dbg_" + name, list(shape), F32, kind="ExternalOutput").ap()
        return dbg_out[name]

    CD = [float(np.float32(np.exp(np.float32(128.0) * np.log1p(-np.exp2(np.float32(-5.0 - h)))))) for h in range(4)]

    with contextlib.ExitStack() as es:
        fw = FW(nc, es)
        V, A, P, T, SP = "dve", "act", "pool", "pe", "sp"

        def sb(stack, name, shape, dt=F32):
            return stack.enter_context(nc.sbuf_tensor("sb_" + name, list(shape), dt))

        def dump(name, ap, shape, reads, stack):
            t = sb(stack, "dmp_" + name, shape, F32)
            r = fw.R("dmp_" + name)
            fw.op(V, lambda: nc.vector.tensor_copy(t[:], ap), reads=reads, writes=[r])
            fw.dma(SP, dbg_tensor(name, shape)[:], t[:], reads=[r], is_out=True)

        identb = sb(es, "identb", [128, 128], BF16); identf = sb(es, "identf", [128, 128])
        pmask = sb(es, "pmask_sb", [128, 1])
        a1 = sb(es, "a1", [128, 16]); sh1 = sb(es, "sh1", [128, 16]); a2 = sb(es, "a2", [128, 16]); sh2 = sb(es, "sh2", [128, 16])
        g1bc = sb(es, "g1bc", [128, D]); g2bc = sb(es, "g2bc", [128, D])
        rnw = sb(es, "rnw_sb", [128, 8]); bglu = sb(es, "bglu_sb", [128, 8])
        R_const = fw.R("const")
        R_mod = fw.R("mod")
        for t, d in [(identf, ident_d), (pmask, pmask_d), (rnw, rnw_d), (bglu, bglu_d)]:
            fw.dma(SP, t[:], d[:], writes=[R_const])
        fw.dma(P, identb[:], ident_d[:], writes=[R_const])

        open_stacks = []
        ph25 = contextlib.ExitStack(); open_stacks.append(ph25)
        mixT = sb(ph25, "mixR", [128, 8, NT], BF16)
        ssmg = sb(ph25, "ssmg", [128, 8, NT], BF16)
        R_mix = fw.R("mixT"); R_u = fw.R("uT"); R_ssmg = fw.R("ssmg")
        ph13 = contextlib.ExitStack(); open_stacks.append(ph13)
        h1T = sb(ph13, "h1T", [128, 16, NP + NT], BF16)
        R_h1 = fw.R("h1T")
        R_mod1 = fw.R("mod1")

        def rms_norm_T(ph, tiles, avec, shvec, hT, hR, tag, bg=None, Rm=None):
            xn4 = sb(ph, "xn4" + tag, [128, 4, D], BF16); junk = sb(ph, "junk" + tag, [128, D], BF16)
            ssq = sb(ph, "ssq" + tag, [128, 1]); rstd = sb(ph, "rstd" + tag, [128, 1])
            R_xn = [fw.R(f"xn{i}") for i in range(4)]; R_j = fw.R("junk"); R_s = fw.R("ssq")
            for gi in range(len(tiles) // 4):
                for t4 in range(4):
                    xap, xR = tiles[gi * 4 + t4]()
                    xRl = xR if isinstance(xR, list) else [xR]
                    fw.op(A, lambda xap=xap: nc.scalar.activation(junk[:], xap, AF.Square, accum_out=ssq[:]),
                          reads=xRl, writes=[R_j, R_s])
                    fw.op(V, lambda: nc.vector.tensor_scalar(rstd[:], ssq[:], 1.0 / D, EPS, op0=ALU.mult, op1=ALU.add),
                          reads=[R_s], writes=[R_s])
                    fw.op(A, lambda: nc.scalar.sqrt(rstd[:], rstd[:]), reads=[R_s], writes=[R_s])
                    fw.op(V, lambda: nc.vector.reciprocal(rstd[:], rstd[:]), reads=[R_s], writes=[R_s])
                    fw.op(V, lambda xap=xap, t4=t4: nc.vector.tensor_scalar(
                        xn4[:, t4, :], xap, rstd[:, 0:1], None, op0=ALU.mult), reads=xRl + [R_s], writes=[R_xn[t4]])
                for fc in range(16):
                    pt, pr = fw.ps()
                    ptb = pt[:].bitcast(BF16)
                    for t4 in range(4):
                        fw.op(T, lambda ptb=ptb, t4=t4, fc=fc: nc.tensor.transpose(
                            ptb[:, t4 * 128:(t4 + 1) * 128], xn4[:, t4, fc * 128:(fc + 1) * 128], identb[:]),
                            reads=[R_xn[t4], R_const], writes=[pr])
                    fw.op(A, lambda ptb=ptb, fc=fc, gi=gi: nc.scalar.activation(
                        hT[:, fc, gi * 512:(gi + 1) * 512], ptb[:, 0:512], AF.Identity,
                        bias=shvec[:, fc:fc + 1], scale=avec[:, fc:fc + 1]), reads=[pr, Rm if Rm is not None else R_mod], writes=[hR])
                    if bg is not None:
                        next(bg, None)


        with contextlib.ExitStack() as ph:
            cT = sb(ph, "cT_sb", [128, 16]); condb = sb(ph, "condb", [128, 16], BF16)
            crep = sb(ph, "crep", [128, 16, 128], BF16)
            n1w = sb(ph, "n1w_sb", [128, 16]); n2w = sb(ph, "n2w_sb", [128, 16])
            seg_sb = sb(ph, "seg_sb", [128, D]); bbc = sb(ph, "bbc", [128, D])
            tmpd = sb(ph, "tmpd", [128, 16, 128])
            wsl = [sb(ph, f"wada{i}", [128, D], BF16) for i in range(4)]
            wslR = [fw.R(f"wada{i}") for i in range(4)]
            R_c = fw.R("c"); R_seg = fw.R("seg"); R_bbc = fw.R("bbc"); R_tmpd = fw.R("tmpd")
            fw.dma(SP, cT[:], cT_d[:], writes=[R_c])
            fw.dma(SP, n1w[:], n1w_d[:], writes=[R_c])
            fw.dma(SP, n2w[:], n2w_d[:], writes=[R_c])
            fw.op(A, lambda: nc.scalar.activation(condb[:], cT[:], AF.Silu), reads=[R_c], writes=[R_c])
            fw.op(V, lambda: nc.vector.tensor_copy(crep[:], condb[:].unsqueeze(2).to_broadcast([128, 16, 128])),
                  reads=[R_c], writes=[R_c])
            wi = [0]

            def mod_segs(segs, banks_fn):
                for seg in segs:
                    Rm = R_mod1 if seg in (0, 1) else R_mod
                    fw.dma(SP, bbc[:], b_ada[0:1, seg * D:(seg + 1) * D].partition_broadcast(128), writes=[R_bbc])
                    pss = banks_fn()
                    for kc in range(16):
                        s = wi[0] % 4; wi[0] += 1
                        fw.dma(P, wsl[s][:], w_ada[kc * 128:(kc + 1) * 128, seg * D:(seg + 1) * D], writes=[wslR[s]])
                        for cg in range(4):
                            pt, pr = pss[cg]
                            fw.op(T, lambda pt=pt, s=s, cg=cg, kc=kc: nc.tensor.matmul(
                                pt, crep[:, kc, :], wsl[s][:, cg * 512:(cg + 1) * 512], start=(kc == 0), stop=(kc == 15)),
                                reads=[R_c, wslR[s]], writes=[pr])
                        yield
                    dst = g1bc if seg == 2 else (g2bc if seg == 5 else seg_sb)
                    Rd = R_mod if seg in (2, 5) else R_seg
                    for cg in range(4):
                        pt, pr = pss[cg]
                        fw.op(V, lambda pt=pt, cg=cg, dst=dst: nc.vector.tensor_tensor(
                            dst[:, cg * 512:(cg + 1) * 512], pt, bbc[:, cg * 512:(cg + 1) * 512], op=ALU.add),
                            reads=[pr, R_bbc], writes=[Rd])
                    if seg in (0, 1, 3, 4):
                        vec = {0: sh1, 1: a1, 3: sh2, 4: a2}[seg]
                        fw.op(V, lambda: nc.vector.tensor_tensor(
                            tmpd[:], seg_sb[:].rearrange("p (f m) -> p f m", m=128),
                            identf[:].unsqueeze(1).to_broadcast([128, 16, 128]), op=ALU.mult),
                            reads=[R_seg, R_const], writes=[R_tmpd])
                        fw.op(V, lambda vec=vec: nc.vector.tensor_reduce(
                            vec[:], tmpd[:], axis=mybir.AxisListType.X, op=ALU.add), reads=[R_tmpd], writes=[Rm])
                        if seg in (1, 4):
                            nw = n1w if seg == 1 else n2w
                            fw.op(V, lambda vec=vec: nc.vector.tensor_scalar(vec[:], vec[:], 1.0, None, op0=ALU.add),
                                  reads=[Rm], writes=[Rm])
                            fw.op(V, lambda vec=vec, nw=nw: nc.vector.tensor_tensor(vec[:], vec[:], nw[:], op=ALU.mult),
                                  reads=[R_c, Rm], writes=[Rm])
                    yield

            def banks_rot():
                return [(t[:], r) for (t, r) in [fw.ps() for _ in range(4)]]
            for _ in mod_segs([1, 0], banks_rot):
                pass
            if dbg == 0:
                for _ in mod_segs([2, 4, 3, 5], banks_rot):
                    pass
                dump("a1", a1[:], [128, 16], [R_mod1], ph); dump("sh1", sh1[:], [128, 16], [R_mod1], ph)
                dump("a2", a2[:], [128, 16], [R_mod], ph); dump("g1", g1bc[:], [128, D], [R_mod], ph); dump("n1w", n1w[:], [128, 16], [R_c], ph)
            if stop_after >= 1:
                resv = fw.reserve(2)
                y2t, y2r = fw.y2
                fixed = [(y2t[:, 0:512], y2r), (y2t[:, 512:1024], fw.R("y2b")), (resv[0][0][:], resv[0][1]), (resv[1][0][:], resv[1][1])]
                bg = mod_segs([2, 4, 3, 5], lambda: fixed)
                with contextlib.ExitStack() as ph1:
                    xs = [sb(ph1, f"xs{i}", [128, D]) for i in range(3)]
                    xsR = [fw.R(f"xs{i}") for i in range(3)]
                    cnt = [0]

                    def mk_loader(src, t):
                        def f():
                            s = cnt[0] % 3; cnt[0] += 1
                            fw.dma(SP, xs[s][:], src[t * 128:(t + 1) * 128, :], writes=[xsR[s]])
                            return xs[s][:], xsR[s]
                        return f
                    tiles = [mk_loader(xp, t) for t in range(8)] + [mk_loader(xm, t) for t in range(8)]
                    rms_norm_T(ph1, tiles, a1, sh1, h1T, R_h1, "_n1", bg=bg, Rm=R_mod1)
                    for _ in bg:
                        pass
                    fw.release(resv)
                    if dbg == 1:
                        dump("h1T0", h1T[:, 0, :], [128, 2048], [R_h1], ph1); dump("h1T15", h1T[:, 15, :], [128, 2048], [R_h1], ph1)
                    fw.barrier()
            else:
                fw.barrier()

        wcnt = [0]

        def load_w(slots, slotR, src_ap):
            s = wcnt[0] % len(slots); wcnt[0] += 1
            fw.dma(P, slots[s][:], src_ap, writes=[slotR[s]])
            return slots[s], slotR[s]

        w_in_v = w_in.rearrange("(kc p) c -> p kc c", p=128)

        if stop_after >= 2:
            with contextlib.ExitStack() as ph:
                wv = [sb(ph, f"wv{i}", [128, 16, 256], BF16) for i in range(3)]
                wvR = [fw.R(f"wv{i}") for i in range(3)]
                maskT = sb(ph, "maskT", [128, 4, 128]); qdec = sb(ph, "qdec", [128, 4, 128]); kdec = sb(ph, "kdec", [128, 4])
                kdecm = sb(ph, "kdecm", [128, 4])
                cosm = sb(ph, "cosm", [128, NT]); sinm = sb(ph, "sinm", [128, NT])
                cosp = sb(ph, "cosp", [128, NP]); sinp = sb(ph, "sinp", [128, NP])
                R_tab = fw.R("tab")
                fw.dma(SP, maskT[:], maskT_d[:].rearrange("p (h i) -> p h i", h=4), writes=[R_tab])
                fw.dma(SP, qdec[:], qdec_d[:].rearrange("p (h i) -> p h i", h=4), writes=[R_tab])
                fw.dma(SP, kdec[:], kdec_d[:], writes=[R_tab])
                for t, d in [(cosm, cosm_d), (sinm, sinm_d), (cosp, cosp_d), (sinp, sinp_d)]:
                    fw.dma(SP, t[:], d[:], writes=[R_tab])
                fw.op(V, lambda: nc.vector.tensor_scalar(kdecm[:], kdec[:], pmask[:, 0:1], None, op0=ALU.mult),
                      reads=[R_tab, R_const], writes=[R_tab])
                qT = sb(ph, "qT", [128, 2, NT], BF16); qdT = sb(ph, "qdT", [128, 2, NT], BF16)
                kT = sb(ph, "kT", [128, 2, NP + NT], BF16)
                ktok = sb(ph, "ktok", [128, 16, 256], BF16); vtok = sb(ph, "vtok", [128, 16, 256], BF16)
                gtok = sb(ph, "gtok", [128, 8, 256], BF16)
                state = sb(ph, "state", [128, 2, 256])
                rt = [sb(ph, f"rt{i}", [128, 512]) for i in range(4)]
                ssflat = ssmg[:].rearrange("p a b -> p (a b)")
                stateb8 = ssflat[:, 0:4096].rearrange("p (n d v) -> p n d v", n=8, d=2)
                yn4L = [ssflat[:, 4096:6144].bitcast(F32).rearrange("p (a b) -> p a b", a=4)] * 2
                sq2L = [ssflat[:, 6144:8192].bitcast(F32).rearrange("p (k b) -> p k b", k=2)] * 2
                msk4L = [sb(ph, f"msk4{i}", [128, 4, 128], BF16) for i in range(2)]
                rtok4L = [sb(ph, "rtok4", [128, 4, 256], BF16)] * 2
                stL = [sb(ph, f"st{i}", [128, 16]) for i in range(2)]
                R_q = fw.R("qT"); R_qd = fw.R("qdT"); R_k = fw.R("kT"); R_kt = fw.R("ktok"); R_v = fw.R("vtok"); R_g = fw.R("gtok")
                R_state = fw.R("state"); R_sb = fw.R("stateb"); R_rt = [fw.R(f"rt{i}") for i in range(4)]
                R_mskL = [fw.R(f"msk{i}") for i in range(2)]; R_rtokL = [fw.R("rtok4")] * 2
                _ryn = fw.R("yn4"); _rsq = fw.R("sq2"); R_ynL = [_ryn, _ryn]; R_sqL = [_rsq, _rsq]
                R_stL = [fw.R(f"st{i}") for i in range(2)]; R_sb8 = [fw.R(f"stateb8_{i}") for i in range(8)]

                def projT_rot(wt, tok0, ntok, dstT, Rdst, ctab, stab, toff):
                    w_, wR_ = wt
                    for g in range(ntok // 512):
                        p1, p1R = fw.ps(); p2, p2R = fw.ps()
                        for (w, wR, pt, pr) in [(w_[:, :, 0:128], wR_, p1, p1R), (w_[:, :, 128:256], wR_, p2, p2R)]:
                            for kc in range(16):
                                fw.op(T, lambda w=w, pt=pt, kc=kc, g=g: nc.tensor.matmul(
                                    pt[:], w[:, kc, :], h1T[:, kc, tok0 + g * 512: tok0 + (g + 1) * 512],
                                    start=(kc == 0), stop=(kc == 15)), reads=[wR, R_h1], writes=[pr])
                        cs = ctab[:, toff + g * 512: toff + (g + 1) * 512]; sn = stab[:, toff + g * 512: toff + (g + 1) * 512]
                        fw.op(V, lambda p1=p1, cs=cs: nc.vector.tensor_tensor(rt[0][:], p1[:], cs, op=ALU.mult), reads=[p1R, R_tab], writes=[R_rt[0]])
                        fw.op(V, lambda p2=p2, sn=sn: nc.vector.tensor_tensor(rt[1][:], p2[:], sn, op=ALU.mult), reads=[p2R, R_tab], writes=[R_rt[1]])
                        fw.op(V, lambda p2=p2, cs=cs: nc.vector.tensor_tensor(rt[2][:], p2[:], cs, op=ALU.mult), reads=[p2R, R_tab], writes=[R_rt[2]])
                        fw.op(V, lambda p1=p1, sn=sn: nc.vector.tensor_tensor(rt[3][:], p1[:], sn, op=ALU.mult), reads=[p1R, R_tab], writes=[R_rt[3]])
                        o0 = dstT[:, 0, tok0 - (0 if dstT is kT else NP) + g * 512: tok0 - (0 if dstT is kT else NP) + (g + 1) * 512]
                        o1 = dstT[:, 1, tok0 - (0 if dstT is kT else NP) + g * 512: tok0 - (0 if dstT is kT else NP) + (g + 1) * 512]
                        fw.op(P, lambda o0=o0: nc.gpsimd.tensor_tensor(o0, rt[0][:], rt[1][:], op=ALU.subtract),
                              reads=[R_rt[0], R_rt[1]], writes=[Rdst])
                        fw.op(P, lambda o1=o1: nc.gpsimd.tensor_tensor(o1, rt[2][:], rt[3][:], op=ALU.add),
                              reads=[R_rt[2], R_rt[3]], writes=[Rdst])

                deferred = []
                for h in range(4):
                    wq_ = load_w(wv, wvR, w_in_v[:, :, h * 256:(h + 1) * 256])
                    wk_ = load_w(wv, wvR, w_in_v[:, :, 1024 + h * 256: 1024 + (h + 1) * 256])
                    projT_rot(wq_, NP, NT, qT, R_q, cosm, sinm, 0)
                    projT_rot(wk_, 0, NP, kT, R_k, cosp, sinp, 0)
                    projT_rot(wk_, NP, NT, kT, R_k, cosm, sinm, 0)
                    while deferred:
                        deferred.pop(0)()
                    for dc in range(2):
                        fw.op(P, lambda dc=dc, h=h: nc.gpsimd.tensor_tensor(
                            qdT[:, dc, :].rearrange("p (t i) -> p t i", i=128), qT[:, dc, :].rearrange("p (t i) -> p t i", i=128),
                            qdec[:, h, :].unsqueeze(1).to_broadcast([128, 8, 128]), op=ALU.mult),
                            reads=[R_q, R_tab], writes=[R_qd])
                    wvt, wvtR = load_w(wv, wvR, w_in_v[:, :, 2048 + h * 256: 2048 + (h + 1) * 256])
                    for tt in range(0, 16, 2):
                        pt, pr = fw.ps()
                        for u2 in range(2):
                            for kc in range(16):
                                fw.op(T, lambda pt=pt, u2=u2, kc=kc, tt=tt: nc.tensor.matmul(
                                    pt[:, u2 * 256:(u2 + 1) * 256], h1T[:, kc, (tt + u2) * 128:(tt + u2 + 1) * 128], wvt[:, kc, :],
                                    start=(kc == 0), stop=(kc == 15)), reads=[wvtR, R_h1], writes=[pr])
                        fw.op(A, lambda pt=pt, tt=tt: nc.scalar.copy(vtok[:, tt:tt + 2, :], pt[:].rearrange("p (a b) -> p a b", a=2)),
                              reads=[pr], writes=[R_v])
                    wgt, wgtR = load_w(wv, wvR, w_in_v[:, :, 3072 + h * 256: 3072 + (h + 1) * 256])
                    for tt in range(0, 8, 2):
                        pt, pr = fw.ps()
                        for u2 in range(2):
                            for kc in range(16):
                                fw.op(T, lambda pt=pt, u2=u2, kc=kc, tt=tt: nc.tensor.matmul(
                                    pt[:, u2 * 256:(u2 + 1) * 256], h1T[:, kc, NP + (tt + u2) * 128: NP + (tt + u2 + 1) * 128], wgt[:, kc, :],
                                    start=(kc == 0), stop=(kc == 15)), reads=[wgtR, R_h1], writes=[pr])
                        fw.op(A, lambda pt=pt, tt=tt: nc.scalar.activation(
                            gtok[:, tt:tt + 2, :], pt[:].rearrange("p (a b) -> p a b", a=2), AF.Silu), reads=[pr], writes=[R_g])
                    for n in range(0, 16, 2):
                        pt, pr = fw.ps(); ptb = pt[:].bitcast(BF16)
                        for u2 in range(2):
                            for dc in range(2):
                                fw.op(T, lambda ptb=ptb, u2=u2, dc=dc, n=n: nc.tensor.transpose(
                                    ptb[:, u2 * 256 + dc * 128: u2 * 256 + (dc + 1) * 128], kT[:, dc, (n + u2) * 128:(n + u2 + 1) * 128], identb[:]),
                                    reads=[R_k, R_const], writes=[pr])
                        kd = kdecm if n < 8 else kdec
                        fw.op(A, lambda ptb=ptb, n=n, kd=kd, h=h: nc.scalar.activation(
                            ktok[:, n:n + 2, :], ptb[:, 0:512].rearrange("p (a b) -> p a b", a=2), AF.Copy, scale=kd[:, h:h + 1]),
                            reads=[pr, R_tab], writes=[R_kt])
                    for bt in range(2):
                        i0_ = bt * 4
                        msk4 = msk4L[bt]; R_msk = R_mskL[bt]
                        ps_s, ps_sR = fw.ps()
                        for ci in range(4):
                            i = i0_ + ci; n = 8 + i
                            for dc in range(2):
                                fw.op(T, lambda dc=dc, n=n, i=i, ci=ci, ps_s=ps_s: nc.tensor.matmul(
                                    ps_s[:, ci * 128:(ci + 1) * 128], kT[:, dc, n * 128:(n + 1) * 128], qT[:, dc, i * 128:(i + 1) * 128],
                                    start=(dc == 0), stop=(dc == 1)), reads=[R_k, R_q], writes=[ps_sR])
                        fw.op(V, lambda ps_s=ps_s, h=h: nc.vector.tensor_tensor(
                            msk4[:], ps_s[:].rearrange("p (a b) -> p a b", a=4), maskT[:, h, :].unsqueeze(1).to_broadcast([128, 4, 128]), op=ALU.mult),
                            reads=[ps_sR, R_tab], writes=[R_msk])
                    fw.op(V, lambda: nc.vector.memset(state[:], 0.0), writes=[R_state])
                    for n in range(15):
                        ps_k, ps_kR = fw.ps()
                        for dc in range(2):
                            fw.op(T, lambda ps_k=ps_k, dc=dc, n=n: nc.tensor.matmul(
                                ps_k[:, dc * 256:(dc + 1) * 256], ktok[:, n, dc * 128:(dc + 1) * 128], vtok[:, n, :], start=True, stop=True),
                                reads=[R_kt, R_v], writes=[ps_kR])
                        fw.op(V, lambda ps_k=ps_k, h=h: nc.vector.scalar_tensor_tensor(
                            state[:], state[:], CD[h], ps_k[:].rearrange("p (a b) -> p a b", a=2), op0=ALU.mult, op1=ALU.add),
                            reads=[ps_kR], writes=[R_state])
                        if n >= 7:
                            fw.op(A, lambda n=n: nc.scalar.copy(stateb8[:, n - 7, :, :], state[:]), reads=[R_state], writes=[R_sb8[n - 7]])
                    for bt in range(2):
                        i0_ = bt * 4
                        msk4, yn4, rtok4, sq2, st = msk4L[bt], yn4L[bt], rtok4L[bt], sq2L[bt], stL[bt]
                        R_msk, R_yn, R_rtok, R_sq, R_st = R_mskL[bt], R_ynL[bt], R_rtokL[bt], R_sqL[bt], R_stL[bt]
                        pos_ = [fw.ps(), fw.ps()]
                        for ci in range(4):
                            i = i0_ + ci; n = 8 + i
                            ps_o, ps_oR = pos_[ci // 2]
                            reg = ps_o[:, (ci % 2) * 256:(ci % 2 + 1) * 256]
                            fw.op(T, lambda reg=reg, n=n, ci=ci: nc.tensor.matmul(reg, msk4[:, ci, :], vtok[:, n, :], start=True, stop=False),
                                  reads=[R_msk, R_v], writes=[ps_oR])
                            if i > 0 or True:
                                for dc in range(2):
                                    fw.op(T, lambda reg=reg, dc=dc, i=i: nc.tensor.matmul(
                                        reg, qdT[:, dc, i * 128:(i + 1) * 128], stateb8[:, i, dc, :], start=False, stop=(dc == 1)),
                                        reads=[R_qd, R_sb8[i]], writes=[ps_oR])
                        if bt == 1:
                            deferred.pop(0)()
                        for ci in range(4):
                            ps_o, ps_oR = pos_[ci // 2]
                            reg = ps_o[:, (ci % 2) * 256:(ci % 2 + 1) * 256]
                            fw.op(A, lambda reg=reg, ci=ci: nc.scalar.activation(sq2[:, 0, 0:256], reg, AF.Copy, accum_out=st[:, ci:ci + 1]),
                                  reads=[ps_oR], writes=[R_sq, R_st])
                            fw.op(A, lambda reg=reg, ci=ci: nc.scalar.activation(sq2[:, 1, 0:256], reg, AF.Square, accum_out=st[:, 4 + ci:5 + ci]),
                                  reads=[ps_oR], writes=[R_sq, R_st])
                        fw.op(V, lambda: nc.vector.tensor_scalar(st[:, 0:8], st[:, 0:8], 1.0 / 256, None, op0=ALU.mult), reads=[R_st], writes=[R_st])
                        fw.op(V, lambda: nc.vector.tensor_tensor(st[:, 8:12], st[:, 0:4], st[:, 0:4], op=ALU.mult), reads=[R_st], writes=[R_st])
                        fw.op(V, lambda: nc.vector.scalar_tensor_tensor(st[:, 12:16], st[:, 4:8], EPS, st[:, 8:12], op0=ALU.add, op1=ALU.subtract),
                              reads=[R_st], writes=[R_st])
                        fw.op(A, lambda: nc.scalar.sqrt(st[:, 12:16], st[:, 12:16]), reads=[R_st], writes=[R_st])
                        fw.op(V, lambda: nc.vector.reciprocal(st[:, 12:16], st[:, 12:16]), reads=[R_st], writes=[R_st])
                        for ci in range(4):
                            ps_o, ps_oR = pos_[ci // 2]
                            reg = ps_o[:, (ci % 2) * 256:(ci % 2 + 1) * 256]
                            fw.op(V, lambda reg=reg, ci=ci: nc.vector.tensor_scalar(
                                yn4[:, ci, :], reg, st[:, ci:ci + 1], st[:, 12 + ci:13 + ci], op0=ALU.subtract, op1=ALU.mult),
                                reads=[ps_oR, R_st], writes=[R_yn])
                        fw.op(P, lambda i0_=i0_: nc.gpsimd.tensor_tensor(rtok4[:], yn4[:], gtok[:, i0_:i0_ + 4, :], op=ALU.mult),
                              reads=[R_yn, R_g], writes=[R_rtok])
                        def _tr_evac(rtok4=rtok4, R_rtok=R_rtok, i0_=i0_, h=h):
                            ps_t, ps_tR = fw.ps(); ptb = ps_t[:].bitcast(BF16)
                            for dc in range(2):
                                for ci in range(4):
                                    fw.op(T, lambda ptb=ptb, dc=dc, ci=ci: nc.tensor.transpose(
                                        ptb[:, dc * 512 + ci * 128: dc * 512 + (ci + 1) * 128], rtok4[:, ci, dc * 128:(dc + 1) * 128], identb[:]),
                                        reads=[R_rtok, R_const], writes=[ps_tR])
                            for dc in range(2):
                                fw.op(A, lambda ptb=ptb, dc=dc, i0_=i0_, h=h: nc.scalar.activation(
                                    mixT[:, h * 2 + dc, i0_ * 128:(i0_ + 4) * 128], ptb[:, dc * 512:(dc + 1) * 512], AF.Copy,
                                    scale=rnw[:, h * 2 + dc: h * 2 + dc + 1]), reads=[ps_tR, R_const], writes=[R_mix])
                        deferred.append(_tr_evac)
                while deferred:
                    deferred.pop(0)()
                fw.barrier()
            if dbg == 2:
                with contextlib.ExitStack() as dph:
                    dump("mixT0", mixT[:, 0, :], [128, NT], [R_mix], dph); dump("mixT7", mixT[:, 7, :], [128, NT], [R_mix], dph)
                    fw.barrier()

        iota5_d = din("iota5", [128, 512])
        if stop_after >= 3:
            with contextlib.ExitStack() as ph:
                uT = sb(ph, "uT", [128, 8, NP + NT], BF16)
                wsl = [sb(ph, f"winu{i}", [128, 16, 128], BF16) for i in range(2)]
                wslR = [fw.R(f"winu{i}") for i in range(2)]
                scr_off = [0]

                h1flat = h1T[:].rearrange("p a b -> p (a b)")

                def scr(shape, dt=F32):
                    nel = int(np.prod(shape[1:]))
                    esz = 4 if dt in (F32, I32) else 2
                    nby = -(-(nel * esz) // 64) * 64
                    o = scr_off[0]; scr_off[0] += nby
                    assert scr_off[0] <= 65536, "out of h1T scratch"
                    flat = h1flat[:, o // 2:(o + nel * esz) // 2]
                    if dt != BF16:
                        flat = flat.bitcast(dt)
                    v = flat
                    if len(shape) == 3:
                        v = v.rearrange("p (a b) -> p a b", a=shape[1])
                    elif len(shape) == 4:
                        v = v.rearrange("p (a b c) -> p a b c", a=shape[1], b=shape[2])
                    elif len(shape) == 5:
                        v = v.rearrange("p (a b c d) -> p a b c d", a=shape[1], b=shape[2], c=shape[3])
                    return v

                class _T:
                    def __init__(self, v):
                        self.v = v

                    def __getitem__(self, k):
                        return self.v[k] if not (isinstance(k, slice) and k == slice(None)) else self.v

                def scrT(shape, dt=F32):
                    return _T(scr(shape, dt))
                def t32(name):
                    return sb(ph, name, [128, 32])[:]
                Are = t32("Are"); Aim = t32("Aim"); dtt = t32("dtt"); ar = t32("ar"); ph2 = t32("ph2"); ph8 = t32("ph8")
                cre = t32("cre"); cim = t32("cim"); tA = t32("tA"); tB = t32("tB"); tC = t32("tC")
                hm = sb(ph, "hm_sb", [128, 2]); bmask = sb(ph, "bmask_sb", [128, 128]); Dq = sb(ph, "Dq_sb", [128, 8])
                rm3 = sb(ph, "rm3_sb", [128, 1]); iota = sb(ph, "iota_sb", [128, 256])
                E9 = sb(ph, "E9_sb", [128, 9, 32]); magE = sb(ph, "magE", [128, 9, 32]); PWre = sb(ph, "PWre", [128, 9, 32]); PWim = sb(ph, "PWim", [128, 9, 32])
                eF = sb(ph, "eF", [128, 9, 32]); eF2 = sb(ph, "eF2", [128, 9, 32]); eA = sb(ph, "eA", [128, 9, 32]); eI = sb(ph, "eI", [128, 9, 32], I32)
                small7 = scr([128, 7, 32, 16])
                Bre, Bim, Cre, Cim, bbre, bbim, tb1 = [small7[:, i_] for i_ in range(7)]
                big_off = scr_off[0]
                big1 = scr([128, 9, 32, 16]); big2 = scr([128, 9, 32, 16])
                CL = [sb(ph, "CLre", [128, 9, 32, 16], BF16), sb(ph, "CLim", [128, 9, 32, 16], BF16)]
                BstA = [sb(ph, "BstAre", [128, 8, 32, 16], BF16), sb(ph, "BstAim", [128, 8, 32, 16], BF16)]
                R_p = fw.R("s5p")
                for t, d in [(Are, A_re_d), (Aim, A_im_d), (dtt, LS_d), (hm[:], hm_d), (bmask[:], bmask_d), (Dq[:], Dq_d), (iota[:], iota_d), (rm3[:], rm3_d)]:
                    fw.dma(SP, t, d[:], writes=[R_p])
                fw.dma(SP, E9[:], E9_d[:].rearrange("p (a b) -> p a b", b=32), writes=[R_p])
                RW = [R_p]

                def vop(f):
                    fw.op(V, f, reads=RW, writes=RW)

                def aop(f):
                    fw.op(A, f, reads=RW, writes=RW)

                def sincos(ang_t, sin_out, cos_out, tmpF, tmpF2, tmpI):
                    vop(lambda: nc.vector.tensor_copy(tmpI, ang_t))
                    vop(lambda: nc.vector.tensor_copy(tmpF, tmpI))
                    vop(lambda: nc.vector.tensor_tensor(tmpF, ang_t, tmpF, op=ALU.subtract))
                    aop(lambda: nc.scalar.activation(sin_out, tmpF, AF.Sin, scale=TWO_PI))
                    vop(lambda: nc.vector.tensor_scalar(tmpF2, ang_t, 0.25, None, op0=ALU.add))
                    vop(lambda: nc.vector.tensor_copy(tmpI, tmpF2))
                    vop(lambda: nc.vector.tensor_copy(tmpF, tmpI))
                    vop(lambda: nc.vector.tensor_tensor(tmpF, tmpF2, tmpF, op=ALU.subtract))
                    aop(lambda: nc.scalar.activation(cos_out, tmpF, AF.Sin, scale=TWO_PI))

                zlhs = sb(ph, "zlhs", [128, 128], BF16)
                vop(lambda: nc.vector.memset(zlhs[:], 0.0))
                aop(lambda: nc.scalar.activation(dtt, dtt, AF.Exp))
                vop(lambda: nc.vector.tensor_tensor(ar, Are, dtt, op=ALU.mult))
                vop(lambda: nc.vector.tensor_tensor(ph2, Aim, dtt, op=ALU.mult))
                vop(lambda: nc.vector.tensor_scalar(ph2, ph2, 1.0 / TWO_PI, None, op0=ALU.mult))
                vop(lambda: nc.vector.tensor_scalar(ph8, ph2, 8.0, None, op0=ALU.mult))
                b9 = lambda t: t.unsqueeze(1).to_broadcast([128, 9, 32])
                vop(lambda: nc.vector.tensor_tensor(eA[:], E9[:], b9(ar), op=ALU.mult))
                aop(lambda: nc.scalar.activation(magE[:], eA[:], AF.Exp))
                vop(lambda: nc.vector.tensor_tensor(eA[:], E9[:], b9(ph2), op=ALU.mult))
                sincos(eA[:], PWim[:], PWre[:], eF[:], eF2[:], eI[:])
                vop(lambda: nc.vector.tensor_tensor(PWre[:], PWre[:], magE[:], op=ALU.mult))
                vop(lambda: nc.vector.tensor_tensor(PWim[:], PWim[:], magE[:], op=ALU.mult))
                lre = PWre[:, 1, :]; lim = PWim[:, 1, :]
                vop(lambda: nc.vector.tensor_scalar(tA, lre, -1.0, None, op0=ALU.add))
                vop(lambda: nc.vector.tensor_tensor(tB, Are, Are, op=ALU.mult))
                vop(lambda: nc.vector.tensor_tensor(tC, Aim, Aim, op=ALU.mult))
                vop(lambda: nc.vector.tensor_tensor(tB, tB, tC, op=ALU.add))
                vop(lambda: nc.vector.reciprocal(tB, tB))
                vop(lambda: nc.vector.tensor_tensor(cre, tA, Are, op=ALU.mult))
                vop(lambda: nc.vector.tensor_tensor(tC, lim, Aim, op=ALU.mult))
                vop(lambda: nc.vector.tensor_tensor(cre, cre, tC, op=ALU.add))
                vop(lambda: nc.vector.tensor_tensor(cre, cre, tB, op=ALU.mult))
                vop(lambda: nc.vector.tensor_tensor(cim, lim, Are, op=ALU.mult))
                vop(lambda: nc.vector.tensor_tensor(tC, tA, Aim, op=ALU.mult))
                vop(lambda: nc.vector.tensor_tensor(cim, cim, tC, op=ALU.subtract))
                vop(lambda: nc.vector.tensor_tensor(cim, cim, tB, op=ALU.mult))
                for cc in range(8):
                    w, wR = load_w(wsl, wslR, w_in_v[:, :, 4096 + cc * 128: 4096 + (cc + 1) * 128])
                    for g in range(4):
                        pt, pr = fw.ps()
                        for kc in range(16):
                            fw.op(T, lambda pt=pt, w=w, kc=kc, g=g: nc.tensor.matmul(
                                pt[:], w[:, kc, :], h1T[:, kc, g * 512:(g + 1) * 512], start=(kc == 0), stop=(kc == 15)),
                                reads=[wR, R_h1], writes=[pr])
                        if g < 2:
                            fw.op(A, lambda pt=pt, cc=cc, g=g: nc.scalar.activation(
                                uT[:, cc, g * 512:(g + 1) * 512], pt[:], AF.Copy, scale=pmask[:, 0:1]), reads=[pr, R_const], writes=[R_u])
                        else:
                            fw.op(A, lambda pt=pt, cc=cc, g=g: nc.scalar.copy(uT[:, cc, g * 512:(g + 1) * 512], pt[:]), reads=[pr], writes=[R_u])
                fw.barrier()
                for t, d in [(Bre, B_re_d), (Bim, B_im_d), (Cre, C_re_d), (Cim, C_im_d)]:
                    fw.dma(SP, t, d[:].rearrange("p (a b) -> p a b", b=16), writes=[R_p])
                bc = lambda t: t.unsqueeze(2).to_broadcast([128, 32, 16])
                vop(lambda: nc.vector.tensor_tensor(bbre, Bre, bc(cre), op=ALU.mult))
                vop(lambda: nc.vector.tensor_tensor(tb1, Bim, bc(cim), op=ALU.mult))
                vop(lambda: nc.vector.tensor_tensor(bbre, bbre, tb1, op=ALU.subtract))
                vop(lambda: nc.vector.tensor_tensor(bbim, Bim, bc(cre), op=ALU.mult))
                vop(lambda: nc.vector.tensor_tensor(tb1, Bre, bc(cim), op=ALU.mult))
                vop(lambda: nc.vector.tensor_tensor(bbim, bbim, tb1, op=ALU.add))
                X9 = lambda t: t.unsqueeze(1).to_broadcast([128, 9, 32, 16])
                PW9 = lambda t: t.unsqueeze(3).to_broadcast([128, 9, 32, 16])
                vop(lambda: nc.vector.tensor_tensor(big1, X9(Cre), PW9(PWre[:]), op=ALU.mult))
                vop(lambda: nc.vector.tensor_tensor(big2, X9(Cim), PW9(PWim[:]), op=ALU.mult))
                vop(lambda: nc.vector.tensor_tensor(CL[0][:], big1, big2, op=ALU.subtract))
                vop(lambda: nc.vector.tensor_tensor(big1, X9(Cre), PW9(PWim[:]), op=ALU.mult))
                vop(lambda: nc.vector.tensor_tensor(big2, X9(Cim), PW9(PWre[:]), op=ALU.mult))
                vop(lambda: nc.vector.tensor_tensor(big1, big1, big2, op=ALU.add))
                vop(lambda: nc.vector.tensor_scalar(CL[1][:], big1, -1.0, None, op0=ALU.mult))
                X8 = lambda t: t.unsqueeze(1).to_broadcast([128, 8, 32, 16])
                PW8 = lambda t: t[:, 0:8, :].unsqueeze(3).to_broadcast([128, 8, 32, 16])
                vop(lambda: nc.vector.tensor_tensor(big1[:, 0:8], X8(bbre), PW8(PWre), op=ALU.mult))
                vop(lambda: nc.vector.tensor_tensor(big2[:, 0:8], X8(bbim), PW8(PWim), op=ALU.mult))
                vop(lambda: nc.vector.tensor_tensor(BstA[0][:], big1[:, 0:8], big2[:, 0:8], op=ALU.subtract))
                vop(lambda: nc.vector.tensor_tensor(big1[:, 0:8], X8(bbim), PW8(PWre), op=ALU.mult))
                vop(lambda: nc.vector.tensor_tensor(big2[:, 0:8], X8(bbre), PW8(PWim), op=ALU.mult))
                vop(lambda: nc.vector.tensor_tensor(BstA[1][:], big1[:, 0:8], big2[:, 0:8], op=ALU.add))
                fw.barrier()
                scr_off[0] = big_off
                XEc = [scrT([128, 8, 4, 2, 16], BF16) for r in range(2)]
                CLXc = [scrT([128, 9, 4, 2, 16], BF16) for r in range(2)]
                CLX3 = [scrT([128, 9, 64], BF16) for r in range(2)]
                LBT = [scrT([128, 8, 128], BF16) for r in range(2)]
                LBT3 = [scrT([128, 8, 128], BF16) for r in range(2)]
                BD = scrT([128, 8, 128], BF16)
                cosC2 = [scrT([128, 256]) for _ in range(2)]; sinC2 = [scrT([128, 256]) for _ in range(2)]; angC = scrT([128, 256])
                tF = scrT([128, 256]); tF2 = scrT([128, 256]); tIl = scrT([128, 256], I32)
                tF3 = sb(ph, "tF3", [128, 256]); tIl2 = sb(ph, "tIl2", [128, 256], I32)
                R_tab2 = [fw.R("tab2a"), fw.R("tab2b")]; R_tmpL = fw.R("tmpL"); R_tmpL2 = fw.R("tmpL2"); nonlocal_RW = [None]
                wa = [scrT([128, 256]) for i in range(4)]
                rr = scrT([128, 256]); rim = scrT([128, 256])
                sbf = [[scrT([128, 128], BF16) for b_ in range(2)] for r in range(2)]
                ysb = [scrT([128, 512]) for i in range(2)]
                R_xe = fw.R("xec"); R_clx = fw.R("clxc"); R_lbt = fw.R("lbt"); R_bd = fw.R("bd")
                R_wa = [fw.R(f"wa{i}") for i in range(4)]; R_rr = fw.R("rr"); R_ri = fw.R("ri")
                R_sbf = [[fw.R(f"sbf{r}{b_}") for b_ in range(2)] for r in range(2)]; R_y = [fw.R(f"ysb{i}") for i in range(2)]
                for r in range(2):
                    fw.op(V, lambda r=r: nc.vector.memset(CLX3[r][:], 0.0), writes=[R_clx])
                y2, y2R = fw.y2
                pairn = 0

                def prep_xe(cc_):
                    for r in range(2):
                        for g2 in range(2):
                            fw.op(V, lambda r=r, g2=g2: nc.vector.tensor_scalar(
                                XEc[r][:, :, :, g2, :], BstA[r][:, :, cc_ * 4:(cc_ + 1) * 4, :], hm[:, g2:g2 + 1], None, op0=ALU.mult),
                                reads=[R_p], writes=[R_xe])
                for cc in range(8):
                    if cc == 0:
                        prep_xe(0)
                    for r in range(2):
                        for g2 in range(2):
                            fw.op(V, lambda r=r, g2=g2, cc=cc: nc.vector.tensor_scalar(
                                CLXc[r][:, :, :, g2, :], CL[r][:, :, cc * 4:(cc + 1) * 4, :], hm[:, g2:g2 + 1], None, op0=ALU.mult),
                                reads=[R_p], writes=[R_clx])
                        fw.op(V, lambda r=r: nc.vector.tensor_copy(
                            CLX3[r][:, :, 32:64], CLXc[r][:, :, 3, :, :].rearrange("p e a b -> p e (a b)")), reads=[R_clx], writes=[R_clx])
                    for r in range(2):
                        for eh in range(2):
                            pt, pr = fw.ps(); ptb = pt[:].bitcast(BF16)
                            for e4 in range(4):
                                e = eh * 4 + e4
                                fw.op(T, lambda ptb=ptb, r=r, e=e, e4=e4: nc.tensor.transpose(
                                    ptb[:, e4 * 128:(e4 + 1) * 128], XEc[r][:, e, :, :, :].rearrange("p a b c -> p (a b c)"), identb[:]),
                                    reads=[R_xe, R_const], writes=[pr])
                            fw.op(A, lambda ptb=ptb, r=r, eh=eh: nc.scalar.copy(
                                LBT[r][:, eh * 4:(eh + 1) * 4, :], ptb[:, 0:512].rearrange("p (a b) -> p a b", a=4)), reads=[pr], writes=[R_lbt])
                            fw.op(V, lambda ptb=ptb, r=r, eh=eh: nc.vector.tensor_scalar(
                                LBT3[r][64:128, eh * 4:(eh + 1) * 4, :], ptb[64:128, 0:512].rearrange("p (a b) -> p a b", a=4), rm3[64:128, 0:1], None, op0=ALU.mult),
                                reads=[pr, R_p], writes=[R_lbt])
                    for dh in range(2):
                        pt, pr = fw.ps()
                        for d4 in range(4):
                            d_ = dh * 4 + d4
                            for r in range(2):
                                fw.op(T, lambda pt=pt, d4=d4, d_=d_, r=r: nc.tensor.matmul(
                                    pt[:, d4 * 128:(d4 + 1) * 128], XEc[r][:, 0, :, :, :].rearrange("p a b c -> p (a b c)"),
                                    CLXc[r][:, d_, :, :, :].rearrange("p a b c -> p (a b c)"), start=(r == 0), stop=(r == 1)),
                                    reads=[R_xe, R_clx], writes=[pr])
                        fw.op(V, lambda pt=pt, dh=dh: nc.vector.tensor_tensor(
                            BD[:, dh * 4:(dh + 1) * 4, :], pt[:].rearrange("p (a b) -> p a b", a=4), bmask[:].unsqueeze(1).to_broadcast([128, 4, 128]), op=ALU.mult),
                            reads=[pr, R_p], writes=[R_bd])
                    for bk in range(2):
                        fw.op(T, lambda bk=bk, cc=cc: nc.tensor.matmul(
                            y2[:, bk * 512:(bk + 1) * 512], zlhs[:], uT[:, cc, 0:512], start=True, stop=False),
                            reads=[R_p, R_u], writes=[y2R])
                    for i in range(8):
                        for j in range(i + 1):
                            fw.op(T, lambda i=i, j=j, cc=cc: nc.tensor.matmul(
                                y2[:, i * 128:(i + 1) * 128], BD[:, i - j, :], uT[:, cc, NP + j::8], start=False, stop=False),
                                reads=[R_bd, R_u], writes=[y2R])
                    if cc + 1 < 8:
                        prep_xe(cc + 1)

                    def stageA(gpl):
                        Pp = cc * 4 + gpl
                        tb = Pp % 2
                        if gpl < 3:
                            rows = slice(32 * gpl, 32 * gpl + 32); Ls = LBT
                        else:
                            rows = slice(64, 128); Ls = LBT3
                        pS, pSR = fw.ps()
                        for r in range(2):
                            for j in range(8):
                                fw.op(T, lambda pS=pS, r=r, j=j, rows=rows, Ls=Ls: nc.tensor.matmul(
                                    pS[:, r * 256:(r + 1) * 256], Ls[r][rows, 7 - j, :], uT[rows, cc, j::8], start=(j == 0), stop=(j == 7)),
                                    reads=[R_lbt, R_u], writes=[pSR])
                        nonlocal_RW[0] = [R_tmpL]
                        fw.op(V, lambda Pp=Pp: nc.vector.tensor_scalar(angC[:], iota[:], 1.0, ph8[:, Pp:Pp + 1], op0=ALU.add, op1=ALU.mult),
                              reads=[R_p, R_tmpL], writes=[R_tmpL])
                        cT_, sT_ = cosC2[tb], sinC2[tb]
                        fw.op(V, lambda: nc.vector.tensor_copy(tIl[:], angC[:]), reads=[R_tmpL], writes=[R_tmpL])
                        fw.op(V, lambda: nc.vector.tensor_copy(tF[:], tIl[:]), reads=[R_tmpL], writes=[R_tmpL])
                        fw.op(V, lambda: nc.vector.tensor_tensor(tF[:], angC[:], tF[:], op=ALU.subtract), reads=[R_tmpL], writes=[R_tmpL])
                        fw.op(A, lambda sT_=sT_: nc.scalar.activation(sT_[:], tF[:], AF.Sin, scale=TWO_PI), reads=[R_tmpL], writes=[R_tab2[tb]])
                        fw.op(V, lambda: nc.vector.tensor_scalar(tF2[:], angC[:], 0.25, None, op0=ALU.add), reads=[R_tmpL], writes=[R_tmpL2])
                        fw.op(V, lambda: nc.vector.tensor_copy(tIl2[:], tF2[:]), reads=[R_tmpL2], writes=[R_tmpL2])
                        fw.op(V, lambda: nc.vector.tensor_copy(tF3[:], tIl2[:]), reads=[R_tmpL2], writes=[R_tmpL2])
                        fw.op(V, lambda: nc.vector.tensor_tensor(tF3[:], tF2[:], tF3[:], op=ALU.subtract), reads=[R_tmpL2], writes=[R_tmpL2])
                        fw.op(A, lambda cT_=cT_: nc.scalar.activation(cT_[:], tF3[:], AF.Sin, scale=TWO_PI), reads=[R_tmpL2], writes=[R_tab2[tb]])
                        return (gpl, Pp, tb, rows, pS, pSR)

                    def stageB(ctx):
                        nonlocal pairn
                        gpl, Pp, tb, rows, pS, pSR = ctx
                        b_ = pairn % 2; pairn += 1
                        cosC, sinC, R_tabL = cosC2[tb], sinC2[tb], R_tab2[tb]
                        Sre = pS[:, 0:256]; Sim = pS[:, 256:512]
                        fw.op(V, lambda: nc.vector.tensor_tensor(wa[0][:], Sre, cosC[:], op=ALU.mult), reads=[pSR, R_tabL], writes=[R_wa[0]])
                        fw.op(V, lambda: nc.vector.tensor_tensor(wa[1][:], Sim, sinC[:], op=ALU.mult), reads=[pSR, R_tabL], writes=[R_wa[1]])
                        fw.op(V, lambda: nc.vector.tensor_tensor(wa[2][:], Sim, cosC[:], op=ALU.mult), reads=[pSR, R_tabL], writes=[R_wa[2]])
                        fw.op(V, lambda: nc.vector.tensor_tensor(wa[3][:], Sre, sinC[:], op=ALU.mult), reads=[pSR, R_tabL], writes=[R_wa[3]])
                        fw.op(P, lambda: nc.gpsimd.tensor_tensor(wa[0][:], wa[0][:], wa[1][:], op=ALU.add), reads=[R_wa[0], R_wa[1]], writes=[R_wa[0]])
                        fw.op(P, lambda: nc.gpsimd.tensor_tensor(wa[2][:], wa[2][:], wa[3][:], op=ALU.subtract), reads=[R_wa[2], R_wa[3]], writes=[R_wa[2]])
                        rho = magE[:, 8, Pp:Pp + 1].to_broadcast([128, 256])
                        fw.op(V, lambda: nc.vector.tensor_tensor_scan(rr[:], rho, wa[0][:], 0.0, ALU.mult, ALU.add),
                              reads=[R_wa[0], R_p], writes=[R_rr])
                        fw.op(V, lambda: nc.vector.tensor_tensor_scan(rim[:], rho, wa[2][:], 0.0, ALU.mult, ALU.add),
                              reads=[R_wa[2], R_p], writes=[R_ri])
                        cs = cosC[:, 127:255]; sn = sinC[:, 127:255]
                        fw.op(P, lambda: nc.gpsimd.tensor_tensor(wa[0][:, 0:128], rr[:, 127:255], cs, op=ALU.mult), reads=[R_rr, R_tabL], writes=[R_wa[0]])
                        fw.op(P, lambda: nc.gpsimd.tensor_tensor(wa[1][:, 0:128], rim[:, 127:255], sn, op=ALU.mult), reads=[R_ri, R_tabL], writes=[R_wa[1]])
                        fw.op(V, lambda: nc.vector.tensor_tensor(wa[2][:, 0:128], rim[:, 127:255], cs, op=ALU.mult), reads=[R_ri, R_tabL], writes=[R_wa[2]])
                        fw.op(V, lambda: nc.vector.tensor_tensor(wa[3][:, 0:128], rr[:, 127:255], sn, op=ALU.mult), reads=[R_rr, R_tabL], writes=[R_wa[3]])
                        fw.op(P, lambda: nc.gpsimd.tensor_tensor(sbf[0][b_][:], wa[0][:, 0:128], wa[1][:, 0:128], op=ALU.subtract),
                              reads=[R_wa[0], R_wa[1]], writes=[R_sbf[0][b_]])
                        fw.op(V, lambda: nc.vector.tensor_tensor(sbf[1][b_][:], wa[2][:, 0:128], wa[3][:, 0:128], op=ALU.add),
                              reads=[R_wa[2], R_wa[3]], writes=[R_sbf[1][b_]])
                        for i in range(8):
                            for r in range(2):
                                if gpl < 3:
                                    lhs = CLXc[r][:, i + 1, gpl, :, :].rearrange("p a b -> p (a b)")
                                else:
                                    lhs = CLX3[r][:, i + 1, :]
                                fw.op(T, lambda i=i, r=r, lhs=lhs: nc.tensor.matmul(
                                    y2[rows, i * 128:(i + 1) * 128], lhs, sbf[r][b_][:], start=False, stop=False),
                                    reads=[R_clx, R_sbf[r][b_]], writes=[y2R])

                    ctxs = [stageA(0), stageA(1)]
                    stageB(ctxs[0]); ctxs.append(stageA(2)); stageB(ctxs[1]); ctxs.append(stageA(3)); stageB(ctxs[2]); stageB(ctxs[3])
                    for bk in range(2):
                        fw.op(T, lambda bk=bk, cc=cc: nc.tensor.matmul(
                            y2[:, bk * 512:(bk + 1) * 512], zlhs[:], uT[:, cc, 0:512], start=False, stop=True),
                            reads=[R_p, R_u], writes=[y2R])
                    y2v = y2[:].rearrange("p (i c) -> p c i", i=8)
                    for g in range(2):
                        fw.op(V, lambda g=g, cc=cc: nc.vector.scalar_tensor_tensor(
                            ysb[g][:].rearrange("p (c i) -> p c i", i=8), uT[:, cc, NP + g * 512: NP + (g + 1) * 512].rearrange("p (c i) -> p c i", i=8),
                            Dq[:, cc:cc + 1], y2v[:, g * 64:(g + 1) * 64, :], op0=ALU.mult, op1=ALU.add),
                            reads=[y2R, R_u, R_p], writes=[R_y[g]])
                        fw.op(A, lambda cc=cc, g=g: nc.scalar.activation(ssmg[:, cc, g * 512:(g + 1) * 512], ysb[g][:], AF.Gelu_apprx_tanh),
                              reads=[R_y[g]], writes=[R_ssmg])
                fw.barrier()
            if dbg == 3:
                with contextlib.ExitStack() as dph:
                    dump("ssmg0", ssmg[:, 0, :], [128, NT], [R_ssmg], dph); dump("ssmg7", ssmg[:, 7, :], [128, NT], [R_ssmg], dph)
                    fw.barrier()
        for stk in reversed(open_stacks[1:]):
            stk.close()
        open_stacks = open_stacks[:1]

        if stop_after >= 5:
            phx = contextlib.ExitStack(); open_stacks.append(phx)
            xres = sb(phx, "xres", [128, 8, D])
            R_xc = [[fw.R(f"xres{t}_{c}") for c in range(4)] for t in range(8)]
            R_x = R_xc
            for tt in range(8):
                fw.dma(SP, xres[:, tt, :], xm[tt * 128:(tt + 1) * 128, :], writes=R_xc[tt])
            with contextlib.ExitStack() as ph:
                mixS = sb(ph, "mixS", [128, 8, NT], BF16); wglu = sb(ph, "wglu", [128, 8, 1024], BF16)
                wo = [sb(ph, f"wo{i}", [128, 16, 512], BF16) for i in range(2)]
                woR = [fw.R(f"wo{i}") for i in range(2)]
                sg = [sb(ph, f"sg{i}", [128, 512]) for i in range(2)]; sgR = [fw.R(f"sg{i}") for i in range(2)]
                R_ms = fw.R("mixS"); R_wg = fw.R("wglu")
                fw.dma(P, wglu[:], w_glu.rearrange("(kc p) c -> p kc c", p=128), writes=[R_wg])
                k = 0
                for co in range(8):
                    for g in range(2):
                        pt, pr = fw.ps()
                        for cc in range(8):
                            fw.op(T, lambda pt=pt, cc=cc, co=co, g=g: nc.tensor.matmul(
                                pt[:], wglu[:, cc, co * 128:(co + 1) * 128], ssmg[:, cc, g * 512:(g + 1) * 512], start=(cc == 0), stop=(cc == 7)),
                                reads=[R_wg, R_ssmg], writes=[pr])
                        s_ = k % 2; k += 1
                        fw.op(A, lambda pt=pt, co=co, s_=s_: nc.scalar.activation(sg[s_][:], pt[:], AF.Sigmoid, bias=bglu[:, co:co + 1]),
                              reads=[pr, R_const], writes=[sgR[s_]])
                        fw.op(V, lambda co=co, g=g, s_=s_: nc.vector.tensor_tensor(
                            mixS[:, co, g * 512:(g + 1) * 512], ssmg[:, co, g * 512:(g + 1) * 512], sg[s_][:], op=ALU.mult),
                            reads=[sgR[s_], R_ssmg], writes=[R_ms])
                w_out_v = w_out.rearrange("(kc p) c -> p kc c", p=128)
                for cg in range(4):
                    w, wR = load_w(wo, woR, w_out_v[:, :, cg * 512:(cg + 1) * 512])
                    fw.op(V, lambda w=w, cg=cg: nc.vector.tensor_tensor(
                        w[:], w[:], g1bc[:, cg * 512:(cg + 1) * 512].unsqueeze(1).to_broadcast([128, 16, 512]), op=ALU.mult),
                        reads=[R_mod, wR], writes=[wR])
                    for tt in range(8):
                        pt, pr = fw.ps()
                        for fc in range(16):
                            src = mixT[:, fc, tt * 128:(tt + 1) * 128] if fc < 8 else mixS[:, fc - 8, tt * 128:(tt + 1) * 128]
                            fw.op(T, lambda pt=pt, src=src, w=w, fc=fc: nc.tensor.matmul(pt[:], src, w[:, fc, :], start=(fc == 0), stop=(fc == 15)),
                                  reads=[R_mix, R_ms, wR], writes=[pr])
                        fw.op(V, lambda pt=pt, tt=tt, cg=cg: nc.vector.tensor_tensor(
                            xres[:, tt, cg * 512:(cg + 1) * 512], pt[:], xres[:, tt, cg * 512:(cg + 1) * 512], op=ALU.add),
                            reads=[pr, R_xc[tt][cg]], writes=[R_xc[tt][cg]])
                fw.barrier()
            if dbg == 5:
                with contextlib.ExitStack() as dph:
                    dump("x1_0", xres[:, 0, :], [128, D], R_x[0], dph); dump("x1_7", xres[:, 7, :], [128, D], R_x[7], dph)
                    fw.barrier()

        if stop_after >= 6:
            with contextlib.ExitStack() as ph:
                h2T = sb(ph, "h2T", [128, 16, NT], BF16); R_h2 = fw.R("h2T")
                with contextlib.ExitStack() as ph_n:
                    tiles = [(lambda t=t: (xres[:, t, :], R_x[t])) for t in range(8)]
                    rms_norm_T(ph_n, tiles, a2, sh2, h2T, R_h2, "_n2")
                    fw.barrier()
                def final_tile(tt, fnw, junk, ssq, ob, obR, R_f, R_sq, R_jf):
                    fw.op(A, lambda: nc.scalar.activation(junk[:], xres[:, tt, :], AF.Square, accum_out=ssq[:, tt:tt + 1]), reads=R_x[tt], writes=[R_jf, R_sq])
                    fw.op(V, lambda: nc.vector.tensor_scalar(ssq[:, tt:tt + 1], ssq[:, tt:tt + 1], 1.0 / D, EPS, op0=ALU.mult, op1=ALU.add), reads=[R_sq], writes=[R_sq])
                    fw.op(A, lambda: nc.scalar.sqrt(ssq[:, tt:tt + 1], ssq[:, tt:tt + 1]), reads=[R_sq], writes=[R_sq])
                    fw.op(V, lambda: nc.vector.reciprocal(ssq[:, tt:tt + 1], ssq[:, tt:tt + 1]), reads=[R_sq], writes=[R_sq])
                    s_ = tt % 2
                    fw.op(V, lambda: nc.vector.scalar_tensor_tensor(ob[s_][:], xres[:, tt, :], ssq[:, tt:tt + 1], fnw[:], op0=ALU.mult, op1=ALU.mult),
                          reads=R_x[tt] + [R_sq, R_f], writes=[obR[s_], R_h2])
                    fw.dma(SP, out_d[tt * 128:(tt + 1) * 128, :], ob[s_][:], reads=[obR[s_]], is_out=True)
                NPART = 11; FPP = 4
                actL = [sb(ph, f"act{i}", [128, FPP, NT], BF16) for i in range(2)]; R_actL = [fw.R(f"act{i}") for i in range(2)]
                wgu = [sb(ph, f"wgu{i}", [128, 16, 128], BF16) for i in range(4)]; wguR = [fw.R(f"wgu{i}") for i in range(4)]
                wd = [sb(ph, f"wd{i}", [128, FPP, 512], BF16) for i in range(4)]; wdR = [fw.R(f"wd{i}") for i in range(4)]
                sg = [sb(ph, f"sgf{i}", [128, 512]) for i in range(2)]; sgR = [fw.R(f"sgf{i}") for i in range(2)]
                sgb = [sb(ph, f"sgb{i}", [128, 512], BF16) for i in range(2)]; sgbR = [fw.R(f"sgb{i}") for i in range(2)]
                w_gu_v = w_gu.rearrange("(kc p) c -> p kc c", p=128)
                k = 0; wdc = [0]
                for part in range(NPART):
                    act = actL[part % 2]; R_act = R_actL[part % 2]
                    for fi in range(FPP):
                        f = part * FPP + fi
                        wg_, wgR_ = load_w(wgu, wguR, w_gu_v[:, :, f * 128:(f + 1) * 128])
                        wu_, wuR_ = load_w(wgu, wguR, w_gu_v[:, :, DFF + f * 128: DFF + (f + 1) * 128])
                        for g in range(2):
                            pg, pgR = fw.ps(); pu, puR = fw.ps()
                            for (w, wR, pt, pr) in [(wg_, wgR_, pg, pgR), (wu_, wuR_, pu, puR)]:
                                for kc in range(16):
                                    fw.op(T, lambda w=w, pt=pt, kc=kc, g=g: nc.tensor.matmul(
                                        pt[:], w[:, kc, :], h2T[:, kc, g * 512:(g + 1) * 512], start=(kc == 0), stop=(kc == 15)),
                                        reads=[wR, R_h2], writes=[pr])
                            s_ = k % 2; k += 1
                            fw.op(A, lambda pg=pg, s_=s_: nc.scalar.activation(sgb[s_][:], pg[:], AF.Silu), reads=[pgR], writes=[sgbR[s_]])
                            fw.op(V, lambda pu=pu, fi=fi, g=g, s_=s_: nc.vector.tensor_tensor(
                                act[:, fi, g * 512:(g + 1) * 512], pu[:], sgb[s_][:], op=ALU.mult), reads=[puR, sgbR[s_]], writes=[R_act])
                    last = (part == NPART - 1)
                    slots = []
                    for cg in range(4):
                        s2 = wdc[0] % 4; wdc[0] += 1
                        slots.append(s2)
                        fw.dma(P, wd[s2][:], w_dn[part * FPP * 128:(part + 1) * FPP * 128, cg * 512:(cg + 1) * 512].rearrange("(f p) c -> p f c", p=128),
                               writes=[wdR[s2]])
                        fw.op(V, lambda s2=s2, cg=cg: nc.vector.tensor_tensor(
                            wd[s2][:], wd[s2][:], g2bc[:, cg * 512:(cg + 1) * 512].unsqueeze(1).to_broadcast([128, FPP, 512]), op=ALU.mult),
                            reads=[R_mod, wdR[s2]], writes=[wdR[s2]])
                        if last:
                            continue
                        for tt in range(8):
                            pt, pr = fw.ps()
                            for fi in range(FPP):
                                fw.op(T, lambda pt=pt, fi=fi, tt=tt, s2=s2: nc.tensor.matmul(
                                    pt[:], act[:, fi, tt * 128:(tt + 1) * 128], wd[s2][:, fi, :], start=(fi == 0), stop=(fi == FPP - 1)),
                                    reads=[R_act, wdR[s2]], writes=[pr])
                            fw.op(V, lambda pt=pt, tt=tt, cg=cg: nc.vector.tensor_tensor(
                                xres[:, tt, cg * 512:(cg + 1) * 512], pt[:], xres[:, tt, cg * 512:(cg + 1) * 512], op=ALU.add),
                                reads=[pr, R_xc[tt][cg]], writes=[R_xc[tt][cg]])
                    if last:
                        class _V:
                            def __init__(self, v):
                                self.v = v

                            def __getitem__(self, k):
                                return self.v
                        h2v = lambda a, b_: h2T[:, a:b_, :].rearrange("p a b -> p (a b)")
                        fnw = _V(h2v(0, 4).bitcast(F32)); ob = [_V(h2v(4, 8).bitcast(F32)), _V(h2v(8, 12).bitcast(F32))]
                        junk = _V(h2v(12, 14)); ssq = sb(ph, "ssqf", [128, 8])
                        R_f = fw.R("fnw"); R_sq = fw.R("ssqf"); R_jf = R_h2
                        obR = [fw.R(f"ob{i}") for i in range(2)]
                        fw.dma(SP, fnw[:], fnw_d[0:1, :].partition_broadcast(128), writes=[R_f, R_h2])
                        for tt in range(8):
                            for cg in range(4):
                                s2 = slots[cg]
                                pt, pr = fw.ps()
                                for fi in range(FPP):
                                    fw.op(T, lambda pt=pt, fi=fi, tt=tt, s2=s2: nc.tensor.matmul(
                                        pt[:], act[:, fi, tt * 128:(tt + 1) * 128], wd[s2][:, fi, :], start=(fi == 0), stop=(fi == FPP - 1)),
                                        reads=[R_act, wdR[s2]], writes=[pr])
                                fw.op(V, lambda pt=pt, tt=tt, cg=cg: nc.vector.tensor_tensor(
                                    xres[:, tt, cg * 512:(cg + 1) * 512], pt[:], xres[:, tt, cg * 512:(cg + 1) * 512], op=ALU.add),
                                    reads=[pr, R_xc[tt][cg]], writes=[R_xc[tt][cg]])
                            final_tile(tt, fnw, junk, ssq, ob, obR, R_f, R_sq, R_jf)
                fw.barrier()
            if False:
                for tt in range(8):
                    fw.op(A, lambda tt=tt: nc.scalar.activation(junk[:], xres[:, tt, :], AF.Square, accum_out=ssq[:, tt:tt + 1]), reads=R_x[tt], writes=[R_jf, R_sq])
                    fw.op(V, lambda tt=tt: nc.vector.tensor_scalar(ssq[:, tt:tt + 1], ssq[:, tt:tt + 1], 1.0 / D, EPS, op0=ALU.mult, op1=ALU.add), reads=[R_sq], writes=[R_sq])
                    fw.op(A, lambda tt=tt: nc.scalar.sqrt(ssq[:, tt:tt + 1], ssq[:, tt:tt + 1]), reads=[R_sq], writes=[R_sq])
                    fw.op(V, lambda tt=tt: nc.vector.reciprocal(ssq[:, tt:tt + 1], ssq[:, tt:tt + 1]), reads=[R_sq], writes=[R_sq])
                    s_ = tt % 2
                    fw.op(V, lambda tt=tt, s_=s_: nc.vector.scalar_tensor_tensor(ob[s_][:], xres[:, tt, :], ssq[:, tt:tt + 1], fnw[:], op0=ALU.mult, op1=ALU.mult),
                          reads=R_x[tt] + [R_sq, R_f], writes=[obR[s_]])
                    fw.dma(SP, out_d[tt * 128:(tt + 1) * 128, :], ob[s_][:], reads=[obR[s_]], is_out=True)
                fw.barrier()
        for stk in reversed(open_stacks):
            stk.close()
        for ev in fw.out_events:
            fw._wait(SP, ev)
    return nc, dbg_out


def _bf(x):
    return np.ascontiguousarray(x.astype(np.float32))


def make_in_maps(inp):
    f32 = np.float32
    x = np.asarray(inp["x"], f32); c = np.asarray(inp["c"], f32)
    g = lambda k: np.asarray(inp[k], f32)
    def pl(v, n):
        return np.ascontiguousarray(v.reshape(n, 128).T)
    hd = np.arange(4, dtype=np.float64)
    lg = np.log1p(-np.exp2(-5.0 - hd))
    idx = np.arange(128, dtype=np.float64)
    diff = idx[None, :] - idx[:, None]
    maskT = np.zeros((128, 4, 128), f32)
    for h in range(4):
        maskT[:, h, :] = np.where(diff >= 0, np.exp(lg[h] * np.maximum(diff, 0.0)), 0.0) / 16.0
    qdec = np.zeros((128, 4, 128), f32)
    for h in range(4):
        qdec[:, h, :] = np.exp(lg[h] * (idx + 1.0))[None, :]
    kdec = np.zeros((128, 4), f32)
    for h in range(4):
        kdec[:, h] = np.exp(lg[h] * (127.0 - idx)) / 16.0
    freqs = (np.float32(10000.0) ** (-np.arange(128, dtype=f32) / np.float32(128))).astype(f32)
    pos = np.arange(2048, dtype=f32)
    ang = (pos[None, :] * freqs[:, None]).astype(f32)
    cos_all = np.cos(ang).astype(f32); sin_all = np.sin(ang).astype(f32)
    ident = np.eye(128, dtype=f32)
    iota = np.tile(np.arange(256, dtype=f32)[None, :], (128, 1))
    E9 = np.tile(np.repeat(np.arange(9, dtype=f32), 32)[None, :], (128, 1))
    hm = np.zeros((128, 2), f32); hm[:64, 0] = 1; hm[64:, 1] = 1
    bmask = np.kron(np.eye(4, dtype=f32), np.ones((32, 32), f32))
    rm3 = np.zeros((128, 1), f32); rm3[96:] = 1
    def pair2(a):
        return np.ascontiguousarray(a.reshape(32, 2, 64).transpose(1, 2, 0).reshape(128, 32))
    A_re = pair2(g("s5_a_re")[0]); A_im = pair2(g("s5_a_im")[0])
    LS = pair2(np.repeat(g("s5_log_step")[0][:, None], 64, axis=1))
    def pairB(bm):
        return np.ascontiguousarray(bm.reshape(32, 2, 64, 16).transpose(1, 2, 0, 3).reshape(128, 512))
    def pairC(cm):
        return np.ascontiguousarray(cm.reshape(32, 2, 16, 64).transpose(1, 3, 0, 2).reshape(128, 512))
    common = dict(
        w_ada=g("w_ada")[0], b_ada=g("b_ada")[0][None, :], n1w=pl(g("norm1_w")[0], 16), n2w=pl(g("norm2_w")[0], 16),
        fnw=g("final_norm_w")[None, :], w_in=g("w_in")[0], rnw=pl(g("ret_norm_w")[0], 8),
        A_re=A_re, A_im=A_im, LS=LS, B_re=pairB(g("s5_b_re")[0]), B_im=pairB(g("s5_b_im")[0]),
        C_re=pairC(g("s5_c_re")[0]), C_im=pairC(g("s5_c_im")[0]), Dq=pl(g("s5_d")[0].reshape(-1), 8),
        w_glu=g("w_glu")[0], bglu=pl(g("b_glu")[0], 8), w_out=g("w_out")[0], w_gu=g("w_gate_up")[0], w_dn=g("w_down")[0],
        ident=ident, maskT=maskT.reshape(128, 512), qdec=qdec.reshape(128, 512), kdec=kdec,
        cosp=np.ascontiguousarray(cos_all[:, :1024]), sinp=np.ascontiguousarray(sin_all[:, :1024]),
        iota=iota, E9=E9, hm=hm, bmask=bmask, rm3=rm3, iota5=np.tile(np.arange(512, dtype=f32)[None, :], (128, 1)),
    )
    maps = []
    for r in range(8):
        b, half = r // 2, r % 2
        m = dict(common)
        m["xm"] = np.ascontiguousarray(x[b, half * 1024:(half + 1) * 1024])
        m["xp"] = np.ascontiguousarray(x[b, 0:1024])
        m["pmask"] = np.full((128, 1), float(half), f32)
        m["cT"] = pl(c[b], 16)
        m["cosm"] = np.ascontiguousarray(cos_all[:, half * 1024:(half + 1) * 1024])
        m["sinm"] = np.ascontiguousarray(sin_all[:, half * 1024:(half + 1) * 1024])
        maps.append(m)
    return maps


def kernel(**inputs):
    nc, _ = build()
    maps = make_in_maps(inputs)
    res = run_bass_kernel_spmd(nc, maps, core_ids=list(range(8)))
    out = np.zeros((4, 2048, 2048), np.float32)
    for r in range(8):
        b, half = r // 2, r % 2
        out[b, half * 1024:(half + 1) * 1024] = res.results[r]["out"]
    return out
```
